# Optimizing a Trainium2 kernel written in Bass

```python
import math
import jax, jax.numpy as jnp
from jax import lax
import numpy as np

D_MODEL = 2048
BATCH = 32
SEQ = 256
DEPTH = 4
DEC_BATCH = 4
DEC_SEQ = 4096
PAST_LEN = 512

GRID_W = 64
POOL_GROUPS = 4
POOL_WIDTH = 1024
POOL_GROUP_DIM = POOL_WIDTH // POOL_GROUPS
POOL_WINDOWS = (2, 4, 8, 16)
DIFF_HEADS = 8
DIFF_QK_DIM = 64
DIFF_V_DIM = 128
DIFF_QK_WIDTH = DIFF_HEADS * 2 * DIFF_QK_DIM
DIFF_WIDTH = DIFF_HEADS * DIFF_V_DIM
NA_HEADS = 8
NA_HEAD_DIM = 128
NA_WIDTH = NA_HEADS * NA_HEAD_DIM
NA_WIN_R = 8
NA_WIN_C = 16
NA_KEY_COLS = 2 * NA_WIN_C
SGU_GROUPS = 4
SGU_WIDTH = 1024
SGU_CHUNK = 128
MIX_WIDTH = POOL_WIDTH + DIFF_WIDTH
EVEN_SPLITS = (POOL_WIDTH, POOL_WIDTH, DIFF_QK_WIDTH, DIFF_QK_WIDTH, DIFF_WIDTH, DIFF_WIDTH)
ODD_SPLITS = (NA_WIDTH, NA_WIDTH, NA_WIDTH, NA_WIDTH, SGU_WIDTH, SGU_WIDTH, SGU_WIDTH)
EVEN_IN = sum(EVEN_SPLITS)
ODD_IN = sum(ODD_SPLITS)
QUERY_BLOCK = 128
ROPE_BASE = 10000.0
ROPE_AXIS_DIM = DIFF_QK_DIM // 2
LN_EPS = 1e-5
NEG_INF = -1e30
DEEPNORM_ALPHA = (2 * DEPTH) ** 0.25
DEEPNORM_BETA = (8 * DEPTH) ** -0.25

kernel_name = "hybrid_diffusion_pool_diffattn_natten_sgu_step"


def split_cols(x, sizes):
    idx = [int(i) for i in np.cumsum(sizes)[:-1]]
    return jnp.split(x, idx, axis=-1)


def layer_norm(x, g, b=None):
    xf = x.astype(jnp.float32)
    mu = jnp.mean(xf, -1, keepdims=True)
    var = jnp.mean(jnp.square(xf - mu), -1, keepdims=True)
    y = ((xf - mu) * lax.rsqrt(var + LN_EPS)).astype(x.dtype) * g
    return y if b is None else y + b


def rms_norm(x, g):
    xf = x.astype(jnp.float32)
    y = xf * lax.rsqrt(jnp.mean(jnp.square(xf), -1, keepdims=True) + LN_EPS)
    return y.astype(x.dtype) * g


def ada_params(cond, w_mod, b_mod):
    m = jax.nn.silu(cond) @ w_mod + b_mod
    return jnp.split(m, 3, axis=-1)


def _rotate(x, ang):
    cos = jnp.cos(ang).astype(x.dtype)
    sin = jnp.sin(ang).astype(x.dtype)
    x1, x2 = jnp.split(x, 2, axis=-1)
    return jnp.concatenate([x1 * cos - x2 * sin, x1 * sin + x2 * cos], axis=-1)


def rope_2d(x):
    L = x.shape[2]
    t = jnp.arange(L)
    row = (t // GRID_W).astype(jnp.float32)[:, None]
    col = (t % GRID_W).astype(jnp.float32)[:, None]
    inv = 1.0 / (ROPE_BASE ** (jnp.arange(0, ROPE_AXIS_DIM, 2, dtype=jnp.float32) / ROPE_AXIS_DIM))
    xr, xc = jnp.split(x, 2, axis=-1)
    return jnp.concatenate([_rotate(xr, row * inv), _rotate(xc, col * inv)], axis=-1)


def map_query_blocks(fn, *qs):
    b, h, L, _ = qs[0].shape
    nb = L // QUERY_BLOCK
    blocks = tuple(jnp.moveaxis(q.reshape(b, h, nb, QUERY_BLOCK, q.shape[-1]), 2, 0) for q in qs)
    out = lax.map(fn, blocks)
    return jnp.moveaxis(out, 0, 2).reshape(b, h, L, out.shape[-1])


def softmax_attention(q, k, v):
    scale = q.shape[-1] ** -0.5

    def block(qs):
        (qb,) = qs
        s = jnp.einsum('bhqd,bhkd->bhqk', qb, k).astype(jnp.float32) * scale
        p = jax.nn.softmax(s, axis=-1).astype(v.dtype)
        return jnp.einsum('bhqk,bhkd->bhqd', p, v)

    return map_query_blocks(block, q)


def diff_attention(q1, q2, k1, k2, v, lam):
    scale = DIFF_QK_DIM ** -0.5

    def block(qs):
        qb1, qb2 = qs
        s1 = jnp.einsum('bhqd,bhkd->bhqk', qb1, k1).astype(jnp.float32) * scale
        s2 = jnp.einsum('bhqd,bhkd->bhqk', qb2, k2).astype(jnp.float32) * scale
        p = jax.nn.softmax(s1, axis=-1) - lam * jax.nn.softmax(s2, axis=-1)
        return jnp.einsum('bhqk,bhkd->bhqd', p.astype(v.dtype), v)

    return map_query_blocks(block, q1, q2)


def _na_column_tables():
    ncb = GRID_W // NA_WIN_C
    blk = np.arange(ncb)
    qcol = blk[:, None] * NA_WIN_C + np.arange(NA_WIN_C)[None, :]
    kstart = np.clip(blk * NA_WIN_C - NA_WIN_C // 2, 0, GRID_W - NA_KEY_COLS)
    kcol = kstart[:, None] + np.arange(NA_KEY_COLS)[None, :]
    cstart = np.clip(qcol - NA_WIN_C // 2, 0, GRID_W - NA_WIN_C)
    valid = (kcol[:, None, :] >= cstart[:, :, None]) & (kcol[:, None, :] < cstart[:, :, None] + NA_WIN_C)
    dc = np.clip(kcol[:, None, :] - qcol[:, :, None] + NA_WIN_C - 1, 0, 2 * NA_WIN_C - 2)
    return kcol, valid, dc


def neighborhood_attention(q, k, v, ck, cv, rpb):
    B, H, L, d = q.shape
    rows = L // GRID_W
    win_r = min(NA_WIN_R, rows)
    ncb = GRID_W // NA_WIN_C
    kcol, valid, dc = _na_column_tables()
    nloc = win_r * NA_KEY_COLS
    kg = k.reshape(B, H, rows, GRID_W, d)
    vg = v.reshape(B, H, rows, GRID_W, v.shape[-1])
    qg = jnp.moveaxis(q.reshape(B, H, rows, ncb, NA_WIN_C, d), 2, 0)
    mask = jnp.asarray(np.broadcast_to(valid[:, :, None, :], (ncb, NA_WIN_C, win_r, NA_KEY_COLS)).reshape(ncb, NA_WIN_C, nloc))
    rpb_cols = rpb[:, :, dc]
    scale = d ** -0.5

    def row_step(args):
        r, qr = args
        rs = jnp.clip(r - win_r // 2, 0, rows - win_r)
        kr = lax.dynamic_slice_in_dim(kg, rs, win_r, axis=2)
        vr = lax.dynamic_slice_in_dim(vg, rs, win_r, axis=2)
        kb = jnp.moveaxis(kr[:, :, :, kcol, :], 2, 3).reshape(B, H, ncb, nloc, d)
        vb = jnp.moveaxis(vr[:, :, :, kcol, :], 2, 3).reshape(B, H, ncb, nloc, vr.shape[-1])
        dr = rs + jnp.arange(win_r) - r + NA_WIN_R - 1
        bias = jnp.transpose(rpb_cols[:, dr], (0, 2, 3, 1, 4)).reshape(H, ncb, NA_WIN_C, nloc)
        s_loc = jnp.einsum('bhnqd,bhnkd->bhnqk', qr, kb).astype(jnp.float32) * scale + bias.astype(jnp.float32)
        s_loc = jnp.where(mask, s_loc, NEG_INF)
        s_ctx = jnp.einsum('bhnqd,bhpd->bhnqp', qr, ck).astype(jnp.float32) * scale
        p = jax.nn.softmax(jnp.concatenate([s_loc, s_ctx], axis=-1), axis=-1).astype(v.dtype)
        return (jnp.einsum('bhnqk,bhnkd->bhnqd', p[..., :nloc], vb)
                + jnp.einsum('bhnqp,bhpd->bhnqd', p[..., nloc:], cv))

    out = lax.map(row_step, (jnp.arange(rows), qg))
    return jnp.moveaxis(out, 0, 2).reshape(B, H, L, out.shape[-1])


def multi_scale_pool(x, w_pool, pool_scale):
    B, L, C = x.shape
    xf = x.reshape(B, L, POOL_GROUPS, POOL_GROUP_DIM).astype(jnp.float32)
    cs = jnp.concatenate([jnp.zeros_like(xf[:, :1]), jnp.cumsum(xf, axis=1)], axis=1)
    t = jnp.arange(L)[:, None]
    half = jnp.asarray(POOL_WINDOWS)[None, :] // 2
    lo = jnp.clip(t - half, 0, L)
    hi = jnp.clip(t + half, 0, L)
    g = jnp.arange(POOL_GROUPS)[None, :]
    win_sum = cs[:, hi, g] - cs[:, lo, g]
    cnt = (hi - lo).astype(jnp.float32)[None, :, :, None]
    pooled = (win_sum / cnt - xf).astype(x.dtype)
    mixed = jnp.einsum('blgc,gcd->blgd', pooled, w_pool)
    return mixed.reshape(B, L, C) * pool_scale


def spatial_gating(u, v, ln_g, w_s, b_s):
    B, L, C = v.shape
    n = L // SGU_CHUNK
    vn = layer_norm(v, ln_g).reshape(B, n, SGU_CHUNK, SGU_GROUPS, C // SGU_GROUPS)
    s = jnp.einsum('gij,bnjgc->bnigc', w_s, vn) + jnp.transpose(b_s)[:, :, None]
    return u * s.reshape(B, L, C)


def even_branch(h, w_in, pool_w, pool_scale, diff_lam, diff_subln, layer_idx, ctx_k=None, ctx_v=None):
    B, L, _ = h.shape
    a_in, a_gate, q, k, v, b_gate = split_cols(h @ w_in, EVEN_SPLITS)
    ya = multi_scale_pool(a_in, pool_w, pool_scale) * jax.nn.silu(a_gate)
    q = q.reshape(B, L, DIFF_HEADS, 2, DIFF_QK_DIM).transpose(0, 2, 1, 3, 4)
    k = k.reshape(B, L, DIFF_HEADS, 2, DIFF_QK_DIM).transpose(0, 2, 1, 3, 4)
    v = v.reshape(B, L, DIFF_HEADS, DIFF_V_DIM).transpose(0, 2, 1, 3)
    q1, q2, k1, k2 = q[:, :, :, 0], q[:, :, :, 1], k[:, :, :, 0], k[:, :, :, 1]
    lam_init = 0.8 - 0.6 * math.exp(-0.3 * layer_idx)
    lam = (jnp.exp(jnp.sum(diff_lam[0] * diff_lam[1]).astype(jnp.float32))
           - jnp.exp(jnp.sum(diff_lam[2] * diff_lam[3]).astype(jnp.float32)) + lam_init)
    if ctx_k is None:
        new_k, new_v = jnp.concatenate([k1, k2], axis=-1), v
        kk1, kk2, vv = k1, k2, v
    else:
        new_k = new_v = None
        q1, q2 = rope_2d(q1), rope_2d(q2)
        kk1 = jnp.concatenate([rope_2d(k1), ctx_k[..., :DIFF_QK_DIM]], axis=2)
        kk2 = jnp.concatenate([rope_2d(k2), ctx_k[..., DIFF_QK_DIM:]], axis=2)
        vv = jnp.concatenate([v, ctx_v], axis=2)
    o = diff_attention(q1, q2, kk1, kk2, vv, lam)
    o = rms_norm(o, diff_subln) * (1.0 - lam_init)
    yb = o.transpose(0, 2, 1, 3).reshape(B, L, DIFF_WIDTH) * jax.nn.silu(b_gate)
    return jnp.concatenate([ya, yb], axis=-1), new_k, new_v


def odd_branch(h, w_in, rpb, sgu_ln, sgu_w, sgu_b, ctx_k=None, ctx_v=None):
    B, L, _ = h.shape
    q, k, v, c_gate, u, vs, d_gate = split_cols(h @ w_in, ODD_SPLITS)
    q, k, v = (t.reshape(B, L, NA_HEADS, NA_HEAD_DIM).transpose(0, 2, 1, 3) for t in (q, k, v))
    if ctx_k is None:
        new_k, new_v = k, v
        o = softmax_attention(q, k, v)
    else:
        new_k = new_v = None
        o = neighborhood_attention(q, k, v, ctx_k, ctx_v, rpb)
    yc = o.transpose(0, 2, 1, 3).reshape(B, L, NA_WIDTH) * jax.nn.silu(c_gate)
    yd = spatial_gating(u, vs, sgu_ln, sgu_w, sgu_b) * jax.nn.silu(d_gate)
    return jnp.concatenate([yc, yd], axis=-1), new_k, new_v


def setup_inputs(seed: int = 0) -> dict:
    key = jax.random.key(seed)
    ks = iter(jax.random.split(key, 64))

    def nrm(shape, s):
        return jax.random.normal(next(ks), shape, jnp.float32) * s

    d = {}
    d["x_prompt"] = nrm((BATCH, SEQ, D_MODEL), 1.0)
    d["x_sample"] = nrm((DEC_BATCH, DEC_SEQ, D_MODEL), 1.0)
    for l in range(DEPTH):
        if l % 2 == 0:
            d[f"cache_k_l{l}"] = nrm((DEC_BATCH, DIFF_HEADS, PAST_LEN, 2 * DIFF_QK_DIM), 1.0)
            d[f"cache_v_l{l}"] = nrm((DEC_BATCH, DIFF_HEADS, PAST_LEN, DIFF_V_DIM), 1.0)
        else:
            d[f"cache_k_l{l}"] = nrm((DEC_BATCH, NA_HEADS, PAST_LEN, NA_HEAD_DIM), 1.0)
            d[f"cache_v_l{l}"] = nrm((DEC_BATCH, NA_HEADS, PAST_LEN, NA_HEAD_DIM), 1.0)
    d["c"] = nrm((DEC_BATCH, D_MODEL), 1.0)
    d["c_ctx"] = nrm((D_MODEL,), 1.0)
    for l in range(DEPTH):
        d[f"w_mod_{l}"] = nrm((D_MODEL, 3 * D_MODEL), 0.5 * D_MODEL ** -0.5)
        d[f"b_mod_{l}"] = nrm((3 * D_MODEL,), 0.01)
        d[f"w_in_{l}"] = nrm((D_MODEL, EVEN_IN if l % 2 == 0 else ODD_IN), D_MODEL ** -0.5)
        d[f"w_out_{l}"] = nrm((MIX_WIDTH, D_MODEL), MIX_WIDTH ** -0.5 * DEEPNORM_BETA)
        d[f"ln_g_{l}"] = 1.0 + nrm((D_MODEL,), 0.01)
        d[f"ln_b_{l}"] = nrm((D_MODEL,), 0.01)
        if l % 2 == 0:
            d[f"pool_w_{l}"] = nrm((POOL_GROUPS, POOL_GROUP_DIM, POOL_GROUP_DIM), POOL_GROUP_DIM ** -0.5)
            d[f"pool_scale_{l}"] = 1.0 + nrm((POOL_WIDTH,), 0.1)
            d[f"diff_lam_{l}"] = nrm((4, DIFF_QK_DIM), 0.1)
            d[f"diff_subln_{l}"] = 1.0 + nrm((DIFF_V_DIM,), 0.01)
        else:
            d[f"rpb_{l}"] = nrm((NA_HEADS, 2 * NA_WIN_R - 1, 2 * NA_WIN_C - 1), 0.1)
            d[f"sgu_ln_{l}"] = 1.0 + nrm((SGU_WIDTH,), 0.01)
            d[f"sgu_w_{l}"] = nrm((SGU_GROUPS, SGU_CHUNK, SGU_CHUNK), SGU_CHUNK ** -0.5)
            d[f"sgu_b_{l}"] = 1.0 + nrm((SGU_GROUPS, SGU_CHUNK), 0.1)
    return d


def reference(x_prompt, x_sample,
              cache_k_l0, cache_v_l0, cache_k_l1, cache_v_l1,
              cache_k_l2, cache_v_l2, cache_k_l3, cache_v_l3,
              c, c_ctx,
              w_mod_0, b_mod_0, w_in_0, w_out_0, ln_g_0, ln_b_0, pool_w_0, pool_scale_0, diff_lam_0, diff_subln_0,
              w_mod_1, b_mod_1, w_in_1, w_out_1, ln_g_1, ln_b_1, rpb_1, sgu_ln_1, sgu_w_1, sgu_b_1,
              w_mod_2, b_mod_2, w_in_2, w_out_2, ln_g_2, ln_b_2, pool_w_2, pool_scale_2, diff_lam_2, diff_subln_2,
              w_mod_3, b_mod_3, w_in_3, w_out_3, ln_g_3, ln_b_3, rpb_3, sgu_ln_3, sgu_w_3, sgu_b_3):
    cache_k = [cache_k_l0, cache_k_l1, cache_k_l2, cache_k_l3]
    cache_v = [cache_v_l0, cache_v_l1, cache_v_l2, cache_v_l3]
    w_mod = [w_mod_0, w_mod_1, w_mod_2, w_mod_3]
    b_mod = [b_mod_0, b_mod_1, b_mod_2, b_mod_3]
    w_in = [w_in_0, w_in_1, w_in_2, w_in_3]
    w_out = [w_out_0, w_out_1, w_out_2, w_out_3]
    ln_g = [ln_g_0, ln_g_1, ln_g_2, ln_g_3]
    ln_b = [ln_b_0, ln_b_1, ln_b_2, ln_b_3]
    even_p = {0: (pool_w_0, pool_scale_0, diff_lam_0, diff_subln_0),
              2: (pool_w_2, pool_scale_2, diff_lam_2, diff_subln_2)}
    odd_p = {1: (rpb_1, sgu_ln_1, sgu_w_1, sgu_b_1),
             3: (rpb_3, sgu_ln_3, sgu_w_3, sgu_b_3)}

    xp, xs = x_prompt, x_sample
    ks, vs = [], []
    for l in range(DEPTH):
        sh_p, sc_p, g_p = ada_params(c_ctx, w_mod[l], b_mod[l])
        sh_s, sc_s, g_s = (m[:, None, :] for m in ada_params(c, w_mod[l], b_mod[l]))
        hp = xp * (1.0 + sc_p) + sh_p
        hs = xs * (1.0 + sc_s) + sh_s
        if l % 2 == 0:
            yp, nk, nv = even_branch(hp, w_in[l], *even_p[l], l)
            ys, _, _ = even_branch(hs, w_in[l], *even_p[l], l, cache_k[l], cache_v[l])
        else:
            yp, nk, nv = odd_branch(hp, w_in[l], *odd_p[l])
            ys, _, _ = odd_branch(hs, w_in[l], *odd_p[l], cache_k[l], cache_v[l])
        xp = layer_norm(DEEPNORM_ALPHA * xp + g_p * (yp @ w_out[l]), ln_g[l], ln_b[l])
        xs = layer_norm(DEEPNORM_ALPHA * xs + g_s * (ys @ w_out[l]), ln_g[l], ln_b[l])
        ks.append(nk)
        vs.append(nv)
    return (xp, xs, ks[0], vs[0], ks[1], vs[1], ks[2], vs[2], ks[3], vs[3])
```

```python
import math
import os
from contextlib import ExitStack

import numpy as np
import concourse.bass as bass
import concourse.mybir as mybir
from concourse.bass_utils import run_bass_kernel_spmd

F32 = mybir.dt.float32
BF16 = mybir.dt.bfloat16
AF = mybir.ActivationFunctionType
ALU = mybir.AluOpType

D = 2048
NS = 4096
NPS = 4
LP = 256
NT = NS + NPS * LP
PAST = 512
NEG = -30000.0
LN_EPS = 1e-5
ALPHA = (2 * 4) ** 0.25
POOL_WINDOWS = (2, 4, 8, 16)


class Tok:
    __slots__ = ("w", "r")

    def __init__(self):
        self.w = None
        self.r = []


class Sched:
    ENG = ("pe", "act", "dve", "pool", "sp")

    def __init__(self, nc, es):
        self.nc = nc
        ndma = {"sp": 28, "pool": 12}
        self.sem = {k: es.enter_context(nc.semaphore("s_" + k)) for k in ("pe", "act", "dve", "pool")}
        self.cnt = {k: 0 for k in self.sem}
        self.dsem = {q: [es.enter_context(nc.semaphore(f"d_{q}{i}")) for i in range(n)] for q, n in ndma.items()}
        self.dcnt = {q: [0] * n for q, n in ndma.items()}
        self.dnext = {q: 0 for q in ndma}
        self.waited = {e: {} for e in self.ENG}
        self.prog = {e: [] for e in self.ENG}
        self.toks = {}
        self.semobj = {}
        self.nins = 0

    def tok(self, key):
        t = self.toks.get(key)
        if t is None:
            t = self.toks[key] = Tok()
        return t

    def _deps(self, e, reads, writes):
        deps = {}

        def add(ev):
            if ev is None:
                return
            s, v = ev
            k = id(s)
            self.semobj[k] = s
            if deps.get(k, 0) < v:
                deps[k] = v

        for t in reads:
            add(t.w)
        for t in writes:
            add(t.w)
            for ev in t.r:
                add(ev)
        waits = []
        own = id(self.sem["pe"]) if e == "pe" else None
        for k, v in deps.items():
            if k == own:
                continue
            if self.waited[e].get(k, 0) < v:
                self.waited[e][k] = v
                waits.append((self.semobj[k], v))
        return waits

    def _finish(self, ev, reads, writes):
        for t in reads:
            t.r.append(ev)
            if len(t.r) > 64:
                best = {}
                for s, v in t.r:
                    if best.get(id(s), (None, 0))[1] < v:
                        best[id(s)] = (s, v)
                t.r = list(best.values())
        for t in writes:
            t.w = ev
            t.r = []

    def _toks(self, lst):
        return [self.tok(t) if not isinstance(t, Tok) else t for t in lst]

    def op(self, e, fn, reads=(), writes=()):
        reads = self._toks(reads)
        writes = self._toks(writes)
        waits = self._deps(e, reads, writes)
        self.cnt[e] += 1
        ev = (self.sem[e], self.cnt[e])
        self.prog[e].append((waits, fn, ev, 1))
        self._finish(ev, reads, writes)
        self.nins += 1 + len(waits)
        return ev

    def dma(self, q, fn, reads=(), writes=()):
        reads = self._toks(reads)
        writes = self._toks(writes)
        waits = self._deps(q, reads, writes)
        j = self.dnext[q]
        self.dnext[q] = (j + 1) % len(self.dsem[q])
        s = self.dsem[q][j]
        prev = self.dcnt[q][j]
        k = id(s)
        self.semobj[k] = s
        if prev > 0 and self.waited[q].get(k, 0) < prev:
            self.waited[q][k] = prev
            waits.append((s, prev))
        self.dcnt[q][j] = prev + 16
        ev = (s, prev + 16)
        self.prog[q].append((waits, fn, ev, 16))
        self._finish(ev, reads, writes)
        self.nins += 1 + len(waits)
        return ev

    def wait_events(self, e, evs):
        waits = []
        for s, v in evs:
            k = id(s)
            if self.waited[e].get(k, 0) < v:
                self.waited[e][k] = v
                waits.append((s, v))
        if waits:
            self.prog[e].append((waits, None, None, 0))
            self.nins += len(waits)

    def all_events(self):
        evs = [(self.sem[k], self.cnt[k]) for k in self.sem if self.cnt[k] > 0]
        for q in self.dsem:
            for s, c in zip(self.dsem[q], self.dcnt[q]):
                if c > 0:
                    evs.append((s, c))
        return evs

    def barrier(self):
        evs = self.all_events()
        for e in self.ENG:
            self.wait_events(e, evs)
        self.toks = {}

    def emit(self):
        nc = self.nc
        prog = self.prog

        def run(eng, lst):
            for waits, fn, ev, inc in lst:
                for s, v in waits:
                    eng.wait_ge(s, v)
                if fn is not None:
                    fn(eng).then_inc(ev[0], inc)

        with nc.Block() as block:
            @block.tensor
            def _(eng):
                run(eng, prog["pe"])

            @block.scalar
            def _(eng):
                run(eng, prog["act"])

            @block.vector
            def _(eng):
                run(eng, prog["dve"])

            @block.gpsimd
            def _(eng):
                run(eng, prog["pool"])

            @block.sync
            def _(eng):
                run(eng, prog["sp"])


class Arena:
    def __init__(self, handle, nbytes):
        self.h32 = handle
        self.h16 = handle.bitcast(BF16)
        self.nbytes = nbytes
        self.off = 0
        self.uid = 0

    def reset(self):
        self.off = 0

    def alloc(self, dt, *shape, parts=128):
        n = 1
        for s in shape:
            n *= s
        esz = 4 if dt == F32 else 2
        size = (n * esz + 31) // 32 * 32
        assert self.off + size <= self.nbytes, (self.off, size, self.nbytes)
        o = self.off
        self.off += size
        h = self.h32 if dt == F32 else self.h16
        ap = h[0:parts, o // esz:o // esz + n]
        if len(shape) == 2:
            ap = ap.rearrange("p (a b) -> p a b", a=shape[0])
        elif len(shape) == 3:
            ap = ap.rearrange("p (a b c) -> p a b c", a=shape[0], b=shape[1])
        self.uid += 1
        return ap, f"ar{self.uid}"


def build_program(n_layers=4, stages=None):
    nc = bass.Bass("TRN2", target_bir_lowering=False)

    def din(name, shape, dt=F32):
        return nc.dram_tensor(name, list(shape), dt, kind="ExternalInput").ap()

    def dout(name, shape, dt=F32):
        return nc.dram_tensor(name, list(shape), dt, kind="ExternalOutput").ap()

    def dscr(name, shape, dt):
        return nc.dram_tensor(name, list(shape), dt, kind="Internal").ap()

    xin = din("xin", [NT, D])
    cvec = din("cvec", [128, 2, 16])
    NL = n_layers
    ck = [din(f"ck{l}", [8, PAST, 128]) for l in range(NL)]
    cv = [din(f"cv{l}", [8, PAST, 128]) for l in range(NL)]
    wmod = [din(f"wmod{l}", [D, 3 * D]) for l in range(NL)]
    bmodT = [din(f"bmodT{l}", [128, 32]) for l in range(NL)]
    bmodG = [din(f"bmodG{l}", [128, D]) for l in range(NL)]
    win = [din(f"win{l}", [D, 6144 if l % 2 == 0 else 7168]) for l in range(NL)]
    wout = [din(f"wout{l}", [D, D]) for l in range(NL)]
    lng = [din(f"lng{l}", [128, D]) for l in range(NL)]
    lnb = [din(f"lnb{l}", [128, D]) for l in range(NL)]
    poolw = {l: din(f"poolw{l}", [4, 256, 256]) for l in (0, 2) if l < NL}
    pscale = {l: din(f"pscale{l}", [128, 8]) for l in (0, 2) if l < NL}
    dlam = {l: din(f"dlam{l}", [128, 256]) for l in (0, 2) if l < NL}
    subln = {l: din(f"subln{l}", [128, 1]) for l in (0, 2) if l < NL}
    rpbT = {l: din(f"rpbT{l}", [64, 8, 15, 64]) for l in (1, 3) if l < NL}
    sguln = {l: din(f"sguln{l}", [128, 1024]) for l in (1, 3) if l < NL}
    sguw = {l: din(f"sguw{l}", [4, 128, 128]) for l in (1, 3) if l < NL}
    sgub = {l: din(f"sgub{l}", [128, 4, 512]) for l in (1, 3) if l < NL}
    ident_d = din("ident", [128, 128])
    permT_d = din("permT", [128, 128])
    ropec_d = din("ropec", [128, NS])
    ropes_d = din("ropes", [128, NS])
    pedge_d = din("pedge", [128, 4, 16])

    yout = dout("yout", [NT, D])
    kout = [dout(f"kout{l}", [NPS * 8 * LP, 128]) for l in range(NL)]
    vout = [dout(f"vout{l}", [NPS * 8 * LP, 128]) for l in range(NL)]

    X = dscr("X", [NT, D], F32)
    QT = dscr("QT", [1024, NT], BF16)
    KT = dscr("KT", [1024, NT], BF16)
    AINT = dscr("AINT", [1024, NT], BF16)
    AGT = dscr("AGT", [1024, NT], BF16)
    BGT = dscr("BGT", [1024, NT], BF16)
    V = dscr("V", [NT, 1024], BF16)
    VS = dscr("VS", [NT, 1024], F32)
    YT = dscr("YT", [D, NT], BF16)

    with ExitStack() as es:
        S = Sched(nc, es)

        def sbt(name, shape, dt):
            return es.enter_context(nc.sbuf_tensor(name, list(shape), dt))

        ident = sbt("ident_s", [128, 128], F32)
        ones_bf = sbt("ones_bf", [128, 128], BF16)
        zeros_bf = sbt("zeros_bf", [128, 128], BF16)
        perm_bf = sbt("perm_bf", [128, 128], BF16)
        lhsc = sbt("lhsc", [128, 2, 16, 128], BF16)
        silc = sbt("silc", [128, 16, 2], BF16)
        sil = sbt("sil", [128, 2, 16], F32)
        mods = sbt("mods", [128, 2, 2, 16], F32)
        G = sbt("G", [128, 2, D], F32)
        lngb = sbt("lngb", [128, 2, D], F32)
        small = sbt("small", [128, 64], F32)
        ARENA_BYTES = 157 * 1024
        arena_h = sbt("arena", [128, ARENA_BYTES // 4], F32)
        A = Arena(arena_h, ARENA_BYTES)
        PSall = es.enter_context(nc.psum_tensor("psall", [128, 4096], F32))
        PS = [PSall[:, i * 512:(i + 1) * 512] for i in range(8)]
        PSN = [f"ps{i}" for i in range(8)]

        eps_col = small[:, 0:1]
        S.dma("sp", lambda e: e.dma_start(out=ident[:], in_=ident_d), writes=["ident"])
        S.dma("pool", lambda e: e.dma_start(out=perm_bf[:], in_=permT_d), writes=["perm"])
        S.dma("sp", lambda e: e.dma_start(out=sil[:], in_=cvec), writes=["sil"])
        S.op("dve", lambda e: e.memset(ones_bf[:], 1.0), writes=["ones"])
        S.op("dve", lambda e: e.memset(zeros_bf[:], 0.0), writes=["zeros"])
        S.op("dve", lambda e: e.memset(eps_col, LN_EPS), writes=["eps"])
        S.op("act", lambda e: e.activation(out=sil[:], in_=sil[:], func=AF.Silu), reads=["sil"], writes=["sil"])
        for cnd in range(2):
            S.op("dve", lambda e, cnd=cnd: e.tensor_copy(out=silc[:, :, cnd], in_=sil[:, cnd, :]), reads=["sil"], writes=["silc"])
            for kc in range(16):
                S.op("dve", lambda e, cnd=cnd, kc=kc: e.tensor_scalar(out=lhsc[:, cnd, kc, :], in0=zeros_bf[:], scalar1=sil[:, cnd, kc:kc + 1],
                                                                      scalar2=None, op0=ALU.add),
                     reads=["sil", "zeros"], writes=["lhsc"])
        S.barrier()

        def wview(w, g):
            return w.rearrange("(c p) n -> p c n", p=128)[:, :, g * 512:(g + 1) * 512]

        def stage_mod(l):
            A.reset()
            wb = [A.alloc(BF16, 16, 512) for _ in range(2)]
            bmT, bmTn = A.alloc(F32, 32)
            bmG, bmGn = A.alloc(F32, D)
            S.dma("sp", lambda e: e.dma_start(out=bmT, in_=bmodT[l]), writes=[bmTn])
            S.dma("sp", lambda e: e.dma_start(out=bmG, in_=bmodG[l]), writes=[bmGn])
            S.dma("sp", lambda e: e.dma_start(out=lngb[:, 0, :], in_=lng[l]), writes=["lngb"])
            S.dma("sp", lambda e: e.dma_start(out=lngb[:, 1, :], in_=lnb[l]), writes=["lngb"])
            psM = PS[7]
            for gi in range(12):
                w_ap, w_n = wb[gi % 2]
                S.dma("pool", lambda e, w_ap=w_ap, gi=gi: e.dma_start(out=w_ap, in_=wview(wmod[l], gi)), writes=[w_n])
                if gi < 8:
                    for j in range(4):
                        cc = gi * 4 + j
                        for kc in range(16):
                            S.op("pe", lambda e, w_ap=w_ap, j=j, kc=kc, cc=cc: e.matmul(
                                out=psM[:, cc * 2:cc * 2 + 2], lhsT=w_ap[:, kc, j * 128:(j + 1) * 128], rhs=silc[:, kc, :],
                                start=(kc == 0), stop=(kc == 15)), reads=[w_n, "silc"], writes=[PSN[7]])
                else:
                    for cnd in range(2):
                        b = (gi * 2 + cnd) % 4
                        for kc in range(16):
                            S.op("pe", lambda e, w_ap=w_ap, cnd=cnd, kc=kc, b=b: e.matmul(
                                out=PS[b][:], lhsT=lhsc[:, cnd, kc, :], rhs=w_ap[:, kc, :], start=(kc == 0), stop=(kc == 15)),
                                reads=[w_n, "lhsc"], writes=[PSN[b]])
                        c0 = (gi - 8) * 512
                        S.op("dve", lambda e, cnd=cnd, b=b, c0=c0: e.tensor_tensor(out=G[:, cnd, c0:c0 + 512], in0=PS[b][:], in1=bmG[:, c0:c0 + 512], op=ALU.add),
                             reads=[PSN[b], bmGn], writes=["G"])
            pv = psM[:, 0:64].rearrange("p (c t) -> p c t", t=2)
            for cnd in range(2):
                S.op("dve", lambda e, cnd=cnd: e.tensor_tensor(out=mods[:, cnd, :, :].rearrange("p a b -> p (a b)"), in0=pv[:, :, cnd], in1=bmT, op=ALU.add),
                     reads=[PSN[7], bmTn], writes=["mods"])
                S.op("dve", lambda e, cnd=cnd: e.tensor_scalar(out=mods[:, cnd, 1, :], in0=mods[:, cnd, 1, :], scalar1=1.0, scalar2=None, op0=ALU.add),
                     reads=["mods"], writes=["mods"])
            S.barrier()

        PDBG = set(os.environ.get('PDBG', 'fm,rope,tm,kv').split(','))

        def stage_P(l):
            even = l % 2 == 0
            ngrp = 12 if even else 14
            src = xin if l == 0 else X
            A.reset()
            hT = [A.alloc(BF16, 16, 1024) for _ in range(2)]
            wb = [A.alloc(BF16, 16, 512) for _ in range(2)]
            xt = [A.alloc(F32, D) for _ in range(2)]
            obf = [A.alloc(BF16, 512) for _ in range(4)]
            of32 = [A.alloc(F32, 512) for _ in range(2)]
            qb = [A.alloc(BF16, 512) for _ in range(2)]
            t1 = [A.alloc(F32, 512) for _ in range(2)]
            t2 = [A.alloc(F32, 512) for _ in range(2)]
            cs = [A.alloc(F32, 2, 512) for _ in range(2)]
            cnt = {"obf": 0, "of32": 0, "ps": 0, "rope": 0, "w": 0, "x": 0, "tp": 0}
            if even:
                kinds = ["ain", "ain", "gA", "gA", "q", "q", "k", "k", "v", "v", "gB", "gB"]
            else:
                kinds = ["q", "q", "k", "k", "v", "v", "gB", "gB", "ain", "ain", "vs", "vs", "gA", "gA"]
            dstT = {"ain": AINT, "gA": AGT, "gB": BGT, "q": QT, "k": KT}
            def tr_tile(tsb, ti):
                tok0 = tsb * 1024
                cnd = 0 if tsb < 4 else 1
                h_ap, h_n = hT[tsb % 2]
                x_ap, x_n = xt[cnt["x"] % 2]
                cnt["x"] += 1
                r0 = tok0 + ti * 128
                S.dma("sp", lambda e: e.dma_start(out=x_ap, in_=src[r0:r0 + 128, :]), writes=[x_n])
                for q4 in range(4):
                    b = 4 + cnt["tp"] % 2
                    cnt["tp"] += 1
                    for j in range(4):
                        kc = q4 * 4 + j
                        S.op("pe", lambda e, kc=kc, j=j, b=b: e.transpose(out=PS[b][:, j * 128:(j + 1) * 128], in_=x_ap[:, kc * 128:(kc + 1) * 128], identity=ident[:]),
                             reads=[x_n, "ident"], writes=[PSN[b]])
                    for j in range(4):
                        kc = q4 * 4 + j
                        S.op("act", lambda e, kc=kc, j=j, b=b: e.activation(
                            out=h_ap[:, kc, ti * 128:(ti + 1) * 128], in_=PS[b][:, j * 128:(j + 1) * 128], func=AF.Identity,
                            scale=mods[:, cnd, 1, kc:kc + 1], bias=mods[:, cnd, 0, kc:kc + 1]),
                            reads=[PSN[b], "mods"], writes=[h_n])

            for ti in range(8):
                tr_tile(0, ti)
            for tsb in range(5):
                tok0 = tsb * 1024
                cnd = 0 if tsb < 4 else 1
                h_ap, h_n = hT[tsb % 2]
                for g in range(ngrp):
                    kind = kinds[g]
                    gsub = g % 2
                    if kind == "k" and False:
                        pass
                    w_ap, w_n = wb[cnt["w"] % 2]
                    cnt["w"] += 1
                    S.dma("pool", lambda e, w_ap=w_ap, g=g: e.dma_start(out=w_ap, in_=wview(win[l], g)), writes=[w_n])
                    if kind in ("ain", "gA", "gB", "q", "k") and "fm" in PDBG:
                        for j in range(4):
                            row0 = (gsub * 4 + j) * 128
                            for hf in range(2):
                                b = cnt["ps"] % 4
                                cnt["ps"] += 1
                                t0 = tok0 + hf * 512
                                for kc in range(16):
                                    S.op("pe", lambda e, w_ap=w_ap, h_ap=h_ap, j=j, kc=kc, hf=hf, b=b: e.matmul(
                                        out=PS[b][:], lhsT=w_ap[:, kc, j * 128:(j + 1) * 128], rhs=h_ap[:, kc, hf * 512:(hf + 1) * 512],
                                        start=(kc == 0), stop=(kc == 15)), reads=[w_n, h_n], writes=[PSN[b]])
                                o_ap, o_n = obf[cnt["obf"] % 4]
                                cnt["obf"] += 1
                                if kind in ("gA", "gB"):
                                    S.op("act", lambda e, o_ap=o_ap, b=b: e.activation(out=o_ap, in_=PS[b][:], func=AF.Silu), reads=[PSN[b]], writes=[o_n])
                                elif kind in ("q", "k") and even and tsb < 4 and "rope" in PDBG:
                                    ri = cnt["rope"] % 2
                                    cnt["rope"] += 1
                                    qb_ap, qb_n = qb[ri]
                                    t1_ap, t1_n = t1[ri]
                                    t2_ap, t2_n = t2[ri]
                                    cs_ap, cs_n = cs[ri]
                                    S.dma("sp", lambda e, cs_ap=cs_ap, t0=t0: e.dma_start(out=cs_ap[:, 0, :], in_=ropec_d[:, t0:t0 + 512]), writes=[cs_n])
                                    S.dma("sp", lambda e, cs_ap=cs_ap, t0=t0: e.dma_start(out=cs_ap[:, 1, :], in_=ropes_d[:, t0:t0 + 512]), writes=[cs_n])
                                    S.op("act", lambda e, qb_ap=qb_ap, b=b: e.activation(out=qb_ap, in_=PS[b][:], func=AF.Copy), reads=[PSN[b]], writes=[qb_n])
                                    b2 = 6 + ri
                                    S.op("pe", lambda e, qb_ap=qb_ap, b2=b2: e.matmul(out=PS[b2][:], lhsT=perm_bf[:], rhs=qb_ap, start=True, stop=True),
                                         reads=[qb_n, "perm"], writes=[PSN[b2]])
                                    S.op("dve", lambda e, t1_ap=t1_ap, cs_ap=cs_ap, b=b: e.tensor_tensor(out=t1_ap, in0=PS[b][:], in1=cs_ap[:, 0, :], op=ALU.mult),
                                         reads=[PSN[b], cs_n], writes=[t1_n])
                                    S.op("dve", lambda e, t2_ap=t2_ap, cs_ap=cs_ap, b2=b2: e.tensor_tensor(out=t2_ap, in0=PS[b2][:], in1=cs_ap[:, 1, :], op=ALU.mult),
                                         reads=[PSN[b2], cs_n], writes=[t2_n])
                                    S.op("dve", lambda e, o_ap=o_ap, t1_ap=t1_ap, t2_ap=t2_ap: e.tensor_tensor(out=o_ap, in0=t1_ap, in1=t2_ap, op=ALU.add),
                                         reads=[t1_n, t2_n], writes=[o_n])
                                else:
                                    S.op("act", lambda e, o_ap=o_ap, b=b: e.activation(out=o_ap, in_=PS[b][:], func=AF.Copy), reads=[PSN[b]], writes=[o_n])
                                dst = dstT[kind]
                                S.dma("sp", lambda e, o_ap=o_ap, dst=dst, row0=row0, t0=t0: e.dma_start(out=dst[row0:row0 + 128, t0:t0 + 512], in_=o_ap), reads=[o_n])
                    if (kind in ("v", "vs") or (kind == "k" and tsb == 4)) and "tm" in PDBG:
                        for ti in range(8):
                            b = cnt["ps"] % 4
                            cnt["ps"] += 1
                            r0 = tok0 + ti * 128
                            c0 = gsub * 512
                            for kc in range(16):
                                S.op("pe", lambda e, w_ap=w_ap, h_ap=h_ap, kc=kc, ti=ti, b=b: e.matmul(
                                    out=PS[b][:], lhsT=h_ap[:, kc, ti * 128:(ti + 1) * 128], rhs=w_ap[:, kc, :], start=(kc == 0), stop=(kc == 15)),
                                    reads=[w_n, h_n], writes=[PSN[b]])
                            if kind == "v":
                                o_ap, o_n = obf[cnt["obf"] % 4]
                                cnt["obf"] += 1
                                S.op("act", lambda e, o_ap=o_ap, b=b: e.activation(out=o_ap, in_=PS[b][:], func=AF.Copy), reads=[PSN[b]], writes=[o_n])
                                S.dma("sp", lambda e, o_ap=o_ap, r0=r0, c0=c0: e.dma_start(out=V[r0:r0 + 128, c0:c0 + 512], in_=o_ap), reads=[o_n])
                            if (kind == "vs" or tsb == 4) and "kv" in PDBG:
                                f_ap, f_n = of32[cnt["of32"] % 2]
                                cnt["of32"] += 1
                                S.op("act", lambda e, f_ap=f_ap, b=b: e.activation(out=f_ap, in_=PS[b][:], func=AF.Copy), reads=[PSN[b]], writes=[f_n])
                                if kind == "vs":
                                    S.dma("sp", lambda e, f_ap=f_ap, r0=r0, c0=c0: e.dma_start(out=VS[r0:r0 + 128, c0:c0 + 512], in_=f_ap), reads=[f_n])
                                else:
                                    dsto = kout[l] if kind == "k" else vout[l]
                                    sq = ti // 2
                                    tt0 = (ti % 2) * 128
                                    for hh in range(4):
                                        S.dma("sp", lambda e, f_ap=f_ap, dsto=dsto, sq=sq, tt0=tt0, gsub=gsub, hh=hh: e.dma_start(
                                            out=(VS[((sq * 8 + gsub * 4 + hh) * LP + tt0) // 8:((sq * 8 + gsub * 4 + hh) * LP + tt0) // 8 + 128, 0:128] if os.environ.get("KVDBG") else dsto[(sq * 8 + gsub * 4 + hh) * LP + tt0:(sq * 8 + gsub * 4 + hh) * LP + tt0 + 128, :]), in_=f_ap[:, hh * 128:(hh + 1) * 128]), reads=[f_n])
                if tsb + 1 < 5:
                    for ti_ in range(8):
                        tr_tile(tsb + 1, ti_)
            S.barrier()

        def attention(keys, q_ap, q_n, nq, two, scale, bank_set, reads_extra, E_bufs, ecnt):
            nkc = len(keys)
            wide = two and nq == 512

            def emit_S(i):
                kT_ap, v_ap, nk = keys[i]
                sb_ = bank_set[i % len(bank_set)]
                if two:
                    S.op("pe", lambda e: e.matmul(out=PS[sb_[0]][0:nk, 0:nq], lhsT=kT_ap, rhs=q_ap[0], start=True, stop=True),
                         reads=reads_extra + [q_n], writes=[PSN[sb_[0]]])
                    S.op("pe", lambda e: e.matmul(out=PS[sb_[1]][0:nk, 0:nq], lhsT=kT_ap, rhs=q_ap[1], start=True, stop=True),
                         reads=reads_extra + [q_n], writes=[PSN[sb_[1]]])
                else:
                    S.op("pe", lambda e: e.matmul(out=PS[sb_[0]][0:nk, 0:nq], lhsT=kT_ap, rhs=q_ap, start=True, stop=True),
                         reads=reads_extra + [q_n], writes=[PSN[sb_[0]]])

            def emit_pv(i, s_i, rhs_ap, e_n):
                kT_ap, v_ap, nk = keys[i]
                ob, zb = 4 + 2 * s_i, 5 + 2 * s_i
                S.op("pe", lambda e: e.matmul(out=PS[ob][:, 0:nq], lhsT=v_ap, rhs=rhs_ap, start=(i == 0), stop=(i == nkc - 1)),
                     reads=reads_extra + [e_n], writes=[PSN[ob]])
                S.op("pe", lambda e: e.matmul(out=PS[zb][:, 0:nq], lhsT=ones_bf[0:nk, :], rhs=rhs_ap, start=(i == 0), stop=(i == nkc - 1)),
                     reads=[e_n, "ones"], writes=[PSN[zb]])

            def emit_rest(i):
                kT_ap, v_ap, nk = keys[i]
                sb_ = bank_set[i % len(bank_set)]
                if wide:
                    e_ap, e_n = E_bufs[ecnt[0] % len(E_bufs)]
                    ecnt[0] += 1
                    base = sb_[0] * 512
                    S.op("act", lambda e: e.activation(out=e_ap[0:nk, 0:1024], in_=PSall[0:nk, base:base + 1024], func=AF.Exp, scale=scale),
                         reads=[PSN[sb_[0]], PSN[sb_[1]]], writes=[e_n])
                    for s_i in range(2):
                        emit_pv(i, s_i, e_ap[0:nk, s_i * 512:(s_i + 1) * 512], e_n)
                else:
                    for s_i in range(2 if two else 1):
                        e_ap, e_n = E_bufs[ecnt[0] % len(E_bufs)]
                        ecnt[0] += 1
                        S.op("act", lambda e, e_ap=e_ap, s_i=s_i: e.activation(out=e_ap[0:nk, 0:nq], in_=PS[sb_[s_i]][0:nk, 0:nq], func=AF.Exp, scale=scale),
                             reads=[PSN[sb_[s_i]]], writes=[e_n])
                        emit_pv(i, s_i, e_ap[0:nk, 0:nq], e_n)

            emit_S(0)
            for i in range(nkc):
                if i + 1 < nkc:
                    emit_S(i + 1)
                emit_rest(i)

        def stage_pool(l):
            A.reset()
            pe_t, pe_n = A.alloc(F32, 4, 16)
            psc, psc_n = A.alloc(F32, 8)
            S.dma("sp", lambda e: e.dma_start(out=pe_t, in_=pedge_d), writes=[pe_n])
            S.dma("sp", lambda e: e.dma_start(out=psc, in_=pscale[l]), writes=[psc_n])
            wp = [A.alloc(BF16, 2, 256) for _ in range(4)]
            for g in range(4):
                S.dma("pool", lambda e, g=g: e.dma_start(out=wp[g][0], in_=poolw[l][g].rearrange("(c p) d -> p c d", p=128)), writes=[wp[g][1]])
            NB = NS + 16
            xb = [A.alloc(BF16, NB) for _ in range(2)]
            pa, pa_n = A.alloc(F32, NB)
            pb, pb_n = A.alloc(F32, NB)
            pooled = [A.alloc(BF16, 2, NS) for _ in range(2)]
            ag = [A.alloc(BF16, NS) for _ in range(2)]
            yo = [A.alloc(BF16, 512) for _ in range(3)]
            c = {"x": 0, "ps": 0, "yo": 0, "ag": 0, "pl": 0}
            seqs = [(0, NS)] + [(NS + s * LP, LP) for s in range(NPS)]
            for (t0, L) in seqs:
                N = L + 16
                for g in range(4):
                    w = POOL_WINDOWS[g]
                    half = w // 2
                    pl_ap, pl_n = pooled[c["pl"] % 2]
                    c["pl"] += 1
                    for j in range(2):
                        ch = 2 * g + j
                        x_ap, x_n = xb[c["x"] % 2]
                        c["x"] += 1
                        S.op("pool", lambda e, x_ap=x_ap: e.memset(x_ap[:, 0:8], 0.0), writes=[x_n])
                        S.op("pool", lambda e, x_ap=x_ap, L=L: e.memset(x_ap[:, 8 + L:16 + L], 0.0), writes=[x_n])
                        S.dma("sp", lambda e, x_ap=x_ap, ch=ch, t0=t0, L=L: e.dma_start(out=x_ap[:, 8:8 + L], in_=AINT[ch * 128:(ch + 1) * 128, t0:t0 + L]), writes=[x_n])
                        cur, cur_n, width, n = x_ap, x_n, 1, N
                        bufs = [(pa, pa_n), (pb, pb_n)]
                        bi = 0
                        while width < w:
                            o_ap, o_n = bufs[bi]
                            bi ^= 1
                            n2 = n - width
                            S.op("dve", lambda e, o_ap=o_ap, cur=cur, n2=n2, width=width: e.tensor_tensor(out=o_ap[:, 0:n2], in0=cur[:, 0:n2], in1=cur[:, width:width + n2], op=ALU.add),
                                 reads=[cur_n], writes=[o_n])
                            cur, cur_n, n, width = o_ap, o_n, n2, width * 2
                        s0 = 8 - half
                        S.op("dve", lambda e, pl_ap=pl_ap, j=j, cur=cur, s0=s0, L=L, w=w, x_ap=x_ap: e.scalar_tensor_tensor(
                            out=pl_ap[:, j, 0:L], in0=cur[:, s0:s0 + L], scalar=1.0 / w, in1=x_ap[:, 8:8 + L], op0=ALU.mult, op1=ALU.subtract),
                            reads=[cur_n, x_n], writes=[pl_n])
                        o_ap, o_n = bufs[bi]
                        for (a0, e0) in ((0, 0), (L - 8, 8)):
                            S.op("dve", lambda e, o_ap=o_ap, cur=cur, s0=s0, a0=a0, e0=e0, g=g: e.tensor_tensor(
                                out=o_ap[:, a0:a0 + 8], in0=cur[:, s0 + a0:s0 + a0 + 8], in1=pe_t[:, g, e0:e0 + 8], op=ALU.mult),
                                reads=[cur_n, pe_n], writes=[o_n])
                            S.op("dve", lambda e, o_ap=o_ap, pl_ap=pl_ap, j=j, a0=a0, x_ap=x_ap: e.tensor_tensor(
                                out=pl_ap[:, j, a0:a0 + 8], in0=o_ap[:, a0:a0 + 8], in1=x_ap[:, 8 + a0:16 + a0], op=ALU.subtract),
                                reads=[o_n, x_n], writes=[pl_n])
                    for dch in range(2):
                        ch = 2 * g + dch
                        ag_ap, ag_n = ag[c["ag"] % 2]
                        c["ag"] += 1
                        S.dma("sp", lambda e, ag_ap=ag_ap, ch=ch, t0=t0, L=L: e.dma_start(out=ag_ap[:, 0:L], in_=AGT[ch * 128:(ch + 1) * 128, t0:t0 + L]), writes=[ag_n])
                        nb = max(1, L // 512)
                        bw = min(L, 512)
                        for tb in range(nb):
                            b = c["ps"] % 4
                            c["ps"] += 1
                            for cc in range(2):
                                S.op("pe", lambda e, g=g, cc=cc, dch=dch, pl_ap=pl_ap, tb=tb, bw=bw, b=b: e.matmul(
                                    out=PS[b][:, 0:bw], lhsT=wp[g][0][:, cc, dch * 128:(dch + 1) * 128], rhs=pl_ap[:, cc, tb * 512:tb * 512 + bw],
                                    start=(cc == 0), stop=(cc == 1)), reads=[wp[g][1], pl_n], writes=[PSN[b]])
                            y_ap, y_n = yo[c["yo"] % 3]
                            c["yo"] += 1
                            S.op("dve", lambda e, y_ap=y_ap, b=b, bw=bw, ch=ch, ag_ap=ag_ap, tb=tb: e.scalar_tensor_tensor(
                                out=y_ap[:, 0:bw], in0=PS[b][:, 0:bw], scalar=psc[:, ch:ch + 1], in1=ag_ap[:, tb * 512:tb * 512 + bw], op0=ALU.mult, op1=ALU.mult),
                                reads=[PSN[b], psc_n, ag_n], writes=[y_n])
                            S.dma("sp", lambda e, y_ap=y_ap, ch=ch, t0=t0, tb=tb, bw=bw: e.dma_start(out=YT[ch * 128:(ch + 1) * 128, t0 + tb * 512:t0 + tb * 512 + bw], in_=y_ap[:, 0:bw]),
                                  reads=[y_n])
            S.barrier()

        def stage_diff(l):
            A.reset()
            lam_init = 0.8 - 0.6 * math.exp(-0.3 * l)
            dl, dl_n = A.alloc(F32, 256)
            sub, sub_n = A.alloc(F32, 1)
            lw, lw_n = A.alloc(F32, 136)
            S.dma("sp", lambda e: e.dma_start(out=dl, in_=dlam[l]), writes=[dl_n])
            S.dma("sp", lambda e: e.dma_start(out=sub, in_=subln[l]), writes=[sub_n])
            S.op("dve", lambda e: e.tensor_tensor(out=lw[:, 0:64], in0=dl[:, 0:64], in1=dl[:, 64:128], op=ALU.mult), reads=[dl_n], writes=[lw_n])
            S.op("dve", lambda e: e.tensor_tensor(out=lw[:, 64:128], in0=dl[:, 128:192], in1=dl[:, 192:256], op=ALU.mult), reads=[dl_n], writes=[lw_n])
            S.op("dve", lambda e: e.reduce_sum(out=lw[:, 128:129], in_=lw[:, 0:64], axis=mybir.AxisListType.X), reads=[lw_n], writes=[lw_n])
            S.op("dve", lambda e: e.reduce_sum(out=lw[:, 129:130], in_=lw[:, 64:128], axis=mybir.AxisListType.X), reads=[lw_n], writes=[lw_n])
            S.op("act", lambda e: e.activation(out=lw[:, 130:132], in_=lw[:, 128:130], func=AF.Exp), reads=[lw_n], writes=[lw_n])
            S.op("dve", lambda e: e.tensor_tensor(out=lw[:, 132:133], in0=lw[:, 131:132], in1=lw[:, 130:131], op=ALU.subtract), reads=[lw_n], writes=[lw_n])
            S.op("dve", lambda e: e.tensor_scalar(out=lw[:, 132:133], in0=lw[:, 132:133], scalar1=-lam_init, scalar2=None, op0=ALU.add), reads=[lw_n], writes=[lw_n])
            S.op("dve", lambda e: e.tensor_scalar(out=lw[:, 133:134], in0=sub, scalar1=(1.0 - lam_init), scalar2=None, op0=ALU.mult), reads=[lw_n, sub_n], writes=[lw_n])
            nlam = lw[:, 132:133]
            c1 = lw[:, 133:134]

            NK = NS + PAST
            kt = [A.alloc(BF16, NK) for _ in range(2)]
            vt = [A.alloc(BF16, NK // 128, 128) for _ in range(2)]
            ckf, ckf_n = A.alloc(F32, 4, 128)
            qt = [A.alloc(BF16, 2, 512) for _ in range(2)]
            for q_ap_, q_n_ in qt:
                S.op("dve", lambda e, q_ap_=q_ap_: e.memset(q_ap_.rearrange("p a b -> p (a b)"), 0.0), writes=[q_n_])
            bg = [A.alloc(BF16, 512) for _ in range(2)]
            E = [A.alloc(BF16, 1024) for _ in range(3)]
            tmp = [A.alloc(F32, 512) for _ in range(6)]
            sqb = [A.alloc(BF16, 512) for _ in range(2)]
            yo = [A.alloc(BF16, 512) for _ in range(2)]
            ecnt = [0]
            c = {"q": 0, "y": 0}

            def epilogue(nq, h, tq0, bg_ap, bg_n):
                (r1, r1n), (r2, r2n), (o1, o1n), (o2, o2n), (oo, oon), (rs, rsn) = tmp
                S.op("act", lambda e: e.activation(out=r1[:, 0:nq], in_=PS[5][:, 0:nq], func=AF.Copy), reads=[PSN[5]], writes=[r1n])
                S.op("act", lambda e: e.activation(out=r2[:, 0:nq], in_=PS[7][:, 0:nq], func=AF.Copy), reads=[PSN[7]], writes=[r2n])
                S.op("dve", lambda e: e.reciprocal(out=r1[:, 0:nq], in_=r1[:, 0:nq]), reads=[r1n], writes=[r1n])
                S.op("dve", lambda e: e.reciprocal(out=r2[:, 0:nq], in_=r2[:, 0:nq]), reads=[r2n], writes=[r2n])
                S.op("dve", lambda e: e.tensor_tensor(out=o1[:, 0:nq], in0=PS[4][:, 0:nq], in1=r1[:, 0:nq], op=ALU.mult), reads=[PSN[4], r1n], writes=[o1n])
                S.op("dve", lambda e: e.tensor_tensor(out=o2[:, 0:nq], in0=PS[6][:, 0:nq], in1=r2[:, 0:nq], op=ALU.mult), reads=[PSN[6], r2n], writes=[o2n])
                S.op("dve", lambda e: e.scalar_tensor_tensor(out=oo[:, 0:nq], in0=o2[:, 0:nq], scalar=nlam, in1=o1[:, 0:nq], op0=ALU.mult, op1=ALU.add),
                     reads=[o1n, o2n, lw_n], writes=[oon])
                sq_ap, sq_n = sqb[c["y"] % 2]
                S.op("act", lambda e: e.activation(out=sq_ap[:, 0:nq], in_=oo[:, 0:nq], func=AF.Square), reads=[oon], writes=[sq_n])
                S.op("pe", lambda e: e.matmul(out=PS[0][:, 0:nq], lhsT=ones_bf[:], rhs=sq_ap[:, 0:nq], start=True, stop=True), reads=[sq_n, "ones"], writes=[PSN[0]])
                S.op("act", lambda e: e.activation(out=rs[:, 0:nq], in_=PS[0][:, 0:nq], func=AF.Sqrt, bias=eps_col, scale=1.0 / 128.0), reads=[PSN[0], "eps"], writes=[rsn])
                S.op("dve", lambda e: e.reciprocal(out=rs[:, 0:nq], in_=rs[:, 0:nq]), reads=[rsn], writes=[rsn])
                S.op("dve", lambda e: e.tensor_tensor(out=oo[:, 0:nq], in0=oo[:, 0:nq], in1=rs[:, 0:nq], op=ALU.mult), reads=[oon, rsn], writes=[oon])
                y_ap, y_n = yo[c["y"] % 2]
                c["y"] += 1
                S.op("dve", lambda e: e.scalar_tensor_tensor(out=y_ap[:, 0:nq], in0=oo[:, 0:nq], scalar=c1, in1=bg_ap[:, 0:nq], op0=ALU.mult, op1=ALU.mult),
                     reads=[oon, lw_n, bg_n], writes=[y_n])
                S.dma("sp", lambda e: e.dma_start(out=YT[1024 + h * 128:1024 + (h + 1) * 128, tq0:tq0 + nq], in_=y_ap[:, 0:nq]), reads=[y_n])

            scale = 64 ** -0.5
            bank_sets = [(0, 1), (2, 3)]
            def load_head(h):
                kt_ap, kt_n = kt[h % 2]
                vt_ap, vt_n = vt[h % 2]
                S.dma("sp", lambda e: e.dma_start(out=kt_ap[:, 0:NS], in_=KT[h * 128:(h + 1) * 128, 0:NS]), writes=[kt_n])
                S.dma("sp", lambda e: e.dma_start(out=vt_ap[:, 0:NS // 128, :], in_=V[0:NS, h * 128:(h + 1) * 128].rearrange("(c p) d -> p c d", p=128)), writes=[vt_n])
                S.dma("pool", lambda e: e.dma_start(out=vt_ap[:, NS // 128:NK // 128, :], in_=cv[l][h].rearrange("(c p) d -> p c d", p=128)), writes=[vt_n])
                S.dma("sp", lambda e: e.dma_start(out=ckf, in_=ck[l][h].rearrange("(c p) d -> p c d", p=128)), writes=[ckf_n])

            def ctx_head(h):
                kt_ap, kt_n = kt[h % 2]
                for j in range(4):
                    S.op("pe", lambda e, j=j: e.transpose(out=PS[2][:, j * 128:(j + 1) * 128], in_=ckf[:, j, :], identity=ident[:]), reads=[ckf_n, "ident"], writes=[PSN[2]])
                S.op("act", lambda e: e.activation(out=kt_ap[:, NS:NK], in_=PS[2][:], func=AF.Copy), reads=[PSN[2]], writes=[kt_n])

            qcur = {}

            def load_q(h, qb_i):
                q_ap, q_n = qt[c["q"] % 2]
                bg_ap, bg_n = bg[c["q"] % 2]
                c["q"] += 1
                tq0 = qb_i * 512
                S.dma("sp", lambda e: e.dma_start(out=q_ap[0:64, 0, :], in_=QT[h * 128:h * 128 + 64, tq0:tq0 + 512]), writes=[q_n])
                S.dma("sp", lambda e: e.dma_start(out=q_ap[64:128, 1, :], in_=QT[h * 128 + 64:(h + 1) * 128, tq0:tq0 + 512]), writes=[q_n])
                S.dma("sp", lambda e: e.dma_start(out=bg_ap, in_=BGT[h * 128:(h + 1) * 128, tq0:tq0 + 512]), writes=[bg_n])
                qcur[(h, qb_i)] = (q_ap, q_n, bg_ap, bg_n)

            blocks = [(h, qb_i) for h in range(8) for qb_i in range(NS // 512)]
            load_head(0)
            ctx_head(0)
            load_q(0, 0)
            for bi, (h, qb_i) in enumerate(blocks):
                kt_ap, kt_n = kt[h % 2]
                vt_ap, vt_n = vt[h % 2]
                if bi + 1 < len(blocks):
                    load_q(*blocks[bi + 1])
                if qb_i == 0 and h + 1 < 8:
                    load_head(h + 1)
                keys = [(kt_ap[:, i * 128:(i + 1) * 128], vt_ap[:, i, :], 128) for i in range(NK // 128)]
                q_ap, q_n, bg_ap, bg_n = qcur[(h, qb_i)]
                tq0 = qb_i * 512
                attention(keys, (q_ap[:, 0, :], q_ap[:, 1, :]), q_n, 512, True, scale, bank_sets, [kt_n, vt_n], E, ecnt)
                epilogue(512, h, tq0, bg_ap, bg_n)
                if qb_i == NS // 512 - 1 and h + 1 < 8:
                    ctx_head(h + 1)
            for s in range(NPS):
                t0 = NS + s * LP
                for h in range(8):
                    kt_ap, kt_n = kt[h % 2]
                    vt_ap, vt_n = vt[h % 2]
                    S.dma("sp", lambda e, kt_ap=kt_ap, h=h, t0=t0: e.dma_start(out=kt_ap[:, 0:LP], in_=KT[h * 128:(h + 1) * 128, t0:t0 + LP]), writes=[kt_n])
                    S.dma("sp", lambda e, vt_ap=vt_ap, h=h, t0=t0: e.dma_start(out=vt_ap[:, 0:2, :], in_=V[t0:t0 + LP, h * 128:(h + 1) * 128].rearrange("(c p) d -> p c d", p=128)), writes=[vt_n])
                    q_ap, q_n = qt[c["q"] % 2]
                    bg_ap, bg_n = bg[c["q"] % 2]
                    c["q"] += 1
                    S.dma("sp", lambda e, q_ap=q_ap, h=h, t0=t0: e.dma_start(out=q_ap[0:64, 0, 0:LP], in_=QT[h * 128:h * 128 + 64, t0:t0 + LP]), writes=[q_n])
                    S.dma("sp", lambda e, q_ap=q_ap, h=h, t0=t0: e.dma_start(out=q_ap[64:128, 1, 0:LP], in_=QT[h * 128 + 64:(h + 1) * 128, t0:t0 + LP]), writes=[q_n])
                    S.dma("sp", lambda e, bg_ap=bg_ap, h=h, t0=t0: e.dma_start(out=bg_ap[:, 0:LP], in_=BGT[h * 128:(h + 1) * 128, t0:t0 + LP]), writes=[bg_n])
                    keys = [(kt_ap[:, i * 128:(i + 1) * 128], vt_ap[:, i, :], 128) for i in range(2)]
                    attention(keys, (q_ap[:, 0, 0:LP], q_ap[:, 1, 0:LP]), q_n, LP, True, scale, bank_sets, [kt_n, vt_n], E, ecnt)
                    epilogue(LP, h, t0, bg_ap, bg_n)
            S.barrier()

        def stage_na(l):
            A.reset()
            scale = 128 ** -0.5
            T, T_n = A.alloc(F32, 8, 15, 64, parts=64)
            S.dma("sp", lambda e: e.dma_start(out=T, in_=rpbT[l]), writes=[T_n])
            NK = NS + PAST
            kt, kt_n = A.alloc(BF16, NK)
            vl, vl_n = A.alloc(BF16, 64, 128, parts=64)
            vc, vc_n = A.alloc(BF16, 4, 128)
            ckf, ckf_n = A.alloc(F32, 4, 128)
            qh, qh_n = A.alloc(BF16, NS)
            cg, cg_n = A.alloc(BF16, NS)
            ost, ost_n = A.alloc(F32, NS)
            tm = [A.alloc(F32, 512, parts=64) for _ in range(2)]
            El = [A.alloc(BF16, 8, 64, parts=64) for _ in range(2)]
            Ec = [A.alloc(BF16, 4, 64) for _ in range(2)]
            rz = [A.alloc(F32, 64) for _ in range(2)]
            yo = [A.alloc(BF16, 512) for _ in range(2)]
            E2 = [A.alloc(BF16, 512) for _ in range(4)]
            for h in range(8):
                S.dma("sp", lambda e, h=h: e.dma_start(out=kt[:, 0:NS], in_=KT[h * 128:(h + 1) * 128, 0:NS]), writes=[kt_n])
                S.dma("sp", lambda e, h=h: e.dma_start(out=vl, in_=V[0:NS, h * 128:(h + 1) * 128].rearrange("(r p) d -> p r d", p=64)), writes=[vl_n])
                S.dma("pool", lambda e, h=h: e.dma_start(out=vc, in_=cv[l][h].rearrange("(c p) d -> p c d", p=128)), writes=[vc_n])
                S.dma("sp", lambda e, h=h: e.dma_start(out=ckf, in_=ck[l][h].rearrange("(c p) d -> p c d", p=128)), writes=[ckf_n])
                S.dma("sp", lambda e, h=h: e.dma_start(out=qh, in_=QT[h * 128:(h + 1) * 128, 0:NS]), writes=[qh_n])
                S.dma("sp", lambda e, h=h: e.dma_start(out=cg, in_=BGT[h * 128:(h + 1) * 128, 0:NS]), writes=[cg_n])
                for j in range(4):
                    S.op("pe", lambda e, j=j: e.transpose(out=PS[7][:, j * 128:(j + 1) * 128], in_=ckf[:, j, :], identity=ident[:]), reads=[ckf_n, "ident"], writes=[PSN[7]])
                S.op("act", lambda e: e.activation(out=kt[:, NS:NK], in_=PS[7][:], func=AF.Copy), reads=[PSN[7]], writes=[kt_n])
                def na_front(r):
                    rs_ = min(max(r - 4, 0), 56)
                    dr0 = rs_ - r + 7
                    par = r % 2
                    bS, bC, bO, bZ = (0, 1, 4, 5) if par == 0 else (2, 3, 6, 7)
                    q_r = qh[:, r * 64:(r + 1) * 64]
                    for j in range(8):
                        kr = rs_ + j
                        S.op("pe", lambda e, j=j, kr=kr: e.matmul(out=PS[bS][0:64, j * 64:(j + 1) * 64], lhsT=kt[:, kr * 64:(kr + 1) * 64], rhs=q_r, start=True, stop=True),
                             reads=[kt_n, qh_n], writes=[PSN[bS]])
                    for j in range(4):
                        S.op("pe", lambda e, j=j: e.matmul(out=PS[bC][:, j * 64:(j + 1) * 64], lhsT=kt[:, NS + j * 128:NS + (j + 1) * 128], rhs=q_r, start=True, stop=True),
                             reads=[kt_n, qh_n], writes=[PSN[bC]])
                    tm_ap, tm_n = tm[par]
                    el_ap, el_n = El[par]
                    ec_ap, ec_n = Ec[par]
                    T_ap = T[:, h, dr0:dr0 + 8, :].rearrange("p a b -> p (a b)")
                    S.op("dve", lambda e: e.scalar_tensor_tensor(
                        out=tm_ap, in0=PS[bS][0:64, :], scalar=scale, in1=T_ap, op0=ALU.mult, op1=ALU.add),
                        reads=[PSN[bS], T_n], writes=[tm_n])
                    S.op("act", lambda e: e.activation(out=el_ap.rearrange("p a b -> p (a b)"), in_=tm_ap, func=AF.Exp), reads=[tm_n], writes=[el_n])
                    S.op("act", lambda e: e.activation(out=ec_ap.rearrange("p a b -> p (a b)"), in_=PS[bC][:, 0:256], func=AF.Exp, scale=scale), reads=[PSN[bC]], writes=[ec_n])

                def na_back(r):
                    rs_ = min(max(r - 4, 0), 56)
                    par = r % 2
                    bS, bC, bO, bZ = (0, 1, 4, 5) if par == 0 else (2, 3, 6, 7)
                    el_ap, el_n = El[par]
                    ec_ap, ec_n = Ec[par]
                    for which in range(2):
                        bb = bO if which == 0 else bZ
                        for j in range(8):
                            kr = rs_ + j
                            lh = vl[:, kr, :] if which == 0 else ones_bf[0:64, :]
                            S.op("pe", lambda e, lh=lh, j=j, bb=bb: e.matmul(out=PS[bb][:, 0:64], lhsT=lh, rhs=el_ap[:, j, :], start=(j == 0), stop=False),
                                 reads=[vl_n, el_n, "ones"], writes=[PSN[bb]])
                        for j in range(4):
                            lh = vc[:, j, :] if which == 0 else ones_bf[:]
                            S.op("pe", lambda e, lh=lh, j=j, bb=bb: e.matmul(out=PS[bb][:, 0:64], lhsT=lh, rhs=ec_ap[:, j, :], start=False, stop=(j == 3)),
                                 reads=[vc_n, ec_n, "ones"], writes=[PSN[bb]])
                    rz_ap, rz_n = rz[par]
                    S.op("act", lambda e: e.activation(out=rz_ap, in_=PS[bZ][:, 0:64], func=AF.Copy), reads=[PSN[bZ]], writes=[rz_n])
                    S.op("dve", lambda e: e.reciprocal(out=rz_ap, in_=rz_ap), reads=[rz_n], writes=[rz_n])
                    S.op("dve", lambda e: e.tensor_tensor(out=ost[:, r * 64:(r + 1) * 64], in0=PS[bO][:, 0:64], in1=rz_ap, op=ALU.mult),
                         reads=[PSN[bO], rz_n], writes=[ost_n])

                na_front(0)
                for r in range(64):
                    if r + 1 < 64:
                        na_front(r + 1)
                    na_back(r)
                for tb in range(8):
                    y_ap, y_n = yo[tb % 2]
                    S.op("dve", lambda e, y_ap=y_ap, tb=tb: e.tensor_tensor(out=y_ap, in0=ost[:, tb * 512:(tb + 1) * 512], in1=cg[:, tb * 512:(tb + 1) * 512], op=ALU.mult),
                         reads=[ost_n, cg_n], writes=[y_n])
                    S.dma("sp", lambda e, y_ap=y_ap, tb=tb, h=h: e.dma_start(out=YT[h * 128:(h + 1) * 128, tb * 512:(tb + 1) * 512], in_=y_ap), reads=[y_n])
            ecnt = [0]
            for s in range(NPS):
                t0 = NS + s * LP
                for h in range(8):
                    S.dma("sp", lambda e, h=h, t0=t0: e.dma_start(out=kt[:, 0:LP], in_=KT[h * 128:(h + 1) * 128, t0:t0 + LP]), writes=[kt_n])
                    S.dma("sp", lambda e, h=h, t0=t0: e.dma_start(out=vc[:, 0:2, :], in_=V[t0:t0 + LP, h * 128:(h + 1) * 128].rearrange("(c p) d -> p c d", p=128)), writes=[vc_n])
                    S.dma("sp", lambda e, h=h, t0=t0: e.dma_start(out=qh[:, 0:LP], in_=QT[h * 128:(h + 1) * 128, t0:t0 + LP]), writes=[qh_n])
                    S.dma("sp", lambda e, h=h, t0=t0: e.dma_start(out=cg[:, 0:LP], in_=BGT[h * 128:(h + 1) * 128, t0:t0 + LP]), writes=[cg_n])
                    keys = [(kt[:, i * 128:(i + 1) * 128], vc[:, i, :], 128) for i in range(2)]
                    attention(keys, qh[:, 0:LP], qh_n, LP, False, scale, [(0,), (1,), (2,), (3,)], [kt_n, vc_n], E2, ecnt)
                    rz_ap, rz_n = tm[0][0], tm[0][1]
                    S.op("act", lambda e: e.activation(out=ost[:, 512:512 + LP], in_=PS[5][:, 0:LP], func=AF.Copy), reads=[PSN[5]], writes=[ost_n])
                    S.op("dve", lambda e: e.reciprocal(out=ost[:, 512:512 + LP], in_=ost[:, 512:512 + LP]), reads=[ost_n], writes=[ost_n])
                    S.op("dve", lambda e: e.tensor_tensor(out=ost[:, 0:LP], in0=PS[4][:, 0:LP], in1=ost[:, 512:512 + LP], op=ALU.mult), reads=[PSN[4], ost_n], writes=[ost_n])
                    y_ap, y_n = yo[h % 2]
                    S.op("dve", lambda e, y_ap=y_ap: e.tensor_tensor(out=y_ap[:, 0:LP], in0=ost[:, 0:LP], in1=cg[:, 0:LP], op=ALU.mult), reads=[ost_n, cg_n], writes=[y_n])
                    S.dma("sp", lambda e, y_ap=y_ap, h=h, t0=t0: e.dma_start(out=YT[h * 128:(h + 1) * 128, t0:t0 + LP], in_=y_ap[:, 0:LP]), reads=[y_n])
            S.barrier()

        def stage_sgu(l):
            A.reset()
            gl, gl_n = A.alloc(F32, 1024)
            bs, bs_n = A.alloc(F32, 4, 512)
            wsf, wsf_n = A.alloc(F32, 4, 128)
            wsT, wsT_n = A.alloc(BF16, 4, 128)
            S.dma("sp", lambda e: e.dma_start(out=gl, in_=sguln[l]), writes=[gl_n])
            S.dma("sp", lambda e: e.dma_start(out=bs, in_=sgub[l]), writes=[bs_n])
            S.dma("sp", lambda e: e.dma_start(out=wsf, in_=sguw[l].rearrange("g i j -> i g j")), writes=[wsf_n])
            for g in range(4):
                S.op("pe", lambda e, g=g: e.transpose(out=PS[0][:, g * 128:(g + 1) * 128], in_=wsf[:, g, :], identity=ident[:]), reads=[wsf_n, "ident"], writes=[PSN[0]])
            S.op("act", lambda e: e.activation(out=wsT.rearrange("p a b -> p (a b)"), in_=PS[0][:], func=AF.Copy), reads=[PSN[0]], writes=[wsT_n])
            vsf = [A.alloc(F32, 1024) for _ in range(2)]
            vnf = [A.alloc(F32, 1024) for _ in range(2)]
            vn = [A.alloc(BF16, 4, 1024) for _ in range(2)]
            st = [A.alloc(F32, 24) for _ in range(2)]
            ut = [A.alloc(BF16, 512) for _ in range(3)]
            dg = [A.alloc(BF16, 512) for _ in range(3)]
            t1 = [A.alloc(F32, 512) for _ in range(2)]
            yo = [A.alloc(BF16, 512) for _ in range(3)]
            c = {"v": 0, "u": 0, "ps": 0, "y": 0}
            for tb in range(NT // 512):
                vn_ap, vn_n = vn[tb % 2]
                for ti in range(4):
                    r0 = tb * 512 + ti * 128
                    v_ap, v_n = vsf[c["v"] % 2]
                    f_ap, f_n = vnf[c["v"] % 2]
                    s_ap, s_n = st[c["v"] % 2]
                    c["v"] += 1
                    S.dma("sp", lambda e, v_ap=v_ap, r0=r0: e.dma_start(out=v_ap, in_=VS[r0:r0 + 128, :]), writes=[v_n])
                    for k2 in range(2):
                        S.op("dve", lambda e, s_ap=s_ap, v_ap=v_ap, k2=k2: e.bn_stats(out=s_ap[:, k2 * 6:(k2 + 1) * 6], in_=v_ap[:, k2 * 512:(k2 + 1) * 512]), reads=[v_n], writes=[s_n])
                    S.op("dve", lambda e, s_ap=s_ap: e.bn_aggr(out=s_ap[:, 12:14], in_=s_ap[:, 0:12]), reads=[s_n], writes=[s_n])
                    S.op("act", lambda e, s_ap=s_ap: e.activation(out=s_ap[:, 14:15], in_=s_ap[:, 13:14], func=AF.Sqrt, bias=eps_col, scale=1.0), reads=[s_n, "eps"], writes=[s_n])
                    S.op("dve", lambda e, s_ap=s_ap: e.reciprocal(out=s_ap[:, 14:15], in_=s_ap[:, 14:15]), reads=[s_n], writes=[s_n])
                    S.op("dve", lambda e, s_ap=s_ap: e.scalar_tensor_tensor(out=s_ap[:, 15:16], in0=s_ap[:, 12:13], scalar=-1.0, in1=s_ap[:, 14:15], op0=ALU.mult, op1=ALU.mult), reads=[s_n], writes=[s_n])
                    S.op("act", lambda e, f_ap=f_ap, v_ap=v_ap, s_ap=s_ap: e.activation(out=f_ap, in_=v_ap, func=AF.Identity, scale=s_ap[:, 14:15], bias=s_ap[:, 15:16]), reads=[v_n, s_n], writes=[f_n])
                    S.op("dve", lambda e, vn_ap=vn_ap, ti=ti, f_ap=f_ap: e.tensor_tensor(out=vn_ap[:, ti, :], in0=f_ap, in1=gl, op=ALU.mult), reads=[f_n, gl_n], writes=[vn_n])
                for cc in range(8):
                    g = cc // 2
                    b = c["ps"] % 4
                    c["ps"] += 1
                    for ti in range(4):
                        S.op("pe", lambda e, vn_ap=vn_ap, ti=ti, cc=cc, g=g, b=b: e.matmul(out=PS[b][:, ti * 128:(ti + 1) * 128], lhsT=vn_ap[:, ti, cc * 128:(cc + 1) * 128], rhs=wsT[:, g, :], start=True, stop=True),
                             reads=[vn_n, wsT_n], writes=[PSN[b]])
                    u_ap, u_n = ut[c["u"] % 3]
                    d_ap, d_n = dg[c["u"] % 3]
                    c["u"] += 1
                    S.dma("sp", lambda e, u_ap=u_ap, cc=cc, tb=tb: e.dma_start(out=u_ap, in_=AINT[cc * 128:(cc + 1) * 128, tb * 512:(tb + 1) * 512]), writes=[u_n])
                    S.dma("sp", lambda e, d_ap=d_ap, cc=cc, tb=tb: e.dma_start(out=d_ap, in_=AGT[cc * 128:(cc + 1) * 128, tb * 512:(tb + 1) * 512]), writes=[d_n])
                    t_ap, t_n = t1[c["y"] % 2]
                    y_ap, y_n = yo[c["y"] % 3]
                    c["y"] += 1
                    S.op("dve", lambda e, t_ap=t_ap, b=b, g=g: e.tensor_tensor(out=t_ap, in0=PS[b][:], in1=bs[:, g, :], op=ALU.add), reads=[PSN[b], bs_n], writes=[t_n])
                    S.op("dve", lambda e, t_ap=t_ap, u_ap=u_ap: e.tensor_tensor(out=t_ap, in0=t_ap, in1=u_ap, op=ALU.mult), reads=[t_n, u_n], writes=[t_n])
                    S.op("dve", lambda e, t_ap=t_ap, d_ap=d_ap, y_ap=y_ap: e.tensor_tensor(out=y_ap, in0=t_ap, in1=d_ap, op=ALU.mult), reads=[t_n, d_n], writes=[y_n])
                    S.dma("sp", lambda e, y_ap=y_ap, cc=cc, tb=tb: e.dma_start(out=YT[1024 + cc * 128:1024 + (cc + 1) * 128, tb * 512:(tb + 1) * 512], in_=y_ap), reads=[y_n])
            S.barrier()

        def stage_O(l, last):
            A.reset()
            src = xin if l == 0 else X
            dst = yout if last else X
            wo, wo_n = A.alloc(BF16, 16, D)
            for g in range(4):
                S.dma("pool", lambda e, g=g: e.dma_start(out=wo[:, :, g * 512:(g + 1) * 512], in_=wview(wout[l], g)), writes=[wo_n])
            yT = [A.alloc(BF16, 16, 256) for _ in range(2)]
            NB_O = 4
            NB_X = 5
            xt = [A.alloc(F32, D) for _ in range(NB_X)]
            tt = [A.alloc(F32, D) for _ in range(NB_O)]
            st = [A.alloc(F32, 32) for _ in range(NB_O)]
            ntile = NT // 128
            ycur = {}

            def o_load(n):
                blk, ti = n // 2, n % 2
                if ti == 0:
                    y_ap, y_n = yT[blk % 2]
                    S.dma("sp", lambda e: e.dma_start(out=y_ap, in_=YT.rearrange("(c p) t -> p c t", p=128)[:, :, blk * 256:(blk + 1) * 256]), writes=[y_n])
                    ycur[blk] = (y_ap, y_n)
                r0 = n * 128
                x_ap, x_n = xt[n % NB_X]
                S.dma("sp", lambda e: e.dma_start(out=x_ap, in_=src[r0:r0 + 128, :]), writes=[x_n])

            def o_front(n):
                blk, ti = n // 2, n % 2
                y_ap, y_n = ycur[blk]
                r0 = n * 128
                cnd = 0 if r0 < NS else 1
                x_ap, x_n = xt[n % NB_X]
                t_ap, t_n = tt[n % NB_O]
                pb = (n % 2) * 4
                for g in range(4):
                    for kc in range(16):
                        S.op("pe", lambda e, kc=kc, g=g: e.matmul(out=PS[pb + g][:], lhsT=y_ap[:, kc, ti * 128:(ti + 1) * 128], rhs=wo[:, kc, g * 512:(g + 1) * 512],
                                                              start=(kc == 0), stop=(kc == 15)), reads=[y_n, wo_n], writes=[PSN[pb + g]])
                    S.op("dve", lambda e, g=g: e.tensor_tensor(out=t_ap[:, g * 512:(g + 1) * 512], in0=PS[pb + g][:], in1=G[:, cnd, g * 512:(g + 1) * 512], op=ALU.mult),
                         reads=[PSN[pb + g], "G"], writes=[t_n])
                S.op("dve", lambda e: e.scalar_tensor_tensor(out=t_ap, in0=x_ap, scalar=ALPHA, in1=t_ap, op0=ALU.mult, op1=ALU.add),
                     reads=[x_n, t_n], writes=[t_n])

            def o_mid1(n):
                t_ap, t_n = tt[n % NB_O]
                s_ap, s_n = st[n % NB_O]
                for k4 in range(4):
                    S.op("dve", lambda e, k4=k4: e.bn_stats(out=s_ap[:, k4 * 6:(k4 + 1) * 6], in_=t_ap[:, k4 * 512:(k4 + 1) * 512]), reads=[t_n], writes=[s_n])
                S.op("dve", lambda e: e.bn_aggr(out=s_ap[:, 24:26], in_=s_ap[:, 0:24]), reads=[s_n], writes=[s_n])
                S.op("act", lambda e: e.activation(out=s_ap[:, 26:27], in_=s_ap[:, 25:26], func=AF.Sqrt, bias=eps_col, scale=1.0), reads=[s_n, "eps"], writes=[s_n])

            def o_mid2(n):
                u_ap, u_n = xt[n % NB_X]
                t_ap, t_n = tt[n % NB_O]
                s_ap, s_n = st[n % NB_O]
                S.op("dve", lambda e: e.reciprocal(out=s_ap[:, 26:27], in_=s_ap[:, 26:27]), reads=[s_n], writes=[s_n])
                S.op("dve", lambda e: e.scalar_tensor_tensor(out=s_ap[:, 27:28], in0=s_ap[:, 24:25], scalar=-1.0, in1=s_ap[:, 26:27], op0=ALU.mult, op1=ALU.mult), reads=[s_n], writes=[s_n])
                S.op("act", lambda e: e.activation(out=u_ap, in_=t_ap, func=AF.Identity, scale=s_ap[:, 26:27], bias=s_ap[:, 27:28]), reads=[t_n, s_n], writes=[u_n])

            def o_back(n):
                r0 = n * 128
                u_ap, u_n = xt[n % NB_X]
                S.op("dve", lambda e: e.tensor_tensor(out=u_ap, in0=u_ap, in1=lngb[:, 0, :], op=ALU.mult), reads=[u_n, "lngb"], writes=[u_n])
                S.op("dve", lambda e: e.tensor_tensor(out=u_ap, in0=u_ap, in1=lngb[:, 1, :], op=ALU.add), reads=[u_n, "lngb"], writes=[u_n])
                S.dma("sp", lambda e: e.dma_start(out=dst[r0:r0 + 128, :], in_=u_ap), reads=[u_n])

            o_load(0)
            for step in range(ntile + 3):
                if step + 1 < ntile:
                    o_load(step + 1)
                if step < ntile:
                    o_front(step)
                if 0 <= step - 1 < ntile:
                    o_mid1(step - 1)
                if 0 <= step - 2 < ntile:
                    o_mid2(step - 2)
                if 0 <= step - 3 < ntile:
                    o_back(step - 3)
            S.barrier()

        def on(name):
            return stages is None or name in stages

        for l in range(n_layers):
            if on("mod"):
                stage_mod(l)
            if on("P"):
                stage_P(l)
            if l % 2 == 0:
                if on("pool"):
                    stage_pool(l)
                if on("diff"):
                    stage_diff(l)
            else:
                if on("na"):
                    stage_na(l)
                if on("sgu"):
                    stage_sgu(l)
            if on("O"):
                stage_O(l, last=(l == n_layers - 1))
        S.wait_events("sp", S.all_events())
        S.emit()
        nins = S.nins
    return nc, nins


def _consts():
    ident = np.eye(128, dtype=np.float32)
    permT = np.zeros((128, 128), np.float32)
    for i in range(128):
        d = i % 64
        half = (d % 32) // 16
        p = i + 16 if half == 0 else i - 16
        permT[p, i] = 1.0
    t = np.arange(NS)
    row = (t // 64).astype(np.float32)
    col = (t % 64).astype(np.float32)
    inv = (1.0 / (10000.0 ** (np.arange(0, 32, 2, dtype=np.float32) / 32.0))).astype(np.float32)
    ropec = np.zeros((128, NS), np.float32)
    ropes = np.zeros((128, NS), np.float32)
    for i in range(128):
        d = i % 64
        axis = d // 32
        j = d % 16
        half = (d % 32) // 16
        pos = row if axis == 0 else col
        ang = (pos * inv[j]).astype(np.float32)
        ropec[i] = np.cos(ang)
        ropes[i] = np.sin(ang) * (-1.0 if half == 0 else 1.0)
    pedge = np.zeros((128, 4, 16), np.float32)
    for g, w in enumerate(POOL_WINDOWS):
        half = w // 2
        for i in range(8):
            pedge[:, g, i] = 1.0 / min(w, i + half)
            pedge[:, g, 8 + i] = 1.0 / min(w, 8 - i + half)
    return ident, permT, ropec, ropes, pedge


def _rpb_table(rpb):
    kc = np.arange(64)[:, None]
    qc = np.arange(64)[None, :]
    cstart = np.clip(qc - 8, 0, 48)
    valid = (kc >= cstart) & (kc < cstart + 16)
    dc = np.clip(kc - qc + 15, 0, 30)
    g = rpb[:, :, dc]
    g = np.where(valid[None, None], g, np.float32(NEG)).astype(np.float32)
    return np.ascontiguousarray(np.transpose(g, (2, 0, 1, 3)))


def _rep(v, n=128):
    return np.ascontiguousarray(np.broadcast_to(np.asarray(v, np.float32).reshape(1, -1), (n, np.asarray(v).size)))


_CACHE = {}


def make_in_maps(inputs, n_layers=4):
    f = lambda a: np.ascontiguousarray(np.asarray(a, dtype=np.float32))
    ident, permT, ropec, ropes, pedge = _consts()
    shared = {"ident": ident, "permT": permT, "ropec": ropec, "ropes": ropes, "pedge": pedge}
    for l in range(n_layers):
        bm = f(inputs[f"b_mod_{l}"])
        shared[f"wmod{l}"] = f(inputs[f"w_mod_{l}"])
        shared[f"bmodT{l}"] = np.ascontiguousarray(bm[:4096].reshape(32, 128).T)
        shared[f"bmodG{l}"] = _rep(bm[4096:])
        shared[f"win{l}"] = f(inputs[f"w_in_{l}"])
        shared[f"wout{l}"] = f(inputs[f"w_out_{l}"])
        shared[f"lng{l}"] = _rep(inputs[f"ln_g_{l}"])
        shared[f"lnb{l}"] = _rep(inputs[f"ln_b_{l}"])
        if l % 2 == 0:
            shared[f"poolw{l}"] = f(inputs[f"pool_w_{l}"])
            shared[f"pscale{l}"] = np.ascontiguousarray(f(inputs[f"pool_scale_{l}"]).reshape(8, 128).T)
            shared[f"dlam{l}"] = _rep(f(inputs[f"diff_lam_{l}"]).reshape(-1))
            shared[f"subln{l}"] = np.ascontiguousarray(f(inputs[f"diff_subln_{l}"]).reshape(128, 1))
        else:
            shared[f"rpbT{l}"] = _rpb_table(f(inputs[f"rpb_{l}"]))
            shared[f"sguln{l}"] = _rep(inputs[f"sgu_ln_{l}"])
            shared[f"sguw{l}"] = f(inputs[f"sgu_w_{l}"])
            sb = f(inputs[f"sgu_b_{l}"])
            shared[f"sgub{l}"] = np.ascontiguousarray(np.broadcast_to(np.tile(sb, (1, 4))[None], (128, 4, 512)))
    xs = f(inputs["x_sample"])
    xp = f(inputs["x_prompt"])
    c = f(inputs["c"])
    cctx = f(inputs["c_ctx"])
    maps = []
    for core in range(8):
        p = core // 2
        m = dict(shared)
        m["xin"] = np.ascontiguousarray(np.concatenate([xs[p], xp[4 * core:4 * core + 4].reshape(NPS * LP, D)], axis=0))
        cv_ = np.stack([c[p], cctx], 0).reshape(2, 16, 128)
        m["cvec"] = np.ascontiguousarray(np.transpose(cv_, (2, 0, 1)))
        for l in range(n_layers):
            m[f"ck{l}"] = f(inputs[f"cache_k_l{l}"])[p]
            m[f"cv{l}"] = f(inputs[f"cache_v_l{l}"])[p]
        maps.append(m)
    return maps


def kernel(**inputs):
    if "nc" not in _CACHE:
        _CACHE["nc"] = build_program(4)[0]
    nc = _CACHE["nc"]
    maps = make_in_maps(inputs)
    res = run_bass_kernel_spmd(nc, maps, core_ids=list(range(8)))
    r = res.results
    y_prompt = np.concatenate([r[cidx]["yout"][NS:].reshape(NPS, LP, D) for cidx in range(8)], axis=0)
    y_sample = np.stack([r[2 * p]["yout"][:NS] for p in range(4)], axis=0)
    outs = [y_prompt.astype(np.float32), y_sample.astype(np.float32)]
    for l in range(4):
        outs.append(np.concatenate([r[cidx][f"kout{l}"].reshape(NPS, 8, LP, 128) for cidx in range(8)], axis=0).astype(np.float32))
        outs.append(np.concatenate([r[cidx][f"vout{l}"].reshape(NPS, 8, LP, 128) for cidx in range(8)], axis=0).astype(np.float32))
    return tuple(outs)
```

```python
import math
import os
from contextlib import ExitStack

import numpy as np
import concourse.bass as bass
import concourse.mybir as mybir
from concourse.bass_utils import run_bass_kernel_spmd

F32 = mybir.dt.float32
BF16 = mybir.dt.bfloat16
AF = mybir.ActivationFunctionType
ALU = mybir.AluOpType

D = 2048
NS = 4096
NPS = 4
LP = 256
NT = NS + NPS * LP
PAST = 512
NEG = -30000.0
LN_EPS = 1e-5
ALPHA = (2 * 4) ** 0.25
POOL_WINDOWS = (2, 4, 8, 16)


class Tok:
    __slots__ = ("w", "r")

    def __init__(self):
        self.w = None
        self.r = []


class Sched:
    ENG = ("pe", "act", "dve", "pool", "sp")

    def __init__(self, nc, es):
        self.nc = nc
        ndma = {"sp": 28, "pool": 12}
        self.sem = {k: es.enter_context(nc.semaphore("s_" + k)) for k in ("pe", "act", "dve", "pool")}
        self.cnt = {k: 0 for k in self.sem}
        self.dsem = {q: [es.enter_context(nc.semaphore(f"d_{q}{i}")) for i in range(n)] for q, n in ndma.items()}
        self.dcnt = {q: [0] * n for q, n in ndma.items()}
        self.dnext = {q: 0 for q in ndma}
        self.waited = {e: {} for e in self.ENG}
        self.prog = {e: [] for e in self.ENG}
        self.toks = {}
        self.semobj = {}
        self.nins = 0

    def tok(self, key):
        t = self.toks.get(key)
        if t is None:
            t = self.toks[key] = Tok()
        return t

    def _deps(self, e, reads, writes):
        deps = {}

        def add(ev):
            if ev is None:
                return
            s, v = ev
            k = id(s)
            self.semobj[k] = s
            if deps.get(k, 0) < v:
                deps[k] = v

        for t in reads:
            add(t.w)
        for t in writes:
            add(t.w)
            for ev in t.r:
                add(ev)
        waits = []
        own = id(self.sem["pe"]) if e == "pe" else None
        for k, v in deps.items():
            if k == own:
                continue
            if self.waited[e].get(k, 0) < v:
                self.waited[e][k] = v
                waits.append((self.semobj[k], v))
        return waits

    def _finish(self, ev, reads, writes):
        for t in reads:
            t.r.append(ev)
            if len(t.r) > 64:
                best = {}
                for s, v in t.r:
                    if best.get(id(s), (None, 0))[1] < v:
                        best[id(s)] = (s, v)
                t.r = list(best.values())
        for t in writes:
            t.w = ev
            t.r = []

    def _toks(self, lst):
        return [self.tok(t) if not isinstance(t, Tok) else t for t in lst]

    def op(self, e, fn, reads=(), writes=()):
        reads = self._toks(reads)
        writes = self._toks(writes)
        waits = self._deps(e, reads, writes)
        self.cnt[e] += 1
        ev = (self.sem[e], self.cnt[e])
        self.prog[e].append((waits, fn, ev, 1))
        self._finish(ev, reads, writes)
        self.nins += 1 + len(waits)
        return ev

    def dma(self, q, fn, reads=(), writes=()):
        reads = self._toks(reads)
        writes = self._toks(writes)
        waits = self._deps(q, reads, writes)
        j = self.dnext[q]
        self.dnext[q] = (j + 1) % len(self.dsem[q])
        s = self.dsem[q][j]
        prev = self.dcnt[q][j]
        k = id(s)
        self.semobj[k] = s
        if prev > 0 and self.waited[q].get(k, 0) < prev:
            self.waited[q][k] = prev
            waits.append((s, prev))
        self.dcnt[q][j] = prev + 16
        ev = (s, prev + 16)
        self.prog[q].append((waits, fn, ev, 16))
        self._finish(ev, reads, writes)
        self.nins += 1 + len(waits)
        return ev

    def wait_events(self, e, evs):
        waits = []
        for s, v in evs:
            k = id(s)
            if self.waited[e].get(k, 0) < v:
                self.waited[e][k] = v
                waits.append((s, v))
        if waits:
            self.prog[e].append((waits, None, None, 0))
            self.nins += len(waits)

    def all_events(self):
        evs = [(self.sem[k], self.cnt[k]) for k in self.sem if self.cnt[k] > 0]
        for q in self.dsem:
            for s, c in zip(self.dsem[q], self.dcnt[q]):
                if c > 0:
                    evs.append((s, c))
        return evs

    def barrier(self):
        evs = self.all_events()
        for e in self.ENG:
            self.wait_events(e, evs)
        self.toks = {}

    def emit(self):
        nc = self.nc
        prog = self.prog

        def run(eng, lst):
            for waits, fn, ev, inc in lst:
                for s, v in waits:
                    eng.wait_ge(s, v)
                if fn is not None:
                    fn(eng).then_inc(ev[0], inc)

        with nc.Block() as block:
            @block.tensor
            def _(eng):
                run(eng, prog["pe"])

            @block.scalar
            def _(eng):
                run(eng, prog["act"])

            @block.vector
            def _(eng):
                run(eng, prog["dve"])

            @block.gpsimd
            def _(eng):
                run(eng, prog["pool"])

            @block.sync
            def _(eng):
                run(eng, prog["sp"])


class Arena:
    def __init__(self, handle, nbytes):
        self.h32 = handle
        self.h16 = handle.bitcast(BF16)
        self.nbytes = nbytes
        self.off = 0
        self.uid = 0

    def reset(self):
        self.off = 0

    def alloc(self, dt, *shape, parts=128):
        n = 1
        for s in shape:
            n *= s
        esz = 4 if dt == F32 else 2
        size = (n * esz + 31) // 32 * 32
        assert self.off + size <= self.nbytes, (self.off, size, self.nbytes)
        o = self.off
        self.off += size
        h = self.h32 if dt == F32 else self.h16
        ap = h[0:parts, o // esz:o // esz + n]
        if len(shape) == 2:
            ap = ap.rearrange("p (a b) -> p a b", a=shape[0])
        elif len(shape) == 3:
            ap = ap.rearrange("p (a b c) -> p a b c", a=shape[0], b=shape[1])
        self.uid += 1
        return ap, f"ar{self.uid}"


def build_program(n_layers=4, stages=None):
    nc = bass.Bass("TRN2", target_bir_lowering=False)

    def din(name, shape, dt=F32):
        return nc.dram_tensor(name, list(shape), dt, kind="ExternalInput").ap()

    def dout(name, shape, dt=F32):
        return nc.dram_tensor(name, list(shape), dt, kind="ExternalOutput").ap()

    def dscr(name, shape, dt):
        return nc.dram_tensor(name, list(shape), dt, kind="Internal").ap()

    xin = din("xin", [NT, D])
    cvec = din("cvec", [128, 2, 16])
    NL = n_layers
    ck = [din(f"ck{l}", [8, PAST, 128]) for l in range(NL)]
    cv = [din(f"cv{l}", [8, PAST, 128]) for l in range(NL)]
    wmod = [din(f"wmod{l}", [D, 3 * D]) for l in range(NL)]
    bmodT = [din(f"bmodT{l}", [128, 32]) for l in range(NL)]
    bmodG = [din(f"bmodG{l}", [128, D]) for l in range(NL)]
    win = [din(f"win{l}", [D, 6144 if l % 2 == 0 else 7168]) for l in range(NL)]
    wout = [din(f"wout{l}", [D, D]) for l in range(NL)]
    lng = [din(f"lng{l}", [128, D]) for l in range(NL)]
    lnb = [din(f"lnb{l}", [128, D]) for l in range(NL)]
    poolw = {l: din(f"poolw{l}", [4, 256, 256]) for l in (0, 2) if l < NL}
    pscale = {l: din(f"pscale{l}", [128, 8]) for l in (0, 2) if l < NL}
    dlam = {l: din(f"dlam{l}", [128, 256]) for l in (0, 2) if l < NL}
    subln = {l: din(f"subln{l}", [128, 1]) for l in (0, 2) if l < NL}
    rpbT = {l: din(f"rpbT{l}", [64, 8, 15, 64]) for l in (1, 3) if l < NL}
    sguln = {l: din(f"sguln{l}", [128, 1024]) for l in (1, 3) if l < NL}
    sguw = {l: din(f"sguw{l}", [4, 128, 128]) for l in (1, 3) if l < NL}
    sgub = {l: din(f"sgub{l}", [128, 4, 512]) for l in (1, 3) if l < NL}
    ident_d = din("ident", [128, 128])
    permT_d = din("permT", [128, 128])
    ropec_d = din("ropec", [128, NS])
    ropes_d = din("ropes", [128, NS])
    pedge_d = din("pedge", [128, 4, 16])

    yout = dout("yout", [NT, D])
    kout = [dout(f"kout{l}", [NPS * 8 * LP, 128]) for l in range(NL)]
    vout = [dout(f"vout{l}", [NPS * 8 * LP, 128]) for l in range(NL)]

    X = dscr("X", [NT, D], F32)
    QT = dscr("QT", [1024, NT], BF16)
    KT = dscr("KT", [1024, NT], BF16)
    AINT = dscr("AINT", [1024, NT], BF16)
    AGT = dscr("AGT", [1024, NT], BF16)
    BGT = dscr("BGT", [1024, NT], BF16)
    V = dscr("V", [NT, 1024], BF16)
    VS = dscr("VS", [NT, 1024], F32)
    YT = dscr("YT", [D, NT], BF16)

    with ExitStack() as es:
        S = Sched(nc, es)

        def sbt(name, shape, dt):
            return es.enter_context(nc.sbuf_tensor(name, list(shape), dt))

        ident = sbt("ident_s", [128, 128], F32)
        ones_bf = sbt("ones_bf", [128, 128], BF16)
        zeros_bf = sbt("zeros_bf", [128, 128], BF16)
        perm_bf = sbt("perm_bf", [128, 128], BF16)
        lhsc = sbt("lhsc", [128, 2, 16, 128], BF16)
        silc = sbt("silc", [128, 16, 2], BF16)
        sil = sbt("sil", [128, 2, 16], F32)
        mods = sbt("mods", [128, 2, 2, 16], F32)
        G = sbt("G", [128, 2, D], F32)
        lngb = sbt("lngb", [128, 2, D], F32)
        small = sbt("small", [128, 64], F32)
        ARENA_BYTES = 157 * 1024
        arena_h = sbt("arena", [128, ARENA_BYTES // 4], F32)
        A = Arena(arena_h, ARENA_BYTES)
        PSall = es.enter_context(nc.psum_tensor("psall", [128, 4096], F32))
        PS = [PSall[:, i * 512:(i + 1) * 512] for i in range(8)]
        PSN = [f"ps{i}" for i in range(8)]

        eps_col = small[:, 0:1]
        S.dma("sp", lambda e: e.dma_start(out=ident[:], in_=ident_d), writes=["ident"])
        S.dma("pool", lambda e: e.dma_start(out=perm_bf[:], in_=permT_d), writes=["perm"])
        S.dma("sp", lambda e: e.dma_start(out=sil[:], in_=cvec), writes=["sil"])
        S.op("dve", lambda e: e.memset(ones_bf[:], 1.0), writes=["ones"])
        S.op("dve", lambda e: e.memset(zeros_bf[:], 0.0), writes=["zeros"])
        S.op("dve", lambda e: e.memset(eps_col, LN_EPS), writes=["eps"])
        S.op("act", lambda e: e.activation(out=sil[:], in_=sil[:], func=AF.Silu), reads=["sil"], writes=["sil"])
        for cnd in range(2):
            S.op("dve", lambda e, cnd=cnd: e.tensor_copy(out=silc[:, :, cnd], in_=sil[:, cnd, :]), reads=["sil"], writes=["silc"])
            for kc in range(16):
                S.op("dve", lambda e, cnd=cnd, kc=kc: e.tensor_scalar(out=lhsc[:, cnd, kc, :], in0=zeros_bf[:], scalar1=sil[:, cnd, kc:kc + 1],
                                                                      scalar2=None, op0=ALU.add),
                     reads=["sil", "zeros"], writes=["lhsc"])
        S.barrier()

        def wview(w, g):
            return w.rearrange("(c p) n -> p c n", p=128)[:, :, g * 512:(g + 1) * 512]

        def stage_mod(l):
            A.reset()
            wb = [A.alloc(BF16, 16, 512) for _ in range(2)]
            bmT, bmTn = A.alloc(F32, 32)
            bmG, bmGn = A.alloc(F32, D)
            S.dma("sp", lambda e: e.dma_start(out=bmT, in_=bmodT[l]), writes=[bmTn])
            S.dma("sp", lambda e: e.dma_start(out=bmG, in_=bmodG[l]), writes=[bmGn])
            S.dma("sp", lambda e: e.dma_start(out=lngb[:, 0, :], in_=lng[l]), writes=["lngb"])
            S.dma("sp", lambda e: e.dma_start(out=lngb[:, 1, :], in_=lnb[l]), writes=["lngb"])
            psM = PS[7]
            for gi in range(12):
                w_ap, w_n = wb[gi % 2]
                S.dma("pool", lambda e, w_ap=w_ap, gi=gi: e.dma_start(out=w_ap, in_=wview(wmod[l], gi)), writes=[w_n])
                if gi < 8:
                    for j in range(4):
                        cc = gi * 4 + j
                        for kc in range(16):
                            S.op("pe", lambda e, w_ap=w_ap, j=j, kc=kc, cc=cc: e.matmul(
                                out=psM[:, cc * 2:cc * 2 + 2], lhsT=w_ap[:, kc, j * 128:(j + 1) * 128], rhs=silc[:, kc, :],
                                start=(kc == 0), stop=(kc == 15)), reads=[w_n, "silc"], writes=[PSN[7]])
                else:
                    for cnd in range(2):
                        b = (gi * 2 + cnd) % 4
                        for kc in range(16):
                            S.op("pe", lambda e, w_ap=w_ap, cnd=cnd, kc=kc, b=b: e.matmul(
                                out=PS[b][:], lhsT=lhsc[:, cnd, kc, :], rhs=w_ap[:, kc, :], start=(kc == 0), stop=(kc == 15)),
                                reads=[w_n, "lhsc"], writes=[PSN[b]])
                        c0 = (gi - 8) * 512
                        S.op("dve", lambda e, cnd=cnd, b=b, c0=c0: e.tensor_tensor(out=G[:, cnd, c0:c0 + 512], in0=PS[b][:], in1=bmG[:, c0:c0 + 512], op=ALU.add),
                             reads=[PSN[b], bmGn], writes=["G"])
            pv = psM[:, 0:64].rearrange("p (c t) -> p c t", t=2)
            for cnd in range(2):
                S.op("dve", lambda e, cnd=cnd: e.tensor_tensor(out=mods[:, cnd, :, :].rearrange("p a b -> p (a b)"), in0=pv[:, :, cnd], in1=bmT, op=ALU.add),
                     reads=[PSN[7], bmTn], writes=["mods"])
                S.op("dve", lambda e, cnd=cnd: e.tensor_scalar(out=mods[:, cnd, 1, :], in0=mods[:, cnd, 1, :], scalar1=1.0, scalar2=None, op0=ALU.add),
                     reads=["mods"], writes=["mods"])
            S.barrier()

        PDBG = set(os.environ.get('PDBG', 'fm,rope,tm,kv').split(','))

        def stage_P(l):
            even = l % 2 == 0
            ngrp = 12 if even else 14
            src = xin if l == 0 else X
            A.reset()
            hT = [A.alloc(BF16, 16, 1024) for _ in range(2)]
            wb = [A.alloc(BF16, 16, 512) for _ in range(2)]
            xt = [A.alloc(F32, D) for _ in range(2)]
            obf = [A.alloc(BF16, 512) for _ in range(4)]
            of32 = [A.alloc(F32, 512) for _ in range(2)]
            qb = [A.alloc(BF16, 512) for _ in range(2)]
            t1 = [A.alloc(F32, 512) for _ in range(2)]
            t2 = [A.alloc(F32, 512) for _ in range(2)]
            cs = [A.alloc(F32, 2, 512) for _ in range(4)]
            cnt = {"obf": 0, "of32": 0, "ps": 0, "rope": 0, "w": 0, "x": 0, "tp": 0}
            if even:
                kinds = ["ain", "ain", "gA", "gA", "q", "q", "k", "k", "v", "v", "gB", "gB"]
            else:
                kinds = ["q", "q", "k", "k", "v", "v", "gB", "gB", "ain", "ain", "vs", "vs", "gA", "gA"]
            dstT = {"ain": AINT, "gA": AGT, "gB": BGT, "q": QT, "k": KT}
            def tr_tile(tsb, ti):
                tok0 = tsb * 1024
                cnd = 0 if tsb < 4 else 1
                h_ap, h_n = hT[tsb % 2]
                x_ap, x_n = xt[cnt["x"] % 2]
                cnt["x"] += 1
                r0 = tok0 + ti * 128
                S.dma("sp", lambda e: e.dma_start(out=x_ap, in_=src[r0:r0 + 128, :]), writes=[x_n])
                for q4 in range(4):
                    b = 4 + cnt["tp"] % 2
                    cnt["tp"] += 1
                    for j in range(4):
                        kc = q4 * 4 + j
                        S.op("pe", lambda e, kc=kc, j=j, b=b: e.transpose(out=PS[b][:, j * 128:(j + 1) * 128], in_=x_ap[:, kc * 128:(kc + 1) * 128], identity=ident[:]),
                             reads=[x_n, "ident"], writes=[PSN[b]])
                    for j in range(4):
                        kc = q4 * 4 + j
                        S.op("act", lambda e, kc=kc, j=j, b=b: e.activation(
                            out=h_ap[:, kc, ti * 128:(ti + 1) * 128], in_=PS[b][:, j * 128:(j + 1) * 128], func=AF.Identity,
                            scale=mods[:, cnd, 1, kc:kc + 1], bias=mods[:, cnd, 0, kc:kc + 1]),
                            reads=[PSN[b], "mods"], writes=[h_n])

            for ti in range(8):
                tr_tile(0, ti)
            for tsb in range(5):
                tok0 = tsb * 1024
                cnd = 0 if tsb < 4 else 1
                h_ap, h_n = hT[tsb % 2]
                if even and tsb < 4:
                    for hf_ in range(2):
                        cs_ap_, cs_n_ = cs[(tsb % 2) * 2 + hf_]
                        tt_ = tok0 + hf_ * 512
                        S.dma("sp", lambda e, cs_ap_=cs_ap_, tt_=tt_: e.dma_start(out=cs_ap_[:, 0, :], in_=ropec_d[:, tt_:tt_ + 512]), writes=[cs_n_])
                        S.dma("sp", lambda e, cs_ap_=cs_ap_, tt_=tt_: e.dma_start(out=cs_ap_[:, 1, :], in_=ropes_d[:, tt_:tt_ + 512]), writes=[cs_n_])
                for g in range(ngrp):
                    kind = kinds[g]
                    gsub = g % 2
                    if kind == "k" and False:
                        pass
                    w_ap, w_n = wb[cnt["w"] % 2]
                    cnt["w"] += 1
                    S.dma("pool", lambda e, w_ap=w_ap, g=g: e.dma_start(out=w_ap, in_=wview(win[l], g)), writes=[w_n])
                    if kind in ("ain", "gA", "gB", "q", "k") and "fm" in PDBG:
                        for j in range(4):
                            row0 = (gsub * 4 + j) * 128
                            for hf in range(2):
                                b = cnt["ps"] % 4
                                cnt["ps"] += 1
                                t0 = tok0 + hf * 512
                                for kc in range(16):
                                    S.op("pe", lambda e, w_ap=w_ap, h_ap=h_ap, j=j, kc=kc, hf=hf, b=b: e.matmul(
                                        out=PS[b][:], lhsT=w_ap[:, kc, j * 128:(j + 1) * 128], rhs=h_ap[:, kc, hf * 512:(hf + 1) * 512],
                                        start=(kc == 0), stop=(kc == 15)), reads=[w_n, h_n], writes=[PSN[b]])
                                o_ap, o_n = obf[cnt["obf"] % 4]
                                cnt["obf"] += 1
                                if kind in ("gA", "gB"):
                                    S.op("act", lambda e, o_ap=o_ap, b=b: e.activation(out=o_ap, in_=PS[b][:], func=AF.Silu), reads=[PSN[b]], writes=[o_n])
                                elif kind in ("q", "k") and even and tsb < 4 and "rope" in PDBG:
                                    ri = cnt["rope"] % 2
                                    cnt["rope"] += 1
                                    qb_ap, qb_n = qb[ri]
                                    t1_ap, t1_n = t1[ri]
                                    t2_ap, t2_n = t2[ri]
                                    cs_ap, cs_n = cs[(tsb % 2) * 2 + hf]
                                    S.op("act", lambda e, qb_ap=qb_ap, b=b: e.activation(out=qb_ap, in_=PS[b][:], func=AF.Copy), reads=[PSN[b]], writes=[qb_n])
                                    b2 = 6 + ri
                                    S.op("pe", lambda e, qb_ap=qb_ap, b2=b2: e.matmul(out=PS[b2][:], lhsT=perm_bf[:], rhs=qb_ap, start=True, stop=True),
                                         reads=[qb_n, "perm"], writes=[PSN[b2]])
                                    S.op("dve", lambda e, t1_ap=t1_ap, cs_ap=cs_ap, qb_ap=qb_ap: e.tensor_tensor(out=t1_ap, in0=qb_ap, in1=cs_ap[:, 0, :], op=ALU.mult),
                                         reads=[qb_n, cs_n], writes=[t1_n])
                                    S.op("dve", lambda e, t2_ap=t2_ap, cs_ap=cs_ap, b2=b2: e.tensor_tensor(out=t2_ap, in0=PS[b2][:], in1=cs_ap[:, 1, :], op=ALU.mult),
                                         reads=[PSN[b2], cs_n], writes=[t2_n])
                                    S.op("dve", lambda e, o_ap=o_ap, t1_ap=t1_ap, t2_ap=t2_ap: e.tensor_tensor(out=o_ap, in0=t1_ap, in1=t2_ap, op=ALU.add),
                                         reads=[t1_n, t2_n], writes=[o_n])
                                else:
                                    S.op("act", lambda e, o_ap=o_ap, b=b: e.activation(out=o_ap, in_=PS[b][:], func=AF.Copy), reads=[PSN[b]], writes=[o_n])
                                dst = dstT[kind]
                                S.dma("sp", lambda e, o_ap=o_ap, dst=dst, row0=row0, t0=t0: e.dma_start(out=dst[row0:row0 + 128, t0:t0 + 512], in_=o_ap), reads=[o_n])
                    if (kind in ("v", "vs") or (kind == "k" and tsb == 4)) and "tm" in PDBG:
                        for ti in range(8):
                            b = cnt["ps"] % 4
                            cnt["ps"] += 1
                            r0 = tok0 + ti * 128
                            c0 = gsub * 512
                            for kc in range(16):
                                S.op("pe", lambda e, w_ap=w_ap, h_ap=h_ap, kc=kc, ti=ti, b=b: e.matmul(
                                    out=PS[b][:], lhsT=h_ap[:, kc, ti * 128:(ti + 1) * 128], rhs=w_ap[:, kc, :], start=(kc == 0), stop=(kc == 15)),
                                    reads=[w_n, h_n], writes=[PSN[b]])
                            if kind == "v":
                                o_ap, o_n = obf[cnt["obf"] % 4]
                                cnt["obf"] += 1
                                S.op("act", lambda e, o_ap=o_ap, b=b: e.activation(out=o_ap, in_=PS[b][:], func=AF.Copy), reads=[PSN[b]], writes=[o_n])
                                S.dma("sp", lambda e, o_ap=o_ap, r0=r0, c0=c0: e.dma_start(out=V[r0:r0 + 128, c0:c0 + 512], in_=o_ap), reads=[o_n])
                            if (kind == "vs" or tsb == 4) and "kv" in PDBG:
                                f_ap, f_n = of32[cnt["of32"] % 2]
                                cnt["of32"] += 1
                                S.op("act", lambda e, f_ap=f_ap, b=b: e.activation(out=f_ap, in_=PS[b][:], func=AF.Copy), reads=[PSN[b]], writes=[f_n])
                                if kind == "vs":
                                    S.dma("sp", lambda e, f_ap=f_ap, r0=r0, c0=c0: e.dma_start(out=VS[r0:r0 + 128, c0:c0 + 512], in_=f_ap), reads=[f_n])
                                else:
                                    dsto = kout[l] if kind == "k" else vout[l]
                                    sq = ti // 2
                                    tt0 = (ti % 2) * 128
                                    for hh in range(4):
                                        S.dma("sp", lambda e, f_ap=f_ap, dsto=dsto, sq=sq, tt0=tt0, gsub=gsub, hh=hh: e.dma_start(
                                            out=(VS[((sq * 8 + gsub * 4 + hh) * LP + tt0) // 8:((sq * 8 + gsub * 4 + hh) * LP + tt0) // 8 + 128, 0:128] if os.environ.get("KVDBG") else dsto[(sq * 8 + gsub * 4 + hh) * LP + tt0:(sq * 8 + gsub * 4 + hh) * LP + tt0 + 128, :]), in_=f_ap[:, hh * 128:(hh + 1) * 128]), reads=[f_n])
                if tsb + 1 < 5:
                    for ti_ in range(8):
                        tr_tile(tsb + 1, ti_)
            S.barrier()

        def attention(keys, q_ap, q_n, nq, two, scale, bank_set, reads_extra, E_bufs, ecnt):
            nkc = len(keys)
            wide = two and nq == 512

            def emit_S(i):
                kT_ap, v_ap, nk = keys[i]
                sb_ = bank_set[i % len(bank_set)]
                if two:
                    S.op("pe", lambda e: e.matmul(out=PS[sb_[0]][0:nk, 0:nq], lhsT=kT_ap, rhs=q_ap[0], start=True, stop=True),
                         reads=reads_extra + [q_n], writes=[PSN[sb_[0]]])
                    S.op("pe", lambda e: e.matmul(out=PS[sb_[1]][0:nk, 0:nq], lhsT=kT_ap, rhs=q_ap[1], start=True, stop=True),
                         reads=reads_extra + [q_n], writes=[PSN[sb_[1]]])
                else:
                    S.op("pe", lambda e: e.matmul(out=PS[sb_[0]][0:nk, 0:nq], lhsT=kT_ap, rhs=q_ap, start=True, stop=True),
                         reads=reads_extra + [q_n], writes=[PSN[sb_[0]]])

            def emit_pv(i, s_i, rhs_ap, e_n):
                kT_ap, v_ap, nk = keys[i]
                ob, zb = 4 + 2 * s_i, 5 + 2 * s_i
                S.op("pe", lambda e: e.matmul(out=PS[ob][:, 0:nq], lhsT=v_ap, rhs=rhs_ap, start=(i == 0), stop=(i == nkc - 1)),
                     reads=reads_extra + [e_n], writes=[PSN[ob]])
                S.op("pe", lambda e: e.matmul(out=PS[zb][:, 0:nq], lhsT=ones_bf[0:nk, :], rhs=rhs_ap, start=(i == 0), stop=(i == nkc - 1)),
                     reads=[e_n, "ones"], writes=[PSN[zb]])

            def emit_exp(i):
                kT_ap, v_ap, nk = keys[i]
                sb_ = bank_set[i % len(bank_set)]
                pend = []
                if wide:
                    e_ap, e_n = E_bufs[ecnt[0] % len(E_bufs)]
                    ecnt[0] += 1
                    base = sb_[0] * 512
                    S.op("act", lambda e: e.activation(out=e_ap[0:nk, 0:1024], in_=PSall[0:nk, base:base + 1024], func=AF.Exp, scale=scale),
                         reads=[PSN[sb_[0]], PSN[sb_[1]]], writes=[e_n])
                    for s_i in range(2):
                        pend.append((s_i, e_ap[0:nk, s_i * 512:(s_i + 1) * 512], e_n))
                else:
                    for s_i in range(2 if two else 1):
                        e_ap, e_n = E_bufs[ecnt[0] % len(E_bufs)]
                        ecnt[0] += 1
                        S.op("act", lambda e, e_ap=e_ap, s_i=s_i: e.activation(out=e_ap[0:nk, 0:nq], in_=PS[sb_[s_i]][0:nk, 0:nq], func=AF.Exp, scale=scale),
                             reads=[PSN[sb_[s_i]]], writes=[e_n])
                        pend.append((s_i, e_ap[0:nk, 0:nq], e_n))
                return pend

            nset = len(bank_set)
            for i0 in range(min(nset, nkc)):
                emit_S(i0)
            for i in range(nkc):
                pend = emit_exp(i)
                if i + nset < nkc:
                    emit_S(i + nset)
                for (s_i, rhs_ap, e_n) in pend:
                    emit_pv(i, s_i, rhs_ap, e_n)

        def stage_pool(l):
            A.reset()
            pe_t, pe_n = A.alloc(F32, 4, 16)
            psc, psc_n = A.alloc(F32, 8)
            S.dma("sp", lambda e: e.dma_start(out=pe_t, in_=pedge_d), writes=[pe_n])
            S.dma("sp", lambda e: e.dma_start(out=psc, in_=pscale[l]), writes=[psc_n])
            wp = [A.alloc(BF16, 2, 256) for _ in range(4)]
            for g in range(4):
                S.dma("pool", lambda e, g=g: e.dma_start(out=wp[g][0], in_=poolw[l][g].rearrange("(c p) d -> p c d", p=128)), writes=[wp[g][1]])
            NB = NS + 16
            xb = [A.alloc(BF16, NB) for _ in range(2)]
            pa, pa_n = A.alloc(F32, NB)
            pb, pb_n = A.alloc(F32, NB)
            pooled = [A.alloc(BF16, 2, NS) for _ in range(2)]
            ag = [A.alloc(BF16, NS) for _ in range(2)]
            yo = [A.alloc(BF16, 512) for _ in range(3)]
            c = {"x": 0, "ps": 0, "yo": 0, "ag": 0, "pl": 0}
            seqs = [(0, NS)] + [(NS + s * LP, LP) for s in range(NPS)]
            for (t0, L) in seqs:
                N = L + 16
                for g in range(4):
                    w = POOL_WINDOWS[g]
                    half = w // 2
                    pl_ap, pl_n = pooled[c["pl"] % 2]
                    c["pl"] += 1
                    for j in range(2):
                        ch = 2 * g + j
                        x_ap, x_n = xb[c["x"] % 2]
                        c["x"] += 1
                        S.op("pool", lambda e, x_ap=x_ap: e.memset(x_ap[:, 0:8], 0.0), writes=[x_n])
                        S.op("pool", lambda e, x_ap=x_ap, L=L: e.memset(x_ap[:, 8 + L:16 + L], 0.0), writes=[x_n])
                        S.dma("sp", lambda e, x_ap=x_ap, ch=ch, t0=t0, L=L: e.dma_start(out=x_ap[:, 8:8 + L], in_=AINT[ch * 128:(ch + 1) * 128, t0:t0 + L]), writes=[x_n])
                        cur, cur_n, width, n = x_ap, x_n, 1, N
                        bufs = [(pa, pa_n), (pb, pb_n)]
                        bi = 0
                        while width < w:
                            o_ap, o_n = bufs[bi]
                            bi ^= 1
                            n2 = n - width
                            S.op("dve", lambda e, o_ap=o_ap, cur=cur, n2=n2, width=width: e.tensor_tensor(out=o_ap[:, 0:n2], in0=cur[:, 0:n2], in1=cur[:, width:width + n2], op=ALU.add),
                                 reads=[cur_n], writes=[o_n])
                            cur, cur_n, n, width = o_ap, o_n, n2, width * 2
                        s0 = 8 - half
                        S.op("dve", lambda e, pl_ap=pl_ap, j=j, cur=cur, s0=s0, L=L, w=w, x_ap=x_ap: e.scalar_tensor_tensor(
                            out=pl_ap[:, j, 0:L], in0=cur[:, s0:s0 + L], scalar=1.0 / w, in1=x_ap[:, 8:8 + L], op0=ALU.mult, op1=ALU.subtract),
                            reads=[cur_n, x_n], writes=[pl_n])
                        o_ap, o_n = bufs[bi]
                        for (a0, e0) in ((0, 0), (L - 8, 8)):
                            S.op("dve", lambda e, o_ap=o_ap, cur=cur, s0=s0, a0=a0, e0=e0, g=g: e.tensor_tensor(
                                out=o_ap[:, a0:a0 + 8], in0=cur[:, s0 + a0:s0 + a0 + 8], in1=pe_t[:, g, e0:e0 + 8], op=ALU.mult),
                                reads=[cur_n, pe_n], writes=[o_n])
                            S.op("dve", lambda e, o_ap=o_ap, pl_ap=pl_ap, j=j, a0=a0, x_ap=x_ap: e.tensor_tensor(
                                out=pl_ap[:, j, a0:a0 + 8], in0=o_ap[:, a0:a0 + 8], in1=x_ap[:, 8 + a0:16 + a0], op=ALU.subtract),
                                reads=[o_n, x_n], writes=[pl_n])
                    for dch in range(2):
                        ch = 2 * g + dch
                        ag_ap, ag_n = ag[c["ag"] % 2]
                        c["ag"] += 1
                        S.dma("sp", lambda e, ag_ap=ag_ap, ch=ch, t0=t0, L=L: e.dma_start(out=ag_ap[:, 0:L], in_=AGT[ch * 128:(ch + 1) * 128, t0:t0 + L]), writes=[ag_n])
                        nb = max(1, L // 512)
                        bw = min(L, 512)
                        for tb in range(nb):
                            b = c["ps"] % 4
                            c["ps"] += 1
                            for cc in range(2):
                                S.op("pe", lambda e, g=g, cc=cc, dch=dch, pl_ap=pl_ap, tb=tb, bw=bw, b=b: e.matmul(
                                    out=PS[b][:, 0:bw], lhsT=wp[g][0][:, cc, dch * 128:(dch + 1) * 128], rhs=pl_ap[:, cc, tb * 512:tb * 512 + bw],
                                    start=(cc == 0), stop=(cc == 1)), reads=[wp[g][1], pl_n], writes=[PSN[b]])
                            y_ap, y_n = yo[c["yo"] % 3]
                            c["yo"] += 1
                            S.op("dve", lambda e, y_ap=y_ap, b=b, bw=bw, ch=ch, ag_ap=ag_ap, tb=tb: e.scalar_tensor_tensor(
                                out=y_ap[:, 0:bw], in0=PS[b][:, 0:bw], scalar=psc[:, ch:ch + 1], in1=ag_ap[:, tb * 512:tb * 512 + bw], op0=ALU.mult, op1=ALU.mult),
                                reads=[PSN[b], psc_n, ag_n], writes=[y_n])
                            S.dma("sp", lambda e, y_ap=y_ap, ch=ch, t0=t0, tb=tb, bw=bw: e.dma_start(out=YT[ch * 128:(ch + 1) * 128, t0 + tb * 512:t0 + tb * 512 + bw], in_=y_ap[:, 0:bw]),
                                  reads=[y_n])
            S.barrier()

        def stage_diff(l):
            A.reset()
            lam_init = 0.8 - 0.6 * math.exp(-0.3 * l)
            dl, dl_n = A.alloc(F32, 256)
            sub, sub_n = A.alloc(F32, 1)
            lw, lw_n = A.alloc(F32, 136)
            S.dma("sp", lambda e: e.dma_start(out=dl, in_=dlam[l]), writes=[dl_n])
            S.dma("sp", lambda e: e.dma_start(out=sub, in_=subln[l]), writes=[sub_n])
            S.op("dve", lambda e: e.tensor_tensor(out=lw[:, 0:64], in0=dl[:, 0:64], in1=dl[:, 64:128], op=ALU.mult), reads=[dl_n], writes=[lw_n])
            S.op("dve", lambda e: e.tensor_tensor(out=lw[:, 64:128], in0=dl[:, 128:192], in1=dl[:, 192:256], op=ALU.mult), reads=[dl_n], writes=[lw_n])
            S.op("dve", lambda e: e.reduce_sum(out=lw[:, 128:129], in_=lw[:, 0:64], axis=mybir.AxisListType.X), reads=[lw_n], writes=[lw_n])
            S.op("dve", lambda e: e.reduce_sum(out=lw[:, 129:130], in_=lw[:, 64:128], axis=mybir.AxisListType.X), reads=[lw_n], writes=[lw_n])
            S.op("act", lambda e: e.activation(out=lw[:, 130:132], in_=lw[:, 128:130], func=AF.Exp), reads=[lw_n], writes=[lw_n])
            S.op("dve", lambda e: e.tensor_tensor(out=lw[:, 132:133], in0=lw[:, 131:132], in1=lw[:, 130:131], op=ALU.subtract), reads=[lw_n], writes=[lw_n])
            S.op("dve", lambda e: e.tensor_scalar(out=lw[:, 132:133], in0=lw[:, 132:133], scalar1=-lam_init, scalar2=None, op0=ALU.add), reads=[lw_n], writes=[lw_n])
            S.op("dve", lambda e: e.tensor_scalar(out=lw[:, 133:134], in0=sub, scalar1=(1.0 - lam_init), scalar2=None, op0=ALU.mult), reads=[lw_n, sub_n], writes=[lw_n])
            nlam = lw[:, 132:133]
            c1 = lw[:, 133:134]

            NK = NS + PAST
            kt = [A.alloc(BF16, NK) for _ in range(2)]
            vt = [A.alloc(BF16, NK // 128, 128) for _ in range(2)]
            ckf, ckf_n = A.alloc(F32, 4, 128)
            qt = [A.alloc(BF16, 2, 512) for _ in range(2)]
            for q_ap_, q_n_ in qt:
                S.op("dve", lambda e, q_ap_=q_ap_: e.memset(q_ap_.rearrange("p a b -> p (a b)"), 0.0), writes=[q_n_])
            bg = [A.alloc(BF16, 512) for _ in range(2)]
            E = [A.alloc(BF16, 1024) for _ in range(3)]
            tmp = [A.alloc(F32, 512) for _ in range(6)]
            sqb = [A.alloc(BF16, 512) for _ in range(2)]
            yo = [A.alloc(BF16, 512) for _ in range(2)]
            ecnt = [0]
            c = {"q": 0, "y": 0}

            def epilogue(nq, h, tq0, bg_ap, bg_n):
                (r1, r1n), (r2, r2n), (o1, o1n), (o2, o2n), (oo, oon), (rs, rsn) = tmp
                S.op("act", lambda e: e.activation(out=r1[:, 0:nq], in_=PS[5][:, 0:nq], func=AF.Copy), reads=[PSN[5]], writes=[r1n])
                S.op("act", lambda e: e.activation(out=r2[:, 0:nq], in_=PS[7][:, 0:nq], func=AF.Copy), reads=[PSN[7]], writes=[r2n])
                S.op("dve", lambda e: e.reciprocal(out=r1[:, 0:nq], in_=r1[:, 0:nq]), reads=[r1n], writes=[r1n])
                S.op("dve", lambda e: e.reciprocal(out=r2[:, 0:nq], in_=r2[:, 0:nq]), reads=[r2n], writes=[r2n])
                S.op("dve", lambda e: e.tensor_tensor(out=o1[:, 0:nq], in0=PS[4][:, 0:nq], in1=r1[:, 0:nq], op=ALU.mult), reads=[PSN[4], r1n], writes=[o1n])
                S.op("dve", lambda e: e.tensor_tensor(out=o2[:, 0:nq], in0=PS[6][:, 0:nq], in1=r2[:, 0:nq], op=ALU.mult), reads=[PSN[6], r2n], writes=[o2n])
                S.op("dve", lambda e: e.scalar_tensor_tensor(out=oo[:, 0:nq], in0=o2[:, 0:nq], scalar=nlam, in1=o1[:, 0:nq], op0=ALU.mult, op1=ALU.add),
                     reads=[o1n, o2n, lw_n], writes=[oon])
                sq_ap, sq_n = sqb[c["y"] % 2]
                S.op("act", lambda e: e.activation(out=sq_ap[:, 0:nq], in_=oo[:, 0:nq], func=AF.Square), reads=[oon], writes=[sq_n])
                S.op("pe", lambda e: e.matmul(out=PS[0][:, 0:nq], lhsT=ones_bf[:], rhs=sq_ap[:, 0:nq], start=True, stop=True), reads=[sq_n, "ones"], writes=[PSN[0]])
                S.op("act", lambda e: e.activation(out=rs[:, 0:nq], in_=PS[0][:, 0:nq], func=AF.Sqrt, bias=eps_col, scale=1.0 / 128.0), reads=[PSN[0], "eps"], writes=[rsn])
                S.op("dve", lambda e: e.reciprocal(out=rs[:, 0:nq], in_=rs[:, 0:nq]), reads=[rsn], writes=[rsn])
                S.op("dve", lambda e: e.tensor_tensor(out=oo[:, 0:nq], in0=oo[:, 0:nq], in1=rs[:, 0:nq], op=ALU.mult), reads=[oon, rsn], writes=[oon])
                y_ap, y_n = yo[c["y"] % 2]
                c["y"] += 1
                S.op("dve", lambda e: e.scalar_tensor_tensor(out=y_ap[:, 0:nq], in0=oo[:, 0:nq], scalar=c1, in1=bg_ap[:, 0:nq], op0=ALU.mult, op1=ALU.mult),
                     reads=[oon, lw_n, bg_n], writes=[y_n])
                S.dma("sp", lambda e: e.dma_start(out=YT[1024 + h * 128:1024 + (h + 1) * 128, tq0:tq0 + nq], in_=y_ap[:, 0:nq]), reads=[y_n])

            scale = 64 ** -0.5
            bank_sets = [(0, 1), (2, 3)]
            def load_head(h):
                kt_ap, kt_n = kt[h % 2]
                vt_ap, vt_n = vt[h % 2]
                S.dma("sp", lambda e: e.dma_start(out=kt_ap[:, 0:NS], in_=KT[h * 128:(h + 1) * 128, 0:NS]), writes=[kt_n])
                S.dma("sp", lambda e: e.dma_start(out=vt_ap[:, 0:NS // 128, :], in_=V[0:NS, h * 128:(h + 1) * 128].rearrange("(c p) d -> p c d", p=128)), writes=[vt_n])
                S.dma("pool", lambda e: e.dma_start(out=vt_ap[:, NS // 128:NK // 128, :], in_=cv[l][h].rearrange("(c p) d -> p c d", p=128)), writes=[vt_n])
                S.dma("sp", lambda e: e.dma_start(out=ckf, in_=ck[l][h].rearrange("(c p) d -> p c d", p=128)), writes=[ckf_n])

            def ctx_head(h):
                kt_ap, kt_n = kt[h % 2]
                for j in range(4):
                    S.op("pe", lambda e, j=j: e.transpose(out=PS[2][:, j * 128:(j + 1) * 128], in_=ckf[:, j, :], identity=ident[:]), reads=[ckf_n, "ident"], writes=[PSN[2]])
                S.op("act", lambda e: e.activation(out=kt_ap[:, NS:NK], in_=PS[2][:], func=AF.Copy), reads=[PSN[2]], writes=[kt_n])

            qcur = {}

            def load_q(h, qb_i):
                q_ap, q_n = qt[c["q"] % 2]
                bg_ap, bg_n = bg[c["q"] % 2]
                c["q"] += 1
                tq0 = qb_i * 512
                S.dma("sp", lambda e: e.dma_start(out=q_ap[0:64, 0, :], in_=QT[h * 128:h * 128 + 64, tq0:tq0 + 512]), writes=[q_n])
                S.dma("sp", lambda e: e.dma_start(out=q_ap[64:128, 1, :], in_=QT[h * 128 + 64:(h + 1) * 128, tq0:tq0 + 512]), writes=[q_n])
                S.dma("sp", lambda e: e.dma_start(out=bg_ap, in_=BGT[h * 128:(h + 1) * 128, tq0:tq0 + 512]), writes=[bg_n])
                qcur[(h, qb_i)] = (q_ap, q_n, bg_ap, bg_n)

            blocks = [(h, qb_i) for h in range(8) for qb_i in range(NS // 512)]
            load_head(0)
            ctx_head(0)
            load_q(0, 0)
            for bi, (h, qb_i) in enumerate(blocks):
                kt_ap, kt_n = kt[h % 2]
                vt_ap, vt_n = vt[h % 2]
                if bi + 1 < len(blocks):
                    load_q(*blocks[bi + 1])
                if qb_i == 0 and h + 1 < 8:
                    load_head(h + 1)
                keys = [(kt_ap[:, i * 128:(i + 1) * 128], vt_ap[:, i, :], 128) for i in range(NK // 128)]
                q_ap, q_n, bg_ap, bg_n = qcur[(h, qb_i)]
                tq0 = qb_i * 512
                attention(keys, (q_ap[:, 0, :], q_ap[:, 1, :]), q_n, 512, True, scale, bank_sets, [kt_n, vt_n], E, ecnt)
                epilogue(512, h, tq0, bg_ap, bg_n)
                if qb_i == NS // 512 - 1 and h + 1 < 8:
                    ctx_head(h + 1)
            for s in range(NPS):
                t0 = NS + s * LP
                for h in range(8):
                    kt_ap, kt_n = kt[h % 2]
                    vt_ap, vt_n = vt[h % 2]
                    S.dma("sp", lambda e, kt_ap=kt_ap, h=h, t0=t0: e.dma_start(out=kt_ap[:, 0:LP], in_=KT[h * 128:(h + 1) * 128, t0:t0 + LP]), writes=[kt_n])
                    S.dma("sp", lambda e, vt_ap=vt_ap, h=h, t0=t0: e.dma_start(out=vt_ap[:, 0:2, :], in_=V[t0:t0 + LP, h * 128:(h + 1) * 128].rearrange("(c p) d -> p c d", p=128)), writes=[vt_n])
                    q_ap, q_n = qt[c["q"] % 2]
                    bg_ap, bg_n = bg[c["q"] % 2]
                    c["q"] += 1
                    S.dma("sp", lambda e, q_ap=q_ap, h=h, t0=t0: e.dma_start(out=q_ap[0:64, 0, 0:LP], in_=QT[h * 128:h * 128 + 64, t0:t0 + LP]), writes=[q_n])
                    S.dma("sp", lambda e, q_ap=q_ap, h=h, t0=t0: e.dma_start(out=q_ap[64:128, 1, 0:LP], in_=QT[h * 128 + 64:(h + 1) * 128, t0:t0 + LP]), writes=[q_n])
                    S.dma("sp", lambda e, bg_ap=bg_ap, h=h, t0=t0: e.dma_start(out=bg_ap[:, 0:LP], in_=BGT[h * 128:(h + 1) * 128, t0:t0 + LP]), writes=[bg_n])
                    keys = [(kt_ap[:, i * 128:(i + 1) * 128], vt_ap[:, i, :], 128) for i in range(2)]
                    attention(keys, (q_ap[:, 0, 0:LP], q_ap[:, 1, 0:LP]), q_n, LP, True, scale, bank_sets, [kt_n, vt_n], E, ecnt)
                    epilogue(LP, h, t0, bg_ap, bg_n)
            S.barrier()

        def stage_na(l):
            A.reset()
            scale = 128 ** -0.5
            T, T_n = A.alloc(F32, 8, 15, 64, parts=64)
            S.dma("sp", lambda e: e.dma_start(out=T, in_=rpbT[l]), writes=[T_n])
            NK = NS + PAST
            kt, kt_n = A.alloc(BF16, NK)
            vl, vl_n = A.alloc(BF16, 64, 128, parts=64)
            vc, vc_n = A.alloc(BF16, 4, 128)
            ckf, ckf_n = A.alloc(F32, 4, 128)
            qh, qh_n = A.alloc(BF16, NS)
            cg, cg_n = A.alloc(BF16, NS)
            ost, ost_n = A.alloc(F32, NS)
            tm = [A.alloc(F32, 512, parts=64) for _ in range(2)]
            El = [A.alloc(BF16, 8, 64, parts=64) for _ in range(2)]
            Ec = [A.alloc(BF16, 4, 64) for _ in range(2)]
            rz = [A.alloc(F32, 64) for _ in range(2)]
            yo = [A.alloc(BF16, 512) for _ in range(2)]
            E2 = [A.alloc(BF16, 512) for _ in range(4)]
            for h in range(8):
                S.dma("sp", lambda e, h=h: e.dma_start(out=kt[:, 0:NS], in_=KT[h * 128:(h + 1) * 128, 0:NS]), writes=[kt_n])
                S.dma("sp", lambda e, h=h: e.dma_start(out=vl, in_=V[0:NS, h * 128:(h + 1) * 128].rearrange("(r p) d -> p r d", p=64)), writes=[vl_n])
                S.dma("pool", lambda e, h=h: e.dma_start(out=vc, in_=cv[l][h].rearrange("(c p) d -> p c d", p=128)), writes=[vc_n])
                S.dma("sp", lambda e, h=h: e.dma_start(out=ckf, in_=ck[l][h].rearrange("(c p) d -> p c d", p=128)), writes=[ckf_n])
                S.dma("sp", lambda e, h=h: e.dma_start(out=qh, in_=QT[h * 128:(h + 1) * 128, 0:NS]), writes=[qh_n])
                S.dma("sp", lambda e, h=h: e.dma_start(out=cg, in_=BGT[h * 128:(h + 1) * 128, 0:NS]), writes=[cg_n])
                for j in range(4):
                    S.op("pe", lambda e, j=j: e.transpose(out=PS[7][:, j * 128:(j + 1) * 128], in_=ckf[:, j, :], identity=ident[:]), reads=[ckf_n, "ident"], writes=[PSN[7]])
                S.op("act", lambda e: e.activation(out=kt[:, NS:NK], in_=PS[7][:], func=AF.Copy), reads=[PSN[7]], writes=[kt_n])
                def na_front(r):
                    rs_ = min(max(r - 4, 0), 56)
                    dr0 = rs_ - r + 7
                    par = r % 2
                    bS, bC, bO, bZ = (0, 1, 4, 5) if par == 0 else (2, 3, 6, 7)
                    q_r = qh[:, r * 64:(r + 1) * 64]
                    for j in range(8):
                        kr = rs_ + j
                        S.op("pe", lambda e, j=j, kr=kr: e.matmul(out=PS[bS][0:64, j * 64:(j + 1) * 64], lhsT=kt[:, kr * 64:(kr + 1) * 64], rhs=q_r, start=True, stop=True),
                             reads=[kt_n, qh_n], writes=[PSN[bS]])
                    for j in range(4):
                        S.op("pe", lambda e, j=j: e.matmul(out=PS[bC][:, j * 64:(j + 1) * 64], lhsT=kt[:, NS + j * 128:NS + (j + 1) * 128], rhs=q_r, start=True, stop=True),
                             reads=[kt_n, qh_n], writes=[PSN[bC]])
                    tm_ap, tm_n = tm[par]
                    el_ap, el_n = El[par]
                    ec_ap, ec_n = Ec[par]
                    T_ap = T[:, h, dr0:dr0 + 8, :].rearrange("p a b -> p (a b)")
                    S.op("dve", lambda e: e.scalar_tensor_tensor(
                        out=tm_ap, in0=PS[bS][0:64, :], scalar=scale, in1=T_ap, op0=ALU.mult, op1=ALU.add),
                        reads=[PSN[bS], T_n], writes=[tm_n])
                    S.op("act", lambda e: e.activation(out=el_ap.rearrange("p a b -> p (a b)"), in_=tm_ap, func=AF.Exp), reads=[tm_n], writes=[el_n])
                    S.op("act", lambda e: e.activation(out=ec_ap.rearrange("p a b -> p (a b)"), in_=PS[bC][:, 0:256], func=AF.Exp, scale=scale), reads=[PSN[bC]], writes=[ec_n])

                def na_back(r):
                    rs_ = min(max(r - 4, 0), 56)
                    par = r % 2
                    bS, bC, bO, bZ = (0, 1, 4, 5) if par == 0 else (2, 3, 6, 7)
                    el_ap, el_n = El[par]
                    ec_ap, ec_n = Ec[par]
                    for which in range(2):
                        bb = bO if which == 0 else bZ
                        for j in range(8):
                            kr = rs_ + j
                            lh = vl[:, kr, :] if which == 0 else ones_bf[0:64, :]
                            S.op("pe", lambda e, lh=lh, j=j, bb=bb: e.matmul(out=PS[bb][:, 0:64], lhsT=lh, rhs=el_ap[:, j, :], start=(j == 0), stop=False),
                                 reads=[vl_n, el_n, "ones"], writes=[PSN[bb]])
                        for j in range(4):
                            lh = vc[:, j, :] if which == 0 else ones_bf[:]
                            S.op("pe", lambda e, lh=lh, j=j, bb=bb: e.matmul(out=PS[bb][:, 0:64], lhsT=lh, rhs=ec_ap[:, j, :], start=False, stop=(j == 3)),
                                 reads=[vc_n, ec_n, "ones"], writes=[PSN[bb]])
                    rz_ap, rz_n = rz[par]
                    S.op("act", lambda e: e.activation(out=rz_ap, in_=PS[bZ][:, 0:64], func=AF.Copy), reads=[PSN[bZ]], writes=[rz_n])
                    S.op("dve", lambda e: e.reciprocal(out=rz_ap, in_=rz_ap), reads=[rz_n], writes=[rz_n])
                    S.op("dve", lambda e: e.tensor_tensor(out=ost[:, r * 64:(r + 1) * 64], in0=PS[bO][:, 0:64], in1=rz_ap, op=ALU.mult),
                         reads=[PSN[bO], rz_n], writes=[ost_n])

                na_front(0)
                for r in range(64):
                    if r + 1 < 64:
                        na_front(r + 1)
                    na_back(r)
                for tb in range(8):
                    y_ap, y_n = yo[tb % 2]
                    S.op("dve", lambda e, y_ap=y_ap, tb=tb: e.tensor_tensor(out=y_ap, in0=ost[:, tb * 512:(tb + 1) * 512], in1=cg[:, tb * 512:(tb + 1) * 512], op=ALU.mult),
                         reads=[ost_n, cg_n], writes=[y_n])
                    S.dma("sp", lambda e, y_ap=y_ap, tb=tb, h=h: e.dma_start(out=YT[h * 128:(h + 1) * 128, tb * 512:(tb + 1) * 512], in_=y_ap), reads=[y_n])
            ecnt = [0]
            for s in range(NPS):
                t0 = NS + s * LP
                for h in range(8):
                    S.dma("sp", lambda e, h=h, t0=t0: e.dma_start(out=kt[:, 0:LP], in_=KT[h * 128:(h + 1) * 128, t0:t0 + LP]), writes=[kt_n])
                    S.dma("sp", lambda e, h=h, t0=t0: e.dma_start(out=vc[:, 0:2, :], in_=V[t0:t0 + LP, h * 128:(h + 1) * 128].rearrange("(c p) d -> p c d", p=128)), writes=[vc_n])
                    S.dma("sp", lambda e, h=h, t0=t0: e.dma_start(out=qh[:, 0:LP], in_=QT[h * 128:(h + 1) * 128, t0:t0 + LP]), writes=[qh_n])
                    S.dma("sp", lambda e, h=h, t0=t0: e.dma_start(out=cg[:, 0:LP], in_=BGT[h * 128:(h + 1) * 128, t0:t0 + LP]), writes=[cg_n])
                    keys = [(kt[:, i * 128:(i + 1) * 128], vc[:, i, :], 128) for i in range(2)]
                    attention(keys, qh[:, 0:LP], qh_n, LP, False, scale, [(0,), (1,), (2,), (3,)], [kt_n, vc_n], E2, ecnt)
                    rz_ap, rz_n = tm[0][0], tm[0][1]
                    S.op("act", lambda e: e.activation(out=ost[:, 512:512 + LP], in_=PS[5][:, 0:LP], func=AF.Copy), reads=[PSN[5]], writes=[ost_n])
                    S.op("dve", lambda e: e.reciprocal(out=ost[:, 512:512 + LP], in_=ost[:, 512:512 + LP]), reads=[ost_n], writes=[ost_n])
                    S.op("dve", lambda e: e.tensor_tensor(out=ost[:, 0:LP], in0=PS[4][:, 0:LP], in1=ost[:, 512:512 + LP], op=ALU.mult), reads=[PSN[4], ost_n], writes=[ost_n])
                    y_ap, y_n = yo[h % 2]
                    S.op("dve", lambda e, y_ap=y_ap: e.tensor_tensor(out=y_ap[:, 0:LP], in0=ost[:, 0:LP], in1=cg[:, 0:LP], op=ALU.mult), reads=[ost_n, cg_n], writes=[y_n])
                    S.dma("sp", lambda e, y_ap=y_ap, h=h, t0=t0: e.dma_start(out=YT[h * 128:(h + 1) * 128, t0:t0 + LP], in_=y_ap[:, 0:LP]), reads=[y_n])
            S.barrier()

        def stage_sgu(l):
            A.reset()
            gl, gl_n = A.alloc(F32, 1024)
            bs, bs_n = A.alloc(F32, 4, 512)
            wsf, wsf_n = A.alloc(F32, 4, 128)
            wsT, wsT_n = A.alloc(BF16, 4, 128)
            S.dma("sp", lambda e: e.dma_start(out=gl, in_=sguln[l]), writes=[gl_n])
            S.dma("sp", lambda e: e.dma_start(out=bs, in_=sgub[l]), writes=[bs_n])
            S.dma("sp", lambda e: e.dma_start(out=wsf, in_=sguw[l].rearrange("g i j -> i g j")), writes=[wsf_n])
            for g in range(4):
                S.op("pe", lambda e, g=g: e.transpose(out=PS[0][:, g * 128:(g + 1) * 128], in_=wsf[:, g, :], identity=ident[:]), reads=[wsf_n, "ident"], writes=[PSN[0]])
            S.op("act", lambda e: e.activation(out=wsT.rearrange("p a b -> p (a b)"), in_=PS[0][:], func=AF.Copy), reads=[PSN[0]], writes=[wsT_n])
            vsf = [A.alloc(F32, 1024) for _ in range(2)]
            vnf = [A.alloc(F32, 1024) for _ in range(2)]
            vn = [A.alloc(BF16, 4, 1024) for _ in range(2)]
            st = [A.alloc(F32, 24) for _ in range(2)]
            ut = [A.alloc(BF16, 512) for _ in range(3)]
            dg = [A.alloc(BF16, 512) for _ in range(3)]
            t1 = [A.alloc(F32, 512) for _ in range(2)]
            yo = [A.alloc(BF16, 512) for _ in range(3)]
            c = {"v": 0, "u": 0, "ps": 0, "y": 0}
            for tb in range(NT // 512):
                vn_ap, vn_n = vn[tb % 2]
                for ti in range(4):
                    r0 = tb * 512 + ti * 128
                    v_ap, v_n = vsf[c["v"] % 2]
                    f_ap, f_n = vnf[c["v"] % 2]
                    s_ap, s_n = st[c["v"] % 2]
                    c["v"] += 1
                    S.dma("sp", lambda e, v_ap=v_ap, r0=r0: e.dma_start(out=v_ap, in_=VS[r0:r0 + 128, :]), writes=[v_n])
                    for k2 in range(2):
                        S.op("dve", lambda e, s_ap=s_ap, v_ap=v_ap, k2=k2: e.bn_stats(out=s_ap[:, k2 * 6:(k2 + 1) * 6], in_=v_ap[:, k2 * 512:(k2 + 1) * 512]), reads=[v_n], writes=[s_n])
                    S.op("dve", lambda e, s_ap=s_ap: e.bn_aggr(out=s_ap[:, 12:14], in_=s_ap[:, 0:12]), reads=[s_n], writes=[s_n])
                    S.op("act", lambda e, s_ap=s_ap: e.activation(out=s_ap[:, 14:15], in_=s_ap[:, 13:14], func=AF.Sqrt, bias=eps_col, scale=1.0), reads=[s_n, "eps"], writes=[s_n])
                    S.op("dve", lambda e, s_ap=s_ap: e.reciprocal(out=s_ap[:, 14:15], in_=s_ap[:, 14:15]), reads=[s_n], writes=[s_n])
                    S.op("dve", lambda e, s_ap=s_ap: e.scalar_tensor_tensor(out=s_ap[:, 15:16], in0=s_ap[:, 12:13], scalar=-1.0, in1=s_ap[:, 14:15], op0=ALU.mult, op1=ALU.mult), reads=[s_n], writes=[s_n])
                    S.op("act", lambda e, f_ap=f_ap, v_ap=v_ap, s_ap=s_ap: e.activation(out=f_ap, in_=v_ap, func=AF.Identity, scale=s_ap[:, 14:15], bias=s_ap[:, 15:16]), reads=[v_n, s_n], writes=[f_n])
                    S.op("dve", lambda e, vn_ap=vn_ap, ti=ti, f_ap=f_ap: e.tensor_tensor(out=vn_ap[:, ti, :], in0=f_ap, in1=gl, op=ALU.mult), reads=[f_n, gl_n], writes=[vn_n])
                for cc in range(8):
                    g = cc // 2
                    b = c["ps"] % 4
                    c["ps"] += 1
                    for ti in range(4):
                        S.op("pe", lambda e, vn_ap=vn_ap, ti=ti, cc=cc, g=g, b=b: e.matmul(out=PS[b][:, ti * 128:(ti + 1) * 128], lhsT=vn_ap[:, ti, cc * 128:(cc + 1) * 128], rhs=wsT[:, g, :], start=True, stop=True),
                             reads=[vn_n, wsT_n], writes=[PSN[b]])
                    u_ap, u_n = ut[c["u"] % 3]
                    d_ap, d_n = dg[c["u"] % 3]
                    c["u"] += 1
                    S.dma("sp", lambda e, u_ap=u_ap, cc=cc, tb=tb: e.dma_start(out=u_ap, in_=AINT[cc * 128:(cc + 1) * 128, tb * 512:(tb + 1) * 512]), writes=[u_n])
                    S.dma("sp", lambda e, d_ap=d_ap, cc=cc, tb=tb: e.dma_start(out=d_ap, in_=AGT[cc * 128:(cc + 1) * 128, tb * 512:(tb + 1) * 512]), writes=[d_n])
                    t_ap, t_n = t1[c["y"] % 2]
                    y_ap, y_n = yo[c["y"] % 3]
                    c["y"] += 1
                    S.op("dve", lambda e, t_ap=t_ap, b=b, g=g: e.tensor_tensor(out=t_ap, in0=PS[b][:], in1=bs[:, g, :], op=ALU.add), reads=[PSN[b], bs_n], writes=[t_n])
                    S.op("dve", lambda e, t_ap=t_ap, u_ap=u_ap: e.tensor_tensor(out=t_ap, in0=t_ap, in1=u_ap, op=ALU.mult), reads=[t_n, u_n], writes=[t_n])
                    S.op("dve", lambda e, t_ap=t_ap, d_ap=d_ap, y_ap=y_ap: e.tensor_tensor(out=y_ap, in0=t_ap, in1=d_ap, op=ALU.mult), reads=[t_n, d_n], writes=[y_n])
                    S.dma("sp", lambda e, y_ap=y_ap, cc=cc, tb=tb: e.dma_start(out=YT[1024 + cc * 128:1024 + (cc + 1) * 128, tb * 512:(tb + 1) * 512], in_=y_ap), reads=[y_n])
            S.barrier()

        def stage_O(l, last):
            A.reset()
            src = xin if l == 0 else X
            dst = yout if last else X
            wo, wo_n = A.alloc(BF16, 16, D)
            for g in range(4):
                S.dma("pool", lambda e, g=g: e.dma_start(out=wo[:, :, g * 512:(g + 1) * 512], in_=wview(wout[l], g)), writes=[wo_n])
            yT = [A.alloc(BF16, 16, 256) for _ in range(2)]
            NB_O = 4
            NB_X = 5
            xt = [A.alloc(F32, D) for _ in range(NB_X)]
            tt = [A.alloc(F32, D) for _ in range(NB_O)]
            st = [A.alloc(F32, 32) for _ in range(NB_O)]
            ntile = NT // 128
            ycur = {}

            def o_load(n):
                blk, ti = n // 2, n % 2
                if ti == 0:
                    y_ap, y_n = yT[blk % 2]
                    S.dma("sp", lambda e: e.dma_start(out=y_ap, in_=YT.rearrange("(c p) t -> p c t", p=128)[:, :, blk * 256:(blk + 1) * 256]), writes=[y_n])
                    ycur[blk] = (y_ap, y_n)
                r0 = n * 128
                x_ap, x_n = xt[n % NB_X]
                S.dma("sp", lambda e: e.dma_start(out=x_ap, in_=src[r0:r0 + 128, :]), writes=[x_n])

            def o_front(n):
                blk, ti = n // 2, n % 2
                y_ap, y_n = ycur[blk]
                r0 = n * 128
                cnd = 0 if r0 < NS else 1
                x_ap, x_n = xt[n % NB_X]
                t_ap, t_n = tt[n % NB_O]
                pb = (n % 2) * 4
                for g in range(4):
                    for kc in range(16):
                        S.op("pe", lambda e, kc=kc, g=g: e.matmul(out=PS[pb + g][:], lhsT=y_ap[:, kc, ti * 128:(ti + 1) * 128], rhs=wo[:, kc, g * 512:(g + 1) * 512],
                                                              start=(kc == 0), stop=(kc == 15)), reads=[y_n, wo_n], writes=[PSN[pb + g]])
                    S.op("dve", lambda e, g=g: e.tensor_tensor(out=t_ap[:, g * 512:(g + 1) * 512], in0=PS[pb + g][:], in1=G[:, cnd, g * 512:(g + 1) * 512], op=ALU.mult),
                         reads=[PSN[pb + g], "G"], writes=[t_n])
                S.op("dve", lambda e: e.scalar_tensor_tensor(out=t_ap, in0=x_ap, scalar=ALPHA, in1=t_ap, op0=ALU.mult, op1=ALU.add),
                     reads=[x_n, t_n], writes=[t_n])

            def o_mid1(n):
                t_ap, t_n = tt[n % NB_O]
                s_ap, s_n = st[n % NB_O]
                for k4 in range(4):
                    S.op("dve", lambda e, k4=k4: e.bn_stats(out=s_ap[:, k4 * 6:(k4 + 1) * 6], in_=t_ap[:, k4 * 512:(k4 + 1) * 512]), reads=[t_n], writes=[s_n])
                S.op("dve", lambda e: e.bn_aggr(out=s_ap[:, 24:26], in_=s_ap[:, 0:24]), reads=[s_n], writes=[s_n])
                S.op("act", lambda e: e.activation(out=s_ap[:, 26:27], in_=s_ap[:, 25:26], func=AF.Sqrt, bias=eps_col, scale=1.0), reads=[s_n, "eps"], writes=[s_n])

            def o_mid2(n):
                u_ap, u_n = xt[n % NB_X]
                t_ap, t_n = tt[n % NB_O]
                s_ap, s_n = st[n % NB_O]
                S.op("dve", lambda e: e.reciprocal(out=s_ap[:, 26:27], in_=s_ap[:, 26:27]), reads=[s_n], writes=[s_n])
                S.op("dve", lambda e: e.scalar_tensor_tensor(out=s_ap[:, 27:28], in0=s_ap[:, 24:25], scalar=-1.0, in1=s_ap[:, 26:27], op0=ALU.mult, op1=ALU.mult), reads=[s_n], writes=[s_n])
                S.op("act", lambda e: e.activation(out=u_ap, in_=t_ap, func=AF.Identity, scale=s_ap[:, 26:27], bias=s_ap[:, 27:28]), reads=[t_n, s_n], writes=[u_n])

            def o_back(n):
                r0 = n * 128
                u_ap, u_n = xt[n % NB_X]
                S.op("dve", lambda e: e.tensor_tensor(out=u_ap, in0=u_ap, in1=lngb[:, 0, :], op=ALU.mult), reads=[u_n, "lngb"], writes=[u_n])
                S.op("dve", lambda e: e.tensor_tensor(out=u_ap, in0=u_ap, in1=lngb[:, 1, :], op=ALU.add), reads=[u_n, "lngb"], writes=[u_n])
                S.dma("sp", lambda e: e.dma_start(out=dst[r0:r0 + 128, :], in_=u_ap), reads=[u_n])

            o_load(0)
            for step in range(ntile + 3):
                if step + 1 < ntile:
                    o_load(step + 1)
                if step < ntile:
                    o_front(step)
                if 0 <= step - 1 < ntile:
                    o_mid1(step - 1)
                if 0 <= step - 2 < ntile:
                    o_mid2(step - 2)
                if 0 <= step - 3 < ntile:
                    o_back(step - 3)
            S.barrier()

        def on(name):
            return stages is None or name in stages

        for l in range(n_layers):
            if on("mod"):
                stage_mod(l)
            if on("P"):
                stage_P(l)
            if l % 2 == 0:
                if on("pool"):
                    stage_pool(l)
                if on("diff"):
                    stage_diff(l)
            else:
                if on("na"):
                    stage_na(l)
                if on("sgu"):
                    stage_sgu(l)
            if on("O"):
                stage_O(l, last=(l == n_layers - 1))
        S.wait_events("sp", S.all_events())
        S.emit()
        nins = S.nins
    return nc, nins


def _consts():
    ident = np.eye(128, dtype=np.float32)
    permT = np.zeros((128, 128), np.float32)
    for i in range(128):
        d = i % 64
        half = (d % 32) // 16
        p = i + 16 if half == 0 else i - 16
        permT[p, i] = 1.0
    t = np.arange(NS)
    row = (t // 64).astype(np.float32)
    col = (t % 64).astype(np.float32)
    inv = (1.0 / (10000.0 ** (np.arange(0, 32, 2, dtype=np.float32) / 32.0))).astype(np.float32)
    ropec = np.zeros((128, NS), np.float32)
    ropes = np.zeros((128, NS), np.float32)
    for i in range(128):
        d = i % 64
        axis = d // 32
        j = d % 16
        half = (d % 32) // 16
        pos = row if axis == 0 else col
        ang = (pos * inv[j]).astype(np.float32)
        ropec[i] = np.cos(ang)
        ropes[i] = np.sin(ang) * (-1.0 if half == 0 else 1.0)
    pedge = np.zeros((128, 4, 16), np.float32)
    for g, w in enumerate(POOL_WINDOWS):
        half = w // 2
        for i in range(8):
            pedge[:, g, i] = 1.0 / min(w, i + half)
            pedge[:, g, 8 + i] = 1.0 / min(w, 8 - i + half)
    return ident, permT, ropec, ropes, pedge


def _rpb_table(rpb):
    kc = np.arange(64)[:, None]
    qc = np.arange(64)[None, :]
    cstart = np.clip(qc - 8, 0, 48)
    valid = (kc >= cstart) & (kc < cstart + 16)
    dc = np.clip(kc - qc + 15, 0, 30)
    g = rpb[:, :, dc]
    g = np.where(valid[None, None], g, np.float32(NEG)).astype(np.float32)
    return np.ascontiguousarray(np.transpose(g, (2, 0, 1, 3)))


def _rep(v, n=128):
    return np.ascontiguousarray(np.broadcast_to(np.asarray(v, np.float32).reshape(1, -1), (n, np.asarray(v).size)))


_CACHE = {}


def make_in_maps(inputs, n_layers=4):
    f = lambda a: np.ascontiguousarray(np.asarray(a, dtype=np.float32))
    ident, permT, ropec, ropes, pedge = _consts()
    shared = {"ident": ident, "permT": permT, "ropec": ropec, "ropes": ropes, "pedge": pedge}
    for l in range(n_layers):
        bm = f(inputs[f"b_mod_{l}"])
        shared[f"wmod{l}"] = f(inputs[f"w_mod_{l}"])
        shared[f"bmodT{l}"] = np.ascontiguousarray(bm[:4096].reshape(32, 128).T)
        shared[f"bmodG{l}"] = _rep(bm[4096:])
        shared[f"win{l}"] = f(inputs[f"w_in_{l}"])
        shared[f"wout{l}"] = f(inputs[f"w_out_{l}"])
        shared[f"lng{l}"] = _rep(inputs[f"ln_g_{l}"])
        shared[f"lnb{l}"] = _rep(inputs[f"ln_b_{l}"])
        if l % 2 == 0:
            shared[f"poolw{l}"] = f(inputs[f"pool_w_{l}"])
            shared[f"pscale{l}"] = np.ascontiguousarray(f(inputs[f"pool_scale_{l}"]).reshape(8, 128).T)
            shared[f"dlam{l}"] = _rep(f(inputs[f"diff_lam_{l}"]).reshape(-1))
            shared[f"subln{l}"] = np.ascontiguousarray(f(inputs[f"diff_subln_{l}"]).reshape(128, 1))
        else:
            shared[f"rpbT{l}"] = _rpb_table(f(inputs[f"rpb_{l}"]))
            shared[f"sguln{l}"] = _rep(inputs[f"sgu_ln_{l}"])
            shared[f"sguw{l}"] = f(inputs[f"sgu_w_{l}"])
            sb = f(inputs[f"sgu_b_{l}"])
            shared[f"sgub{l}"] = np.ascontiguousarray(np.broadcast_to(np.tile(sb, (1, 4))[None], (128, 4, 512)))
    xs = f(inputs["x_sample"])
    xp = f(inputs["x_prompt"])
    c = f(inputs["c"])
    cctx = f(inputs["c_ctx"])
    maps = []
    for core in range(8):
        p = core // 2
        m = dict(shared)
        m["xin"] = np.ascontiguousarray(np.concatenate([xs[p], xp[4 * core:4 * core + 4].reshape(NPS * LP, D)], axis=0))
        cv_ = np.stack([c[p], cctx], 0).reshape(2, 16, 128)
        m["cvec"] = np.ascontiguousarray(np.transpose(cv_, (2, 0, 1)))
        for l in range(n_layers):
            m[f"ck{l}"] = f(inputs[f"cache_k_l{l}"])[p]
            m[f"cv{l}"] = f(inputs[f"cache_v_l{l}"])[p]
        maps.append(m)
    return maps


def kernel(**inputs):
    if "nc" not in _CACHE:
        _CACHE["nc"] = build_program(4)[0]
    nc = _CACHE["nc"]
    maps = make_in_maps(inputs)
    res = run_bass_kernel_spmd(nc, maps, core_ids=list(range(8)))
    r = res.results
    y_prompt = np.concatenate([r[cidx]["yout"][NS:].reshape(NPS, LP, D) for cidx in range(8)], axis=0)
    y_sample = np.stack([r[2 * p]["yout"][:NS] for p in range(4)], axis=0)
    outs = [y_prompt.astype(np.float32), y_sample.astype(np.float32)]
    for l in range(4):
        outs.append(np.concatenate([r[cidx][f"kout{l}"].reshape(NPS, 8, LP, 128) for cidx in range(8)], axis=0).astype(np.float32))
        outs.append(np.concatenate([r[cidx][f"vout{l}"].reshape(NPS, 8, LP, 128) for cidx in range(8)], axis=0).astype(np.float32))
    return tuple(outs)
```

```python
import math
import os
from contextlib import ExitStack

import numpy as np
import concourse.bass as bass
import concourse.mybir as mybir
from concourse.bass_utils import run_bass_kernel_spmd

F32 = mybir.dt.float32
BF16 = mybir.dt.bfloat16
AF = mybir.ActivationFunctionType
ALU = mybir.AluOpType

D = 2048
NS = 4096
NPS = 4
LP = 256
NT = NS + NPS * LP
PAST = 512
NEG = -30000.0
LN_EPS = 1e-5
ALPHA = (2 * 4) ** 0.25
POOL_WINDOWS = (2, 4, 8, 16)


class Tok:
    __slots__ = ("w", "r")

    def __init__(self):
        self.w = None
        self.r = []


class Sched:
    ENG = ("pe", "act", "dve", "pool", "sp")

    def __init__(self, nc, es):
        self.nc = nc
        ndma = {"sp": 28, "pool": 12}
        self.sem = {k: es.enter_context(nc.semaphore("s_" + k)) for k in ("pe", "act", "dve", "pool")}
        self.cnt = {k: 0 for k in self.sem}
        self.dsem = {q: [es.enter_context(nc.semaphore(f"d_{q}{i}")) for i in range(n)] for q, n in ndma.items()}
        self.dcnt = {q: [0] * n for q, n in ndma.items()}
        self.dnext = {q: 0 for q in ndma}
        self.waited = {e: {} for e in self.ENG}
        self.prog = {e: [] for e in self.ENG}
        self.toks = {}
        self.semobj = {}
        self.nins = 0

    def tok(self, key):
        t = self.toks.get(key)
        if t is None:
            t = self.toks[key] = Tok()
        return t

    def _deps(self, e, reads, writes):
        deps = {}

        def add(ev):
            if ev is None:
                return
            s, v = ev
            k = id(s)
            self.semobj[k] = s
            if deps.get(k, 0) < v:
                deps[k] = v

        for t in reads:
            add(t.w)
        for t in writes:
            add(t.w)
            for ev in t.r:
                add(ev)
        waits = []
        own = id(self.sem["pe"]) if e == "pe" else None
        for k, v in deps.items():
            if k == own:
                continue
            if self.waited[e].get(k, 0) < v:
                self.waited[e][k] = v
                waits.append((self.semobj[k], v))
        return waits

    def _finish(self, ev, reads, writes):
        for t in reads:
            t.r.append(ev)
            if len(t.r) > 64:
                best = {}
                for s, v in t.r:
                    if best.get(id(s), (None, 0))[1] < v:
                        best[id(s)] = (s, v)
                t.r = list(best.values())
        for t in writes:
            t.w = ev
            t.r = []

    def _toks(self, lst):
        return [self.tok(t) if not isinstance(t, Tok) else t for t in lst]

    def op(self, e, fn, reads=(), writes=()):
        reads = self._toks(reads)
        writes = self._toks(writes)
        waits = self._deps(e, reads, writes)
        self.cnt[e] += 1
        ev = (self.sem[e], self.cnt[e])
        self.prog[e].append((waits, fn, ev, 1))
        self._finish(ev, reads, writes)
        self.nins += 1 + len(waits)
        return ev

    def dma(self, q, fn, reads=(), writes=()):
        reads = self._toks(reads)
        writes = self._toks(writes)
        waits = self._deps(q, reads, writes)
        j = self.dnext[q]
        self.dnext[q] = (j + 1) % len(self.dsem[q])
        s = self.dsem[q][j]
        prev = self.dcnt[q][j]
        k = id(s)
        self.semobj[k] = s
        if prev > 0 and self.waited[q].get(k, 0) < prev:
            self.waited[q][k] = prev
            waits.append((s, prev))
        self.dcnt[q][j] = prev + 16
        ev = (s, prev + 16)
        self.prog[q].append((waits, fn, ev, 16))
        self._finish(ev, reads, writes)
        self.nins += 1 + len(waits)
        return ev

    def wait_events(self, e, evs):
        waits = []
        for s, v in evs:
            k = id(s)
            if self.waited[e].get(k, 0) < v:
                self.waited[e][k] = v
                waits.append((s, v))
        if waits:
            self.prog[e].append((waits, None, None, 0))
            self.nins += len(waits)

    def all_events(self):
        evs = [(self.sem[k], self.cnt[k]) for k in self.sem if self.cnt[k] > 0]
        for q in self.dsem:
            for s, c in zip(self.dsem[q], self.dcnt[q]):
                if c > 0:
                    evs.append((s, c))
        return evs

    def barrier(self):
        evs = self.all_events()
        for e in self.ENG:
            self.wait_events(e, evs)
        self.toks = {}

    def emit(self):
        nc = self.nc
        prog = self.prog

        def run(eng, lst):
            for waits, fn, ev, inc in lst:
                for s, v in waits:
                    eng.wait_ge(s, v)
                if fn is not None:
                    fn(eng).then_inc(ev[0], inc)

        with nc.Block() as block:
            @block.tensor
            def _(eng):
                run(eng, prog["pe"])

            @block.scalar
            def _(eng):
                run(eng, prog["act"])

            @block.vector
            def _(eng):
                run(eng, prog["dve"])

            @block.gpsimd
            def _(eng):
                run(eng, prog["pool"])

            @block.sync
            def _(eng):
                run(eng, prog["sp"])


class Arena:
    def __init__(self, handle, nbytes):
        self.h32 = handle
        self.h16 = handle.bitcast(BF16)
        self.nbytes = nbytes
        self.off = 0
        self.uid = 0

    def reset(self):
        self.off = 0

    def alloc(self, dt, *shape, parts=128):
        n = 1
        for s in shape:
            n *= s
        esz = 4 if dt == F32 else 2
        size = (n * esz + 31) // 32 * 32
        assert self.off + size <= self.nbytes, (self.off, size, self.nbytes)
        o = self.off
        self.off += size
        h = self.h32 if dt == F32 else self.h16
        ap = h[0:parts, o // esz:o // esz + n]
        if len(shape) == 2:
            ap = ap.rearrange("p (a b) -> p a b", a=shape[0])
        elif len(shape) == 3:
            ap = ap.rearrange("p (a b c) -> p a b c", a=shape[0], b=shape[1])
        self.uid += 1
        return ap, f"ar{self.uid}"


def build_program(n_layers=4, stages=None):
    nc = bass.Bass("TRN2", target_bir_lowering=False)

    def din(name, shape, dt=F32):
        return nc.dram_tensor(name, list(shape), dt, kind="ExternalInput").ap()

    def dout(name, shape, dt=F32):
        return nc.dram_tensor(name, list(shape), dt, kind="ExternalOutput").ap()

    def dscr(name, shape, dt):
        return nc.dram_tensor(name, list(shape), dt, kind="Internal").ap()

    xin = din("xin", [NT, D])
    cvec = din("cvec", [128, 2, 16])
    NL = n_layers
    ck = [din(f"ck{l}", [8, PAST, 128]) for l in range(NL)]
    cv = [din(f"cv{l}", [8, PAST, 128]) for l in range(NL)]
    wmod = [din(f"wmod{l}", [D, 3 * D]) for l in range(NL)]
    bmodT = [din(f"bmodT{l}", [128, 32]) for l in range(NL)]
    bmodG = [din(f"bmodG{l}", [128, D]) for l in range(NL)]
    win = [din(f"win{l}", [D, 6144 if l % 2 == 0 else 7168]) for l in range(NL)]
    wout = [din(f"wout{l}", [D, D]) for l in range(NL)]
    lng = [din(f"lng{l}", [128, D]) for l in range(NL)]
    lnb = [din(f"lnb{l}", [128, D]) for l in range(NL)]
    poolw = {l: din(f"poolw{l}", [4, 256, 256]) for l in (0, 2) if l < NL}
    pscale = {l: din(f"pscale{l}", [128, 8]) for l in (0, 2) if l < NL}
    dlam = {l: din(f"dlam{l}", [128, 256]) for l in (0, 2) if l < NL}
    subln = {l: din(f"subln{l}", [128, 1]) for l in (0, 2) if l < NL}
    rpbT = {l: din(f"rpbT{l}", [64, 8, 15, 64]) for l in (1, 3) if l < NL}
    sguln = {l: din(f"sguln{l}", [128, 1024]) for l in (1, 3) if l < NL}
    sguw = {l: din(f"sguw{l}", [4, 128, 128]) for l in (1, 3) if l < NL}
    sgub = {l: din(f"sgub{l}", [128, 4, 512]) for l in (1, 3) if l < NL}
    ident_d = din("ident", [128, 128])
    permT_d = din("permT", [128, 128])
    ropec_d = din("ropec", [128, NS])
    ropes_d = din("ropes", [128, NS])
    pedge_d = din("pedge", [128, 4, 16])

    yout = dout("yout", [NT, D])
    kout = [dout(f"kout{l}", [NPS * 8 * LP, 128]) for l in range(NL)]
    vout = [dout(f"vout{l}", [NPS * 8 * LP, 128]) for l in range(NL)]

    X = dscr("X", [NT, D], F32)
    QT = dscr("QT", [1024, NT], BF16)
    KT = dscr("KT", [1024, NT], BF16)
    AINT = dscr("AINT", [1024, NT], BF16)
    AGT = dscr("AGT", [1024, NT], BF16)
    BGT = dscr("BGT", [1024, NT], BF16)
    V = dscr("V", [NT, 1024], BF16)
    VS = dscr("VS", [NT, 1024], F32)
    YT = dscr("YT", [D, NT], BF16)

    with ExitStack() as es:
        S = Sched(nc, es)

        def sbt(name, shape, dt):
            return es.enter_context(nc.sbuf_tensor(name, list(shape), dt))

        ident = sbt("ident_s", [128, 128], F32)
        ones_bf = sbt("ones_bf", [128, 128], BF16)
        zeros_bf = sbt("zeros_bf", [128, 128], BF16)
        perm_bf = sbt("perm_bf", [128, 128], BF16)
        lhsc = sbt("lhsc", [128, 2, 16, 128], BF16)
        silc = sbt("silc", [128, 16, 2], BF16)
        sil = sbt("sil", [128, 2, 16], F32)
        mods = sbt("mods", [128, 2, 2, 16], F32)
        G = sbt("G", [128, 2, D], F32)
        lngb = sbt("lngb", [128, 2, D], F32)
        small = sbt("small", [128, 64], F32)
        ARENA_BYTES = 157 * 1024
        arena_h = sbt("arena", [128, ARENA_BYTES // 4], F32)
        A = Arena(arena_h, ARENA_BYTES)
        PSall = es.enter_context(nc.psum_tensor("psall", [128, 4096], F32))
        PS = [PSall[:, i * 512:(i + 1) * 512] for i in range(8)]
        PSN = [f"ps{i}" for i in range(8)]

        eps_col = small[:, 0:1]
        S.dma("sp", lambda e: e.dma_start(out=ident[:], in_=ident_d), writes=["ident"])
        S.dma("pool", lambda e: e.dma_start(out=perm_bf[:], in_=permT_d), writes=["perm"])
        S.dma("sp", lambda e: e.dma_start(out=sil[:], in_=cvec), writes=["sil"])
        S.op("dve", lambda e: e.memset(ones_bf[:], 1.0), writes=["ones"])
        S.op("dve", lambda e: e.memset(zeros_bf[:], 0.0), writes=["zeros"])
        S.op("dve", lambda e: e.memset(eps_col, LN_EPS), writes=["eps"])
        S.op("act", lambda e: e.activation(out=sil[:], in_=sil[:], func=AF.Silu), reads=["sil"], writes=["sil"])
        for cnd in range(2):
            S.op("dve", lambda e, cnd=cnd: e.tensor_copy(out=silc[:, :, cnd], in_=sil[:, cnd, :]), reads=["sil"], writes=["silc"])
            for kc in range(16):
                S.op("dve", lambda e, cnd=cnd, kc=kc: e.tensor_scalar(out=lhsc[:, cnd, kc, :], in0=zeros_bf[:], scalar1=sil[:, cnd, kc:kc + 1],
                                                                      scalar2=None, op0=ALU.add),
                     reads=["sil", "zeros"], writes=["lhsc"])
        S.barrier()

        def wview(w, g):
            return w.rearrange("(c p) n -> p c n", p=128)[:, :, g * 512:(g + 1) * 512]

        def stage_mod(l):
            A.reset()
            wb = [A.alloc(BF16, 16, 512) for _ in range(2)]
            bmT, bmTn = A.alloc(F32, 32)
            bmG, bmGn = A.alloc(F32, D)
            S.dma("sp", lambda e: e.dma_start(out=bmT, in_=bmodT[l]), writes=[bmTn])
            S.dma("sp", lambda e: e.dma_start(out=bmG, in_=bmodG[l]), writes=[bmGn])
            S.dma("sp", lambda e: e.dma_start(out=lngb[:, 0, :], in_=lng[l]), writes=["lngb"])
            S.dma("sp", lambda e: e.dma_start(out=lngb[:, 1, :], in_=lnb[l]), writes=["lngb"])
            psM = PS[7]
            for gi in range(12):
                w_ap, w_n = wb[gi % 2]
                S.dma("pool", lambda e, w_ap=w_ap, gi=gi: e.dma_start(out=w_ap, in_=wview(wmod[l], gi)), writes=[w_n])
                if gi < 8:
                    for j in range(4):
                        cc = gi * 4 + j
                        for kc in range(16):
                            S.op("pe", lambda e, w_ap=w_ap, j=j, kc=kc, cc=cc: e.matmul(
                                out=psM[:, cc * 2:cc * 2 + 2], lhsT=w_ap[:, kc, j * 128:(j + 1) * 128], rhs=silc[:, kc, :],
                                start=(kc == 0), stop=(kc == 15)), reads=[w_n, "silc"], writes=[PSN[7]])
                else:
                    for cnd in range(2):
                        b = (gi * 2 + cnd) % 4
                        for kc in range(16):
                            S.op("pe", lambda e, w_ap=w_ap, cnd=cnd, kc=kc, b=b: e.matmul(
                                out=PS[b][:], lhsT=lhsc[:, cnd, kc, :], rhs=w_ap[:, kc, :], start=(kc == 0), stop=(kc == 15)),
                                reads=[w_n, "lhsc"], writes=[PSN[b]])
                        c0 = (gi - 8) * 512
                        S.op("dve", lambda e, cnd=cnd, b=b, c0=c0: e.tensor_tensor(out=G[:, cnd, c0:c0 + 512], in0=PS[b][:], in1=bmG[:, c0:c0 + 512], op=ALU.add),
                             reads=[PSN[b], bmGn], writes=["G"])
            pv = psM[:, 0:64].rearrange("p (c t) -> p c t", t=2)
            for cnd in range(2):
                S.op("dve", lambda e, cnd=cnd: e.tensor_tensor(out=mods[:, cnd, :, :].rearrange("p a b -> p (a b)"), in0=pv[:, :, cnd], in1=bmT, op=ALU.add),
                     reads=[PSN[7], bmTn], writes=["mods"])
                S.op("dve", lambda e, cnd=cnd: e.tensor_scalar(out=mods[:, cnd, 1, :], in0=mods[:, cnd, 1, :], scalar1=1.0, scalar2=None, op0=ALU.add),
                     reads=["mods"], writes=["mods"])
            S.barrier()

        PDBG = set(os.environ.get('PDBG', 'fm,rope,tm,kv').split(','))

        def stage_P(l):
            even = l % 2 == 0
            ngrp = 12 if even else 14
            src = xin if l == 0 else X
            A.reset()
            hT = [A.alloc(BF16, 16, 1024) for _ in range(2)]
            wb = [A.alloc(BF16, 16, 512) for _ in range(2)]
            xt = [A.alloc(F32, D) for _ in range(2)]
            obf = [A.alloc(BF16, 512) for _ in range(4)]
            of32 = [A.alloc(F32, 512) for _ in range(2)]
            qb = [A.alloc(BF16, 512) for _ in range(2)]
            t1 = [A.alloc(F32, 512) for _ in range(2)]
            t2 = [A.alloc(F32, 512) for _ in range(2)]
            cs = [A.alloc(F32, 2, 512) for _ in range(4)]
            cnt = {"obf": 0, "of32": 0, "ps": 0, "rope": 0, "w": 0, "x": 0, "tp": 0}
            if even:
                kinds = ["ain", "ain", "gA", "gA", "q", "q", "k", "k", "v", "v", "gB", "gB"]
            else:
                kinds = ["q", "q", "k", "k", "v", "v", "gB", "gB", "ain", "ain", "vs", "vs", "gA", "gA"]
            dstT = {"ain": AINT, "gA": AGT, "gB": BGT, "q": QT, "k": KT}
            def tr_tile(tsb, ti):
                tok0 = tsb * 1024
                cnd = 0 if tsb < 4 else 1
                h_ap, h_n = hT[tsb % 2]
                x_ap, x_n = xt[cnt["x"] % 2]
                cnt["x"] += 1
                r0 = tok0 + ti * 128
                S.dma("sp", lambda e: e.dma_start(out=x_ap, in_=src[r0:r0 + 128, :]), writes=[x_n])
                for q4 in range(4):
                    b = 4 + cnt["tp"] % 2
                    cnt["tp"] += 1
                    for j in range(4):
                        kc = q4 * 4 + j
                        S.op("pe", lambda e, kc=kc, j=j, b=b: e.transpose(out=PS[b][:, j * 128:(j + 1) * 128], in_=x_ap[:, kc * 128:(kc + 1) * 128], identity=ident[:]),
                             reads=[x_n, "ident"], writes=[PSN[b]])
                    for j in range(4):
                        kc = q4 * 4 + j
                        S.op("act", lambda e, kc=kc, j=j, b=b: e.activation(
                            out=h_ap[:, kc, ti * 128:(ti + 1) * 128], in_=PS[b][:, j * 128:(j + 1) * 128], func=AF.Identity,
                            scale=mods[:, cnd, 1, kc:kc + 1], bias=mods[:, cnd, 0, kc:kc + 1]),
                            reads=[PSN[b], "mods"], writes=[h_n])

            for ti in range(8):
                tr_tile(0, ti)
            for tsb in range(5):
                tok0 = tsb * 1024
                cnd = 0 if tsb < 4 else 1
                h_ap, h_n = hT[tsb % 2]
                if even and tsb < 4:
                    for hf_ in range(2):
                        cs_ap_, cs_n_ = cs[(tsb % 2) * 2 + hf_]
                        tt_ = tok0 + hf_ * 512
                        S.dma("sp", lambda e, cs_ap_=cs_ap_, tt_=tt_: e.dma_start(out=cs_ap_[:, 0, :], in_=ropec_d[:, tt_:tt_ + 512]), writes=[cs_n_])
                        S.dma("sp", lambda e, cs_ap_=cs_ap_, tt_=tt_: e.dma_start(out=cs_ap_[:, 1, :], in_=ropes_d[:, tt_:tt_ + 512]), writes=[cs_n_])
                for g in range(ngrp):
                    kind = kinds[g]
                    gsub = g % 2
                    if kind == "k" and False:
                        pass
                    w_ap, w_n = wb[cnt["w"] % 2]
                    cnt["w"] += 1
                    S.dma("pool", lambda e, w_ap=w_ap, g=g: e.dma_start(out=w_ap, in_=wview(win[l], g)), writes=[w_n])
                    if kind in ("ain", "gA", "gB", "q", "k") and "fm" in PDBG:
                        for j in range(4):
                            row0 = (gsub * 4 + j) * 128
                            for hf in range(2):
                                b = cnt["ps"] % 4
                                cnt["ps"] += 1
                                t0 = tok0 + hf * 512
                                for kc in range(16):
                                    S.op("pe", lambda e, w_ap=w_ap, h_ap=h_ap, j=j, kc=kc, hf=hf, b=b: e.matmul(
                                        out=PS[b][:], lhsT=w_ap[:, kc, j * 128:(j + 1) * 128], rhs=h_ap[:, kc, hf * 512:(hf + 1) * 512],
                                        start=(kc == 0), stop=(kc == 15)), reads=[w_n, h_n], writes=[PSN[b]])
                                o_ap, o_n = obf[cnt["obf"] % 4]
                                cnt["obf"] += 1
                                if kind in ("gA", "gB"):
                                    S.op("act", lambda e, o_ap=o_ap, b=b: e.activation(out=o_ap, in_=PS[b][:], func=AF.Silu), reads=[PSN[b]], writes=[o_n])
                                elif kind in ("q", "k") and even and tsb < 4 and "rope" in PDBG:
                                    ri = cnt["rope"] % 2
                                    cnt["rope"] += 1
                                    qb_ap, qb_n = qb[ri]
                                    t1_ap, t1_n = t1[ri]
                                    t2_ap, t2_n = t2[ri]
                                    cs_ap, cs_n = cs[(tsb % 2) * 2 + hf]
                                    S.op("act", lambda e, qb_ap=qb_ap, b=b: e.activation(out=qb_ap, in_=PS[b][:], func=AF.Copy), reads=[PSN[b]], writes=[qb_n])
                                    b2 = 6 + ri
                                    S.op("pe", lambda e, qb_ap=qb_ap, b2=b2: e.matmul(out=PS[b2][:], lhsT=perm_bf[:], rhs=qb_ap, start=True, stop=True),
                                         reads=[qb_n, "perm"], writes=[PSN[b2]])
                                    S.op("dve", lambda e, t1_ap=t1_ap, cs_ap=cs_ap, qb_ap=qb_ap: e.tensor_tensor(out=t1_ap, in0=qb_ap, in1=cs_ap[:, 0, :], op=ALU.mult),
                                         reads=[qb_n, cs_n], writes=[t1_n])
                                    S.op("dve", lambda e, t2_ap=t2_ap, cs_ap=cs_ap, b2=b2: e.tensor_tensor(out=t2_ap, in0=PS[b2][:], in1=cs_ap[:, 1, :], op=ALU.mult),
                                         reads=[PSN[b2], cs_n], writes=[t2_n])
                                    S.op("dve", lambda e, o_ap=o_ap, t1_ap=t1_ap, t2_ap=t2_ap: e.tensor_tensor(out=o_ap, in0=t1_ap, in1=t2_ap, op=ALU.add),
                                         reads=[t1_n, t2_n], writes=[o_n])
                                else:
                                    S.op("act", lambda e, o_ap=o_ap, b=b: e.activation(out=o_ap, in_=PS[b][:], func=AF.Copy), reads=[PSN[b]], writes=[o_n])
                                dst = dstT[kind]
                                S.dma("sp", lambda e, o_ap=o_ap, dst=dst, row0=row0, t0=t0: e.dma_start(out=dst[row0:row0 + 128, t0:t0 + 512], in_=o_ap), reads=[o_n])
                    if (kind in ("v", "vs") or (kind == "k" and tsb == 4)) and "tm" in PDBG:
                        for ti in range(8):
                            b = cnt["ps"] % 4
                            cnt["ps"] += 1
                            r0 = tok0 + ti * 128
                            c0 = gsub * 512
                            for kc in range(16):
                                S.op("pe", lambda e, w_ap=w_ap, h_ap=h_ap, kc=kc, ti=ti, b=b: e.matmul(
                                    out=PS[b][:], lhsT=h_ap[:, kc, ti * 128:(ti + 1) * 128], rhs=w_ap[:, kc, :], start=(kc == 0), stop=(kc == 15)),
                                    reads=[w_n, h_n], writes=[PSN[b]])
                            if kind == "v":
                                o_ap, o_n = obf[cnt["obf"] % 4]
                                cnt["obf"] += 1
                                S.op("act", lambda e, o_ap=o_ap, b=b: e.activation(out=o_ap, in_=PS[b][:], func=AF.Copy), reads=[PSN[b]], writes=[o_n])
                                S.dma("sp", lambda e, o_ap=o_ap, r0=r0, c0=c0: e.dma_start(out=V[r0:r0 + 128, c0:c0 + 512], in_=o_ap), reads=[o_n])
                            if (kind == "vs" or tsb == 4) and "kv" in PDBG:
                                f_ap, f_n = of32[cnt["of32"] % 2]
                                cnt["of32"] += 1
                                S.op("act", lambda e, f_ap=f_ap, b=b: e.activation(out=f_ap, in_=PS[b][:], func=AF.Copy), reads=[PSN[b]], writes=[f_n])
                                if kind == "vs":
                                    S.dma("sp", lambda e, f_ap=f_ap, r0=r0, c0=c0: e.dma_start(out=VS[r0:r0 + 128, c0:c0 + 512], in_=f_ap), reads=[f_n])
                                else:
                                    dsto = kout[l] if kind == "k" else vout[l]
                                    sq = ti // 2
                                    tt0 = (ti % 2) * 128
                                    for hh in range(4):
                                        S.dma("sp", lambda e, f_ap=f_ap, dsto=dsto, sq=sq, tt0=tt0, gsub=gsub, hh=hh: e.dma_start(
                                            out=(VS[((sq * 8 + gsub * 4 + hh) * LP + tt0) // 8:((sq * 8 + gsub * 4 + hh) * LP + tt0) // 8 + 128, 0:128] if os.environ.get("KVDBG") else dsto[(sq * 8 + gsub * 4 + hh) * LP + tt0:(sq * 8 + gsub * 4 + hh) * LP + tt0 + 128, :]), in_=f_ap[:, hh * 128:(hh + 1) * 128]), reads=[f_n])
                if tsb + 1 < 5:
                    for ti_ in range(8):
                        tr_tile(tsb + 1, ti_)
            S.barrier()

        def attention(keys, q_ap, q_n, nq, two, scale, bank_set, reads_extra, E_bufs, ecnt):
            nkc = len(keys)
            wide = two and nq == 512

            def emit_S(i):
                kT_ap, v_ap, nk = keys[i]
                sb_ = bank_set[i % len(bank_set)]
                if two:
                    S.op("pe", lambda e: e.matmul(out=PS[sb_[0]][0:nk, 0:nq], lhsT=kT_ap, rhs=q_ap[0], start=True, stop=True),
                         reads=reads_extra + [q_n], writes=[PSN[sb_[0]]])
                    S.op("pe", lambda e: e.matmul(out=PS[sb_[1]][0:nk, 0:nq], lhsT=kT_ap, rhs=q_ap[1], start=True, stop=True),
                         reads=reads_extra + [q_n], writes=[PSN[sb_[1]]])
                else:
                    S.op("pe", lambda e: e.matmul(out=PS[sb_[0]][0:nk, 0:nq], lhsT=kT_ap, rhs=q_ap, start=True, stop=True),
                         reads=reads_extra + [q_n], writes=[PSN[sb_[0]]])

            def emit_pv(i, s_i, rhs_ap, e_n):
                kT_ap, v_ap, nk = keys[i]
                ob, zb = 4 + 2 * s_i, 5 + 2 * s_i
                S.op("pe", lambda e: e.matmul(out=PS[ob][:, 0:nq], lhsT=v_ap, rhs=rhs_ap, start=(i == 0), stop=(i == nkc - 1)),
                     reads=reads_extra + [e_n], writes=[PSN[ob]])
                S.op("pe", lambda e: e.matmul(out=PS[zb][:, 0:nq], lhsT=ones_bf[0:nk, :], rhs=rhs_ap, start=(i == 0), stop=(i == nkc - 1)),
                     reads=[e_n, "ones"], writes=[PSN[zb]])

            def emit_exp(i):
                kT_ap, v_ap, nk = keys[i]
                sb_ = bank_set[i % len(bank_set)]
                pend = []
                if wide:
                    e_ap, e_n = E_bufs[ecnt[0] % len(E_bufs)]
                    ecnt[0] += 1
                    base = sb_[0] * 512
                    S.op("act", lambda e: e.activation(out=e_ap[0:nk, 0:1024], in_=PSall[0:nk, base:base + 1024], func=AF.Exp, scale=scale),
                         reads=[PSN[sb_[0]], PSN[sb_[1]]], writes=[e_n])
                    for s_i in range(2):
                        pend.append((s_i, e_ap[0:nk, s_i * 512:(s_i + 1) * 512], e_n))
                else:
                    for s_i in range(2 if two else 1):
                        e_ap, e_n = E_bufs[ecnt[0] % len(E_bufs)]
                        ecnt[0] += 1
                        S.op("act", lambda e, e_ap=e_ap, s_i=s_i: e.activation(out=e_ap[0:nk, 0:nq], in_=PS[sb_[s_i]][0:nk, 0:nq], func=AF.Exp, scale=scale),
                             reads=[PSN[sb_[s_i]]], writes=[e_n])
                        pend.append((s_i, e_ap[0:nk, 0:nq], e_n))
                return pend

            nset = len(bank_set)
            for i0 in range(min(nset, nkc)):
                emit_S(i0)
            for i in range(nkc):
                pend = emit_exp(i)
                if i + nset < nkc:
                    emit_S(i + nset)
                for (s_i, rhs_ap, e_n) in pend:
                    emit_pv(i, s_i, rhs_ap, e_n)

        def stage_pool(l):
            A.reset()
            pe_t, pe_n = A.alloc(F32, 4, 16)
            psc, psc_n = A.alloc(F32, 8)
            S.dma("sp", lambda e: e.dma_start(out=pe_t, in_=pedge_d), writes=[pe_n])
            S.dma("sp", lambda e: e.dma_start(out=psc, in_=pscale[l]), writes=[psc_n])
            wp = [A.alloc(BF16, 2, 256) for _ in range(4)]
            for g in range(4):
                S.dma("pool", lambda e, g=g: e.dma_start(out=wp[g][0], in_=poolw[l][g].rearrange("(c p) d -> p c d", p=128)), writes=[wp[g][1]])
            NB = NS + 16
            xb = [A.alloc(BF16, NB) for _ in range(2)]
            pa, pa_n = A.alloc(F32, NB)
            pb, pb_n = A.alloc(F32, NB)
            pooled = [A.alloc(BF16, 2, NS) for _ in range(2)]
            ag = [A.alloc(BF16, NS) for _ in range(2)]
            yo = [A.alloc(BF16, 512) for _ in range(3)]
            c = {"x": 0, "ps": 0, "yo": 0, "ag": 0, "pl": 0}
            seqs = [(0, NS)] + [(NS + s * LP, LP) for s in range(NPS)]
            for (t0, L) in seqs:
                N = L + 16
                for g in range(4):
                    w = POOL_WINDOWS[g]
                    half = w // 2
                    pl_ap, pl_n = pooled[c["pl"] % 2]
                    c["pl"] += 1
                    for j in range(2):
                        ch = 2 * g + j
                        x_ap, x_n = xb[c["x"] % 2]
                        c["x"] += 1
                        S.op("pool", lambda e, x_ap=x_ap: e.memset(x_ap[:, 0:8], 0.0), writes=[x_n])
                        S.op("pool", lambda e, x_ap=x_ap, L=L: e.memset(x_ap[:, 8 + L:16 + L], 0.0), writes=[x_n])
                        S.dma("sp", lambda e, x_ap=x_ap, ch=ch, t0=t0, L=L: e.dma_start(out=x_ap[:, 8:8 + L], in_=AINT[ch * 128:(ch + 1) * 128, t0:t0 + L]), writes=[x_n])
                        cur, cur_n, width, n = x_ap, x_n, 1, N
                        bufs = [(pa, pa_n), (pb, pb_n)]
                        bi = 0
                        while width < w:
                            o_ap, o_n = bufs[bi]
                            bi ^= 1
                            n2 = n - width
                            S.op("dve", lambda e, o_ap=o_ap, cur=cur, n2=n2, width=width: e.tensor_tensor(out=o_ap[:, 0:n2], in0=cur[:, 0:n2], in1=cur[:, width:width + n2], op=ALU.add),
                                 reads=[cur_n], writes=[o_n])
                            cur, cur_n, n, width = o_ap, o_n, n2, width * 2
                        s0 = 8 - half
                        S.op("dve", lambda e, pl_ap=pl_ap, j=j, cur=cur, s0=s0, L=L, w=w, x_ap=x_ap: e.scalar_tensor_tensor(
                            out=pl_ap[:, j, 0:L], in0=cur[:, s0:s0 + L], scalar=1.0 / w, in1=x_ap[:, 8:8 + L], op0=ALU.mult, op1=ALU.subtract),
                            reads=[cur_n, x_n], writes=[pl_n])
                        o_ap, o_n = bufs[bi]
                        for (a0, e0) in ((0, 0), (L - 8, 8)):
                            S.op("dve", lambda e, o_ap=o_ap, cur=cur, s0=s0, a0=a0, e0=e0, g=g: e.tensor_tensor(
                                out=o_ap[:, a0:a0 + 8], in0=cur[:, s0 + a0:s0 + a0 + 8], in1=pe_t[:, g, e0:e0 + 8], op=ALU.mult),
                                reads=[cur_n, pe_n], writes=[o_n])
                            S.op("dve", lambda e, o_ap=o_ap, pl_ap=pl_ap, j=j, a0=a0, x_ap=x_ap: e.tensor_tensor(
                                out=pl_ap[:, j, a0:a0 + 8], in0=o_ap[:, a0:a0 + 8], in1=x_ap[:, 8 + a0:16 + a0], op=ALU.subtract),
                                reads=[o_n, x_n], writes=[pl_n])
                    for dch in range(2):
                        ch = 2 * g + dch
                        ag_ap, ag_n = ag[c["ag"] % 2]
                        c["ag"] += 1
                        S.dma("sp", lambda e, ag_ap=ag_ap, ch=ch, t0=t0, L=L: e.dma_start(out=ag_ap[:, 0:L], in_=AGT[ch * 128:(ch + 1) * 128, t0:t0 + L]), writes=[ag_n])
                        nb = max(1, L // 512)
                        bw = min(L, 512)
                        for tb in range(nb):
                            b = c["ps"] % 4
                            c["ps"] += 1
                            for cc in range(2):
                                S.op("pe", lambda e, g=g, cc=cc, dch=dch, pl_ap=pl_ap, tb=tb, bw=bw, b=b: e.matmul(
                                    out=PS[b][:, 0:bw], lhsT=wp[g][0][:, cc, dch * 128:(dch + 1) * 128], rhs=pl_ap[:, cc, tb * 512:tb * 512 + bw],
                                    start=(cc == 0), stop=(cc == 1)), reads=[wp[g][1], pl_n], writes=[PSN[b]])
                            y_ap, y_n = yo[c["yo"] % 3]
                            c["yo"] += 1
                            S.op("dve", lambda e, y_ap=y_ap, b=b, bw=bw, ch=ch, ag_ap=ag_ap, tb=tb: e.scalar_tensor_tensor(
                                out=y_ap[:, 0:bw], in0=PS[b][:, 0:bw], scalar=psc[:, ch:ch + 1], in1=ag_ap[:, tb * 512:tb * 512 + bw], op0=ALU.mult, op1=ALU.mult),
                                reads=[PSN[b], psc_n, ag_n], writes=[y_n])
                            S.dma("sp", lambda e, y_ap=y_ap, ch=ch, t0=t0, tb=tb, bw=bw: e.dma_start(out=YT[ch * 128:(ch + 1) * 128, t0 + tb * 512:t0 + tb * 512 + bw], in_=y_ap[:, 0:bw]),
                                  reads=[y_n])
            S.barrier()

        def stage_diff(l):
            A.reset()
            lam_init = 0.8 - 0.6 * math.exp(-0.3 * l)
            dl, dl_n = A.alloc(F32, 256)
            sub, sub_n = A.alloc(F32, 1)
            lw, lw_n = A.alloc(F32, 136)
            S.dma("sp", lambda e: e.dma_start(out=dl, in_=dlam[l]), writes=[dl_n])
            S.dma("sp", lambda e: e.dma_start(out=sub, in_=subln[l]), writes=[sub_n])
            S.op("dve", lambda e: e.tensor_tensor(out=lw[:, 0:64], in0=dl[:, 0:64], in1=dl[:, 64:128], op=ALU.mult), reads=[dl_n], writes=[lw_n])
            S.op("dve", lambda e: e.tensor_tensor(out=lw[:, 64:128], in0=dl[:, 128:192], in1=dl[:, 192:256], op=ALU.mult), reads=[dl_n], writes=[lw_n])
            S.op("dve", lambda e: e.reduce_sum(out=lw[:, 128:129], in_=lw[:, 0:64], axis=mybir.AxisListType.X), reads=[lw_n], writes=[lw_n])
            S.op("dve", lambda e: e.reduce_sum(out=lw[:, 129:130], in_=lw[:, 64:128], axis=mybir.AxisListType.X), reads=[lw_n], writes=[lw_n])
            S.op("act", lambda e: e.activation(out=lw[:, 130:132], in_=lw[:, 128:130], func=AF.Exp), reads=[lw_n], writes=[lw_n])
            S.op("dve", lambda e: e.tensor_tensor(out=lw[:, 132:133], in0=lw[:, 131:132], in1=lw[:, 130:131], op=ALU.subtract), reads=[lw_n], writes=[lw_n])
            S.op("dve", lambda e: e.tensor_scalar(out=lw[:, 132:133], in0=lw[:, 132:133], scalar1=-lam_init, scalar2=None, op0=ALU.add), reads=[lw_n], writes=[lw_n])
            S.op("dve", lambda e: e.tensor_scalar(out=lw[:, 133:134], in0=sub, scalar1=(1.0 - lam_init), scalar2=None, op0=ALU.mult), reads=[lw_n, sub_n], writes=[lw_n])
            nlam = lw[:, 132:133]
            c1 = lw[:, 133:134]

            NK = NS + PAST
            kt = [A.alloc(BF16, NK) for _ in range(2)]
            vt = [A.alloc(BF16, NK // 128, 128) for _ in range(2)]
            ckf, ckf_n = A.alloc(F32, 4, 128)
            qt = [A.alloc(BF16, 2, 512) for _ in range(2)]
            for q_ap_, q_n_ in qt:
                S.op("dve", lambda e, q_ap_=q_ap_: e.memset(q_ap_.rearrange("p a b -> p (a b)"), 0.0), writes=[q_n_])
            bg = [A.alloc(BF16, 512) for _ in range(2)]
            E = [A.alloc(BF16, 1024) for _ in range(3)]
            tmp = [A.alloc(F32, 512) for _ in range(6)]
            sqb = [A.alloc(BF16, 512) for _ in range(2)]
            yo = [A.alloc(BF16, 512) for _ in range(2)]
            ecnt = [0]
            c = {"q": 0, "y": 0}

            def epilogue(nq, h, tq0, bg_ap, bg_n):
                (r1, r1n), (r2, r2n), (o1, o1n), (o2, o2n), (oo, oon), (rs, rsn) = tmp
                S.op("act", lambda e: e.activation(out=r1[:, 0:nq], in_=PS[5][:, 0:nq], func=AF.Copy), reads=[PSN[5]], writes=[r1n])
                S.op("act", lambda e: e.activation(out=r2[:, 0:nq], in_=PS[7][:, 0:nq], func=AF.Copy), reads=[PSN[7]], writes=[r2n])
                S.op("dve", lambda e: e.reciprocal(out=r1[:, 0:nq], in_=r1[:, 0:nq]), reads=[r1n], writes=[r1n])
                S.op("dve", lambda e: e.reciprocal(out=r2[:, 0:nq], in_=r2[:, 0:nq]), reads=[r2n], writes=[r2n])
                S.op("dve", lambda e: e.tensor_tensor(out=o1[:, 0:nq], in0=PS[4][:, 0:nq], in1=r1[:, 0:nq], op=ALU.mult), reads=[PSN[4], r1n], writes=[o1n])
                S.op("dve", lambda e: e.tensor_tensor(out=o2[:, 0:nq], in0=PS[6][:, 0:nq], in1=r2[:, 0:nq], op=ALU.mult), reads=[PSN[6], r2n], writes=[o2n])
                S.op("dve", lambda e: e.scalar_tensor_tensor(out=oo[:, 0:nq], in0=o2[:, 0:nq], scalar=nlam, in1=o1[:, 0:nq], op0=ALU.mult, op1=ALU.add),
                     reads=[o1n, o2n, lw_n], writes=[oon])
                sq_ap, sq_n = sqb[c["y"] % 2]
                S.op("act", lambda e: e.activation(out=sq_ap[:, 0:nq], in_=oo[:, 0:nq], func=AF.Square), reads=[oon], writes=[sq_n])
                S.op("pe", lambda e: e.matmul(out=PS[0][:, 0:nq], lhsT=ones_bf[:], rhs=sq_ap[:, 0:nq], start=True, stop=True), reads=[sq_n, "ones"], writes=[PSN[0]])
                S.op("act", lambda e: e.activation(out=rs[:, 0:nq], in_=PS[0][:, 0:nq], func=AF.Sqrt, bias=eps_col, scale=1.0 / 128.0), reads=[PSN[0], "eps"], writes=[rsn])
                S.op("dve", lambda e: e.reciprocal(out=rs[:, 0:nq], in_=rs[:, 0:nq]), reads=[rsn], writes=[rsn])
                S.op("dve", lambda e: e.tensor_tensor(out=oo[:, 0:nq], in0=oo[:, 0:nq], in1=rs[:, 0:nq], op=ALU.mult), reads=[oon, rsn], writes=[oon])
                y_ap, y_n = yo[c["y"] % 2]
                c["y"] += 1
                S.op("dve", lambda e: e.scalar_tensor_tensor(out=y_ap[:, 0:nq], in0=oo[:, 0:nq], scalar=c1, in1=bg_ap[:, 0:nq], op0=ALU.mult, op1=ALU.mult),
                     reads=[oon, lw_n, bg_n], writes=[y_n])
                S.dma("sp", lambda e: e.dma_start(out=YT[1024 + h * 128:1024 + (h + 1) * 128, tq0:tq0 + nq], in_=y_ap[:, 0:nq]), reads=[y_n])

            scale = 64 ** -0.5
            bank_sets = [(0, 1), (2, 3)]
            def load_head(h):
                kt_ap, kt_n = kt[h % 2]
                vt_ap, vt_n = vt[h % 2]
                S.dma("sp", lambda e: e.dma_start(out=kt_ap[:, 0:NS], in_=KT[h * 128:(h + 1) * 128, 0:NS]), writes=[kt_n])
                S.dma("sp", lambda e: e.dma_start(out=vt_ap[:, 0:NS // 128, :], in_=V[0:NS, h * 128:(h + 1) * 128].rearrange("(c p) d -> p c d", p=128)), writes=[vt_n])
                S.dma("pool", lambda e: e.dma_start(out=vt_ap[:, NS // 128:NK // 128, :], in_=cv[l][h].rearrange("(c p) d -> p c d", p=128)), writes=[vt_n])
                S.dma("sp", lambda e: e.dma_start(out=ckf, in_=ck[l][h].rearrange("(c p) d -> p c d", p=128)), writes=[ckf_n])

            def ctx_head(h):
                kt_ap, kt_n = kt[h % 2]
                for j in range(4):
                    S.op("pe", lambda e, j=j: e.transpose(out=PS[2][:, j * 128:(j + 1) * 128], in_=ckf[:, j, :], identity=ident[:]), reads=[ckf_n, "ident"], writes=[PSN[2]])
                S.op("act", lambda e: e.activation(out=kt_ap[:, NS:NK], in_=PS[2][:], func=AF.Copy), reads=[PSN[2]], writes=[kt_n])

            qcur = {}

            def load_q(h, qb_i):
                q_ap, q_n = qt[c["q"] % 2]
                bg_ap, bg_n = bg[c["q"] % 2]
                c["q"] += 1
                tq0 = qb_i * 512
                S.dma("sp", lambda e: e.dma_start(out=q_ap[0:64, 0, :], in_=QT[h * 128:h * 128 + 64, tq0:tq0 + 512]), writes=[q_n])
                S.dma("sp", lambda e: e.dma_start(out=q_ap[64:128, 1, :], in_=QT[h * 128 + 64:(h + 1) * 128, tq0:tq0 + 512]), writes=[q_n])
                S.dma("sp", lambda e: e.dma_start(out=bg_ap, in_=BGT[h * 128:(h + 1) * 128, tq0:tq0 + 512]), writes=[bg_n])
                qcur[(h, qb_i)] = (q_ap, q_n, bg_ap, bg_n)

            blocks = [(h, qb_i) for h in range(8) for qb_i in range(NS // 512)]
            load_head(0)
            ctx_head(0)
            load_q(0, 0)
            for bi, (h, qb_i) in enumerate(blocks):
                kt_ap, kt_n = kt[h % 2]
                vt_ap, vt_n = vt[h % 2]
                if bi + 1 < len(blocks):
                    load_q(*blocks[bi + 1])
                if qb_i == 0 and h + 1 < 8:
                    load_head(h + 1)
                keys = [(kt_ap[:, i * 128:(i + 1) * 128], vt_ap[:, i, :], 128) for i in range(NK // 128)]
                q_ap, q_n, bg_ap, bg_n = qcur[(h, qb_i)]
                tq0 = qb_i * 512
                attention(keys, (q_ap[:, 0, :], q_ap[:, 1, :]), q_n, 512, True, scale, bank_sets, [kt_n, vt_n], E, ecnt)
                epilogue(512, h, tq0, bg_ap, bg_n)
                if qb_i == NS // 512 - 1 and h + 1 < 8:
                    ctx_head(h + 1)
            plist = [(s_, h_) for s_ in range(NPS) for h_ in range(8)]
            pcur = {}

            def load_p(idx):
                s_, h_ = plist[idx]
                t0 = NS + s_ * LP
                kt_ap, kt_n = kt[idx % 2]
                vt_ap, vt_n = vt[idx % 2]
                q_ap, q_n = qt[c["q"] % 2]
                bg_ap, bg_n = bg[c["q"] % 2]
                c["q"] += 1
                S.dma("sp", lambda e: e.dma_start(out=kt_ap[:, 0:LP], in_=KT[h_ * 128:(h_ + 1) * 128, t0:t0 + LP]), writes=[kt_n])
                S.dma("sp", lambda e: e.dma_start(out=vt_ap[:, 0:2, :], in_=V[t0:t0 + LP, h_ * 128:(h_ + 1) * 128].rearrange("(c p) d -> p c d", p=128)), writes=[vt_n])
                S.dma("sp", lambda e: e.dma_start(out=q_ap[0:64, 0, 0:LP], in_=QT[h_ * 128:h_ * 128 + 64, t0:t0 + LP]), writes=[q_n])
                S.dma("sp", lambda e: e.dma_start(out=q_ap[64:128, 1, 0:LP], in_=QT[h_ * 128 + 64:(h_ + 1) * 128, t0:t0 + LP]), writes=[q_n])
                S.dma("sp", lambda e: e.dma_start(out=bg_ap[:, 0:LP], in_=BGT[h_ * 128:(h_ + 1) * 128, t0:t0 + LP]), writes=[bg_n])
                pcur[idx] = (kt_ap, kt_n, vt_ap, vt_n, q_ap, q_n, bg_ap, bg_n)

            load_p(0)
            for idx, (s_, h_) in enumerate(plist):
                if idx + 1 < len(plist):
                    load_p(idx + 1)
                kt_ap, kt_n, vt_ap, vt_n, q_ap, q_n, bg_ap, bg_n = pcur[idx]
                keys = [(kt_ap[:, i * 128:(i + 1) * 128], vt_ap[:, i, :], 128) for i in range(2)]
                attention(keys, (q_ap[:, 0, 0:LP], q_ap[:, 1, 0:LP]), q_n, LP, True, scale, bank_sets, [kt_n, vt_n], E, ecnt)
                epilogue(LP, h_, NS + s_ * LP, bg_ap, bg_n)
            S.barrier()

        def stage_na(l):
            A.reset()
            scale = 128 ** -0.5
            T, T_n = A.alloc(F32, 8, 15, 64, parts=64)
            S.dma("sp", lambda e: e.dma_start(out=T, in_=rpbT[l]), writes=[T_n])
            NK = NS + PAST
            kt, kt_n = A.alloc(BF16, NK)
            vl, vl_n = A.alloc(BF16, 64, 128, parts=64)
            vc, vc_n = A.alloc(BF16, 4, 128)
            ckf, ckf_n = A.alloc(F32, 4, 128)
            qh, qh_n = A.alloc(BF16, NS)
            cg, cg_n = A.alloc(BF16, NS)
            ost, ost_n = A.alloc(F32, NS)
            tm = [A.alloc(F32, 512, parts=64) for _ in range(2)]
            El = [A.alloc(BF16, 8, 64, parts=64) for _ in range(2)]
            Ec = [A.alloc(BF16, 4, 64) for _ in range(2)]
            rz = [A.alloc(F32, 64) for _ in range(2)]
            yo = [A.alloc(BF16, 512) for _ in range(2)]
            E2 = [A.alloc(BF16, 512) for _ in range(4)]
            for h in range(8):
                S.dma("sp", lambda e, h=h: e.dma_start(out=kt[:, 0:NS], in_=KT[h * 128:(h + 1) * 128, 0:NS]), writes=[kt_n])
                S.dma("sp", lambda e, h=h: e.dma_start(out=vl, in_=V[0:NS, h * 128:(h + 1) * 128].rearrange("(r p) d -> p r d", p=64)), writes=[vl_n])
                S.dma("pool", lambda e, h=h: e.dma_start(out=vc, in_=cv[l][h].rearrange("(c p) d -> p c d", p=128)), writes=[vc_n])
                S.dma("sp", lambda e, h=h: e.dma_start(out=ckf, in_=ck[l][h].rearrange("(c p) d -> p c d", p=128)), writes=[ckf_n])
                S.dma("sp", lambda e, h=h: e.dma_start(out=qh, in_=QT[h * 128:(h + 1) * 128, 0:NS]), writes=[qh_n])
                S.dma("sp", lambda e, h=h: e.dma_start(out=cg, in_=BGT[h * 128:(h + 1) * 128, 0:NS]), writes=[cg_n])
                for j in range(4):
                    S.op("pe", lambda e, j=j: e.transpose(out=PS[7][:, j * 128:(j + 1) * 128], in_=ckf[:, j, :], identity=ident[:]), reads=[ckf_n, "ident"], writes=[PSN[7]])
                S.op("act", lambda e: e.activation(out=kt[:, NS:NK], in_=PS[7][:], func=AF.Copy), reads=[PSN[7]], writes=[kt_n])
                def na_front(r):
                    rs_ = min(max(r - 4, 0), 56)
                    dr0 = rs_ - r + 7
                    par = r % 2
                    bS, bC, bO, bZ = (0, 1, 4, 5) if par == 0 else (2, 3, 6, 7)
                    q_r = qh[:, r * 64:(r + 1) * 64]
                    for j in range(8):
                        kr = rs_ + j
                        S.op("pe", lambda e, j=j, kr=kr: e.matmul(out=PS[bS][0:64, j * 64:(j + 1) * 64], lhsT=kt[:, kr * 64:(kr + 1) * 64], rhs=q_r, start=True, stop=True),
                             reads=[kt_n, qh_n], writes=[PSN[bS]])
                    for j in range(4):
                        S.op("pe", lambda e, j=j: e.matmul(out=PS[bC][:, j * 64:(j + 1) * 64], lhsT=kt[:, NS + j * 128:NS + (j + 1) * 128], rhs=q_r, start=True, stop=True),
                             reads=[kt_n, qh_n], writes=[PSN[bC]])
                    tm_ap, tm_n = tm[par]
                    el_ap, el_n = El[par]
                    ec_ap, ec_n = Ec[par]
                    T_ap = T[:, h, dr0:dr0 + 8, :].rearrange("p a b -> p (a b)")
                    S.op("dve", lambda e: e.scalar_tensor_tensor(
                        out=tm_ap, in0=PS[bS][0:64, :], scalar=scale, in1=T_ap, op0=ALU.mult, op1=ALU.add),
                        reads=[PSN[bS], T_n], writes=[tm_n])
                    S.op("act", lambda e: e.activation(out=el_ap.rearrange("p a b -> p (a b)"), in_=tm_ap, func=AF.Exp), reads=[tm_n], writes=[el_n])
                    S.op("act", lambda e: e.activation(out=ec_ap.rearrange("p a b -> p (a b)"), in_=PS[bC][:, 0:256], func=AF.Exp, scale=scale), reads=[PSN[bC]], writes=[ec_n])

                def na_back(r):
                    rs_ = min(max(r - 4, 0), 56)
                    par = r % 2
                    bS, bC, bO, bZ = (0, 1, 4, 5) if par == 0 else (2, 3, 6, 7)
                    el_ap, el_n = El[par]
                    ec_ap, ec_n = Ec[par]
                    for which in range(2):
                        bb = bO if which == 0 else bZ
                        for j in range(8):
                            kr = rs_ + j
                            lh = vl[:, kr, :] if which == 0 else ones_bf[0:64, :]
                            S.op("pe", lambda e, lh=lh, j=j, bb=bb: e.matmul(out=PS[bb][:, 0:64], lhsT=lh, rhs=el_ap[:, j, :], start=(j == 0), stop=False),
                                 reads=[vl_n, el_n, "ones"], writes=[PSN[bb]])
                        for j in range(4):
                            lh = vc[:, j, :] if which == 0 else ones_bf[:]
                            S.op("pe", lambda e, lh=lh, j=j, bb=bb: e.matmul(out=PS[bb][:, 0:64], lhsT=lh, rhs=ec_ap[:, j, :], start=False, stop=(j == 3)),
                                 reads=[vc_n, ec_n, "ones"], writes=[PSN[bb]])
                    rz_ap, rz_n = rz[par]
                    S.op("act", lambda e: e.activation(out=rz_ap, in_=PS[bZ][:, 0:64], func=AF.Copy), reads=[PSN[bZ]], writes=[rz_n])
                    S.op("dve", lambda e: e.reciprocal(out=rz_ap, in_=rz_ap), reads=[rz_n], writes=[rz_n])
                    S.op("dve", lambda e: e.tensor_tensor(out=ost[:, r * 64:(r + 1) * 64], in0=PS[bO][:, 0:64], in1=rz_ap, op=ALU.mult),
                         reads=[PSN[bO], rz_n], writes=[ost_n])

                na_front(0)
                for r in range(64):
                    if r + 1 < 64:
                        na_front(r + 1)
                    na_back(r)
                for tb in range(8):
                    y_ap, y_n = yo[tb % 2]
                    S.op("dve", lambda e, y_ap=y_ap, tb=tb: e.tensor_tensor(out=y_ap, in0=ost[:, tb * 512:(tb + 1) * 512], in1=cg[:, tb * 512:(tb + 1) * 512], op=ALU.mult),
                         reads=[ost_n, cg_n], writes=[y_n])
                    S.dma("sp", lambda e, y_ap=y_ap, tb=tb, h=h: e.dma_start(out=YT[h * 128:(h + 1) * 128, tb * 512:(tb + 1) * 512], in_=y_ap), reads=[y_n])
            ecnt = [0]
            pk = [A.alloc(BF16, LP) for _ in range(2)]
            pv_ = [A.alloc(BF16, 2, 128) for _ in range(2)]
            pq = [A.alloc(BF16, LP) for _ in range(2)]
            pg = [A.alloc(BF16, LP) for _ in range(2)]
            po = [A.alloc(F32, 2, LP) for _ in range(2)]
            plist = [(s_, h_) for s_ in range(NPS) for h_ in range(8)]
            pcur = {}

            def load_p(idx):
                s_, h_ = plist[idx]
                t0 = NS + s_ * LP
                (k_ap, k_n), (v_ap, v_n), (q_ap, q_n), (g_ap, g_n) = pk[idx % 2], pv_[idx % 2], pq[idx % 2], pg[idx % 2]
                S.dma("sp", lambda e: e.dma_start(out=k_ap, in_=KT[h_ * 128:(h_ + 1) * 128, t0:t0 + LP]), writes=[k_n])
                S.dma("sp", lambda e: e.dma_start(out=v_ap, in_=V[t0:t0 + LP, h_ * 128:(h_ + 1) * 128].rearrange("(c p) d -> p c d", p=128)), writes=[v_n])
                S.dma("sp", lambda e: e.dma_start(out=q_ap, in_=QT[h_ * 128:(h_ + 1) * 128, t0:t0 + LP]), writes=[q_n])
                S.dma("sp", lambda e: e.dma_start(out=g_ap, in_=BGT[h_ * 128:(h_ + 1) * 128, t0:t0 + LP]), writes=[g_n])

            load_p(0)
            for idx, (s_, h_) in enumerate(plist):
                if idx + 1 < len(plist):
                    load_p(idx + 1)
                t0 = NS + s_ * LP
                (k_ap, k_n), (v_ap, v_n), (q_ap, q_n), (g_ap, g_n) = pk[idx % 2], pv_[idx % 2], pq[idx % 2], pg[idx % 2]
                o_ap, o_n = po[idx % 2]
                keys = [(k_ap[:, i * 128:(i + 1) * 128], v_ap[:, i, :], 128) for i in range(2)]
                attention(keys, q_ap, q_n, LP, False, scale, [(0,), (1,), (2,), (3,)], [k_n, v_n], E2, ecnt)
                S.op("act", lambda e, o_ap=o_ap: e.activation(out=o_ap[:, 1, :], in_=PS[5][:, 0:LP], func=AF.Copy), reads=[PSN[5]], writes=[o_n])
                S.op("dve", lambda e, o_ap=o_ap: e.reciprocal(out=o_ap[:, 1, :], in_=o_ap[:, 1, :]), reads=[o_n], writes=[o_n])
                S.op("dve", lambda e, o_ap=o_ap: e.tensor_tensor(out=o_ap[:, 0, :], in0=PS[4][:, 0:LP], in1=o_ap[:, 1, :], op=ALU.mult), reads=[PSN[4], o_n], writes=[o_n])
                y_ap, y_n = yo[idx % 2]
                S.op("dve", lambda e, y_ap=y_ap, o_ap=o_ap, g_ap=g_ap: e.tensor_tensor(out=y_ap[:, 0:LP], in0=o_ap[:, 0, :], in1=g_ap, op=ALU.mult), reads=[o_n, g_n], writes=[y_n])
                S.dma("sp", lambda e, y_ap=y_ap, h_=h_, t0=t0: e.dma_start(out=YT[h_ * 128:(h_ + 1) * 128, t0:t0 + LP], in_=y_ap[:, 0:LP]), reads=[y_n])
            S.barrier()

        def stage_sgu(l):
            A.reset()
            gl, gl_n = A.alloc(F32, 1024)
            bs, bs_n = A.alloc(F32, 4, 512)
            wsf, wsf_n = A.alloc(F32, 4, 128)
            wsT, wsT_n = A.alloc(BF16, 4, 128)
            S.dma("sp", lambda e: e.dma_start(out=gl, in_=sguln[l]), writes=[gl_n])
            S.dma("sp", lambda e: e.dma_start(out=bs, in_=sgub[l]), writes=[bs_n])
            S.dma("sp", lambda e: e.dma_start(out=wsf, in_=sguw[l].rearrange("g i j -> i g j")), writes=[wsf_n])
            for g in range(4):
                S.op("pe", lambda e, g=g: e.transpose(out=PS[0][:, g * 128:(g + 1) * 128], in_=wsf[:, g, :], identity=ident[:]), reads=[wsf_n, "ident"], writes=[PSN[0]])
            S.op("act", lambda e: e.activation(out=wsT.rearrange("p a b -> p (a b)"), in_=PS[0][:], func=AF.Copy), reads=[PSN[0]], writes=[wsT_n])
            vsf = [A.alloc(F32, 1024) for _ in range(2)]
            vnf = [A.alloc(F32, 1024) for _ in range(2)]
            vn = [A.alloc(BF16, 4, 1024) for _ in range(2)]
            st = [A.alloc(F32, 24) for _ in range(2)]
            ut = [A.alloc(BF16, 512) for _ in range(3)]
            dg = [A.alloc(BF16, 512) for _ in range(3)]
            t1 = [A.alloc(F32, 512) for _ in range(2)]
            yo = [A.alloc(BF16, 512) for _ in range(3)]
            c = {"v": 0, "u": 0, "ps": 0, "y": 0}
            for tb in range(NT // 512):
                vn_ap, vn_n = vn[tb % 2]
                for ti in range(4):
                    r0 = tb * 512 + ti * 128
                    v_ap, v_n = vsf[c["v"] % 2]
                    f_ap, f_n = vnf[c["v"] % 2]
                    s_ap, s_n = st[c["v"] % 2]
                    c["v"] += 1
                    S.dma("sp", lambda e, v_ap=v_ap, r0=r0: e.dma_start(out=v_ap, in_=VS[r0:r0 + 128, :]), writes=[v_n])
                    for k2 in range(2):
                        S.op("dve", lambda e, s_ap=s_ap, v_ap=v_ap, k2=k2: e.bn_stats(out=s_ap[:, k2 * 6:(k2 + 1) * 6], in_=v_ap[:, k2 * 512:(k2 + 1) * 512]), reads=[v_n], writes=[s_n])
                    S.op("dve", lambda e, s_ap=s_ap: e.bn_aggr(out=s_ap[:, 12:14], in_=s_ap[:, 0:12]), reads=[s_n], writes=[s_n])
                    S.op("act", lambda e, s_ap=s_ap: e.activation(out=s_ap[:, 14:15], in_=s_ap[:, 13:14], func=AF.Sqrt, bias=eps_col, scale=1.0), reads=[s_n, "eps"], writes=[s_n])
                    S.op("dve", lambda e, s_ap=s_ap: e.reciprocal(out=s_ap[:, 14:15], in_=s_ap[:, 14:15]), reads=[s_n], writes=[s_n])
                    S.op("dve", lambda e, s_ap=s_ap: e.scalar_tensor_tensor(out=s_ap[:, 15:16], in0=s_ap[:, 12:13], scalar=-1.0, in1=s_ap[:, 14:15], op0=ALU.mult, op1=ALU.mult), reads=[s_n], writes=[s_n])
                    S.op("act", lambda e, f_ap=f_ap, v_ap=v_ap, s_ap=s_ap: e.activation(out=f_ap, in_=v_ap, func=AF.Identity, scale=s_ap[:, 14:15], bias=s_ap[:, 15:16]), reads=[v_n, s_n], writes=[f_n])
                    S.op("dve", lambda e, vn_ap=vn_ap, ti=ti, f_ap=f_ap: e.tensor_tensor(out=vn_ap[:, ti, :], in0=f_ap, in1=gl, op=ALU.mult), reads=[f_n, gl_n], writes=[vn_n])
                for cc in range(8):
                    g = cc // 2
                    b = c["ps"] % 4
                    c["ps"] += 1
                    for ti in range(4):
                        S.op("pe", lambda e, vn_ap=vn_ap, ti=ti, cc=cc, g=g, b=b: e.matmul(out=PS[b][:, ti * 128:(ti + 1) * 128], lhsT=vn_ap[:, ti, cc * 128:(cc + 1) * 128], rhs=wsT[:, g, :], start=True, stop=True),
                             reads=[vn_n, wsT_n], writes=[PSN[b]])
                    u_ap, u_n = ut[c["u"] % 3]
                    d_ap, d_n = dg[c["u"] % 3]
                    c["u"] += 1
                    S.dma("sp", lambda e, u_ap=u_ap, cc=cc, tb=tb: e.dma_start(out=u_ap, in_=AINT[cc * 128:(cc + 1) * 128, tb * 512:(tb + 1) * 512]), writes=[u_n])
                    S.dma("sp", lambda e, d_ap=d_ap, cc=cc, tb=tb: e.dma_start(out=d_ap, in_=AGT[cc * 128:(cc + 1) * 128, tb * 512:(tb + 1) * 512]), writes=[d_n])
                    t_ap, t_n = t1[c["y"] % 2]
                    y_ap, y_n = yo[c["y"] % 3]
                    c["y"] += 1
                    S.op("dve", lambda e, t_ap=t_ap, b=b, g=g: e.tensor_tensor(out=t_ap, in0=PS[b][:], in1=bs[:, g, :], op=ALU.add), reads=[PSN[b], bs_n], writes=[t_n])
                    S.op("dve", lambda e, t_ap=t_ap, u_ap=u_ap: e.tensor_tensor(out=t_ap, in0=t_ap, in1=u_ap, op=ALU.mult), reads=[t_n, u_n], writes=[t_n])
                    S.op("dve", lambda e, t_ap=t_ap, d_ap=d_ap, y_ap=y_ap: e.tensor_tensor(out=y_ap, in0=t_ap, in1=d_ap, op=ALU.mult), reads=[t_n, d_n], writes=[y_n])
                    S.dma("sp", lambda e, y_ap=y_ap, cc=cc, tb=tb: e.dma_start(out=YT[1024 + cc * 128:1024 + (cc + 1) * 128, tb * 512:(tb + 1) * 512], in_=y_ap), reads=[y_n])
            S.barrier()

        def stage_O(l, last):
            A.reset()
            src = xin if l == 0 else X
            dst = yout if last else X
            wo, wo_n = A.alloc(BF16, 16, D)
            for g in range(4):
                S.dma("pool", lambda e, g=g: e.dma_start(out=wo[:, :, g * 512:(g + 1) * 512], in_=wview(wout[l], g)), writes=[wo_n])
            yT = [A.alloc(BF16, 16, 256) for _ in range(2)]
            NB_O = 4
            NB_X = 5
            xt = [A.alloc(F32, D) for _ in range(NB_X)]
            tt = [A.alloc(F32, D) for _ in range(NB_O)]
            st = [A.alloc(F32, 32) for _ in range(NB_O)]
            ntile = NT // 128
            ycur = {}

            def o_load(n):
                blk, ti = n // 2, n % 2
                if ti == 0:
                    y_ap, y_n = yT[blk % 2]
                    S.dma("sp", lambda e: e.dma_start(out=y_ap, in_=YT.rearrange("(c p) t -> p c t", p=128)[:, :, blk * 256:(blk + 1) * 256]), writes=[y_n])
                    ycur[blk] = (y_ap, y_n)
                r0 = n * 128
                x_ap, x_n = xt[n % NB_X]
                S.dma("sp", lambda e: e.dma_start(out=x_ap, in_=src[r0:r0 + 128, :]), writes=[x_n])

            def o_front(n):
                blk, ti = n // 2, n % 2
                y_ap, y_n = ycur[blk]
                r0 = n * 128
                cnd = 0 if r0 < NS else 1
                x_ap, x_n = xt[n % NB_X]
                t_ap, t_n = tt[n % NB_O]
                pb = (n % 2) * 4
                for g in range(4):
                    for kc in range(16):
                        S.op("pe", lambda e, kc=kc, g=g: e.matmul(out=PS[pb + g][:], lhsT=y_ap[:, kc, ti * 128:(ti + 1) * 128], rhs=wo[:, kc, g * 512:(g + 1) * 512],
                                                              start=(kc == 0), stop=(kc == 15)), reads=[y_n, wo_n], writes=[PSN[pb + g]])
                    S.op("dve", lambda e, g=g: e.tensor_tensor(out=t_ap[:, g * 512:(g + 1) * 512], in0=PS[pb + g][:], in1=G[:, cnd, g * 512:(g + 1) * 512], op=ALU.mult),
                         reads=[PSN[pb + g], "G"], writes=[t_n])
                S.op("dve", lambda e: e.scalar_tensor_tensor(out=t_ap, in0=x_ap, scalar=ALPHA, in1=t_ap, op0=ALU.mult, op1=ALU.add),
                     reads=[x_n, t_n], writes=[t_n])

            def o_mid1(n):
                t_ap, t_n = tt[n % NB_O]
                s_ap, s_n = st[n % NB_O]
                for k4 in range(4):
                    S.op("dve", lambda e, k4=k4: e.bn_stats(out=s_ap[:, k4 * 6:(k4 + 1) * 6], in_=t_ap[:, k4 * 512:(k4 + 1) * 512]), reads=[t_n], writes=[s_n])
                S.op("dve", lambda e: e.bn_aggr(out=s_ap[:, 24:26], in_=s_ap[:, 0:24]), reads=[s_n], writes=[s_n])
                S.op("act", lambda e: e.activation(out=s_ap[:, 26:27], in_=s_ap[:, 25:26], func=AF.Sqrt, bias=eps_col, scale=1.0), reads=[s_n, "eps"], writes=[s_n])

            def o_mid2(n):
                u_ap, u_n = xt[n % NB_X]
                t_ap, t_n = tt[n % NB_O]
                s_ap, s_n = st[n % NB_O]
                S.op("dve", lambda e: e.reciprocal(out=s_ap[:, 26:27], in_=s_ap[:, 26:27]), reads=[s_n], writes=[s_n])
                S.op("dve", lambda e: e.scalar_tensor_tensor(out=s_ap[:, 27:28], in0=s_ap[:, 24:25], scalar=-1.0, in1=s_ap[:, 26:27], op0=ALU.mult, op1=ALU.mult), reads=[s_n], writes=[s_n])
                S.op("act", lambda e: e.activation(out=u_ap, in_=t_ap, func=AF.Identity, scale=s_ap[:, 26:27], bias=s_ap[:, 27:28]), reads=[t_n, s_n], writes=[u_n])

            def o_back(n):
                r0 = n * 128
                u_ap, u_n = xt[n % NB_X]
                S.op("dve", lambda e: e.tensor_tensor(out=u_ap, in0=u_ap, in1=lngb[:, 0, :], op=ALU.mult), reads=[u_n, "lngb"], writes=[u_n])
                S.op("dve", lambda e: e.tensor_tensor(out=u_ap, in0=u_ap, in1=lngb[:, 1, :], op=ALU.add), reads=[u_n, "lngb"], writes=[u_n])
                S.dma("sp", lambda e: e.dma_start(out=dst[r0:r0 + 128, :], in_=u_ap), reads=[u_n])

            o_load(0)
            for step in range(ntile + 3):
                if step + 1 < ntile:
                    o_load(step + 1)
                if step < ntile:
                    o_front(step)
                if 0 <= step - 1 < ntile:
                    o_mid1(step - 1)
                if 0 <= step - 2 < ntile:
                    o_mid2(step - 2)
                if 0 <= step - 3 < ntile:
                    o_back(step - 3)
            S.barrier()

        def on(name):
            return stages is None or name in stages

        for l in range(n_layers):
            if on("mod"):
                stage_mod(l)
            if on("P"):
                stage_P(l)
            if l % 2 == 0:
                if on("pool"):
                    stage_pool(l)
                if on("diff"):
                    stage_diff(l)
            else:
                if on("na"):
                    stage_na(l)
                if on("sgu"):
                    stage_sgu(l)
            if on("O"):
                stage_O(l, last=(l == n_layers - 1))
        S.wait_events("sp", S.all_events())
        S.emit()
        nins = S.nins
    return nc, nins


def _consts():
    ident = np.eye(128, dtype=np.float32)
    permT = np.zeros((128, 128), np.float32)
    for i in range(128):
        d = i % 64
        half = (d % 32) // 16
        p = i + 16 if half == 0 else i - 16
        permT[p, i] = 1.0
    t = np.arange(NS)
    row = (t // 64).astype(np.float32)
    col = (t % 64).astype(np.float32)
    inv = (1.0 / (10000.0 ** (np.arange(0, 32, 2, dtype=np.float32) / 32.0))).astype(np.float32)
    ropec = np.zeros((128, NS), np.float32)
    ropes = np.zeros((128, NS), np.float32)
    for i in range(128):
        d = i % 64
        axis = d // 32
        j = d % 16
        half = (d % 32) // 16
        pos = row if axis == 0 else col
        ang = (pos * inv[j]).astype(np.float32)
        ropec[i] = np.cos(ang)
        ropes[i] = np.sin(ang) * (-1.0 if half == 0 else 1.0)
    pedge = np.zeros((128, 4, 16), np.float32)
    for g, w in enumerate(POOL_WINDOWS):
        half = w // 2
        for i in range(8):
            pedge[:, g, i] = 1.0 / min(w, i + half)
            pedge[:, g, 8 + i] = 1.0 / min(w, 8 - i + half)
    return ident, permT, ropec, ropes, pedge


def _rpb_table(rpb):
    kc = np.arange(64)[:, None]
    qc = np.arange(64)[None, :]
    cstart = np.clip(qc - 8, 0, 48)
    valid = (kc >= cstart) & (kc < cstart + 16)
    dc = np.clip(kc - qc + 15, 0, 30)
    g = rpb[:, :, dc]
    g = np.where(valid[None, None], g, np.float32(NEG)).astype(np.float32)
    return np.ascontiguousarray(np.transpose(g, (2, 0, 1, 3)))


def _rep(v, n=128):
    return np.ascontiguousarray(np.broadcast_to(np.asarray(v, np.float32).reshape(1, -1), (n, np.asarray(v).size)))


_CACHE = {}


def make_in_maps(inputs, n_layers=4):
    f = lambda a: np.ascontiguousarray(np.asarray(a, dtype=np.float32))
    ident, permT, ropec, ropes, pedge = _consts()
    shared = {"ident": ident, "permT": permT, "ropec": ropec, "ropes": ropes, "pedge": pedge}
    for l in range(n_layers):
        bm = f(inputs[f"b_mod_{l}"])
        shared[f"wmod{l}"] = f(inputs[f"w_mod_{l}"])
        shared[f"bmodT{l}"] = np.ascontiguousarray(bm[:4096].reshape(32, 128).T)
        shared[f"bmodG{l}"] = _rep(bm[4096:])
        shared[f"win{l}"] = f(inputs[f"w_in_{l}"])
        shared[f"wout{l}"] = f(inputs[f"w_out_{l}"])
        shared[f"lng{l}"] = _rep(inputs[f"ln_g_{l}"])
        shared[f"lnb{l}"] = _rep(inputs[f"ln_b_{l}"])
        if l % 2 == 0:
            shared[f"poolw{l}"] = f(inputs[f"pool_w_{l}"])
            shared[f"pscale{l}"] = np.ascontiguousarray(f(inputs[f"pool_scale_{l}"]).reshape(8, 128).T)
            shared[f"dlam{l}"] = _rep(f(inputs[f"diff_lam_{l}"]).reshape(-1))
            shared[f"subln{l}"] = np.ascontiguousarray(f(inputs[f"diff_subln_{l}"]).reshape(128, 1))
        else:
            shared[f"rpbT{l}"] = _rpb_table(f(inputs[f"rpb_{l}"]))
            shared[f"sguln{l}"] = _rep(inputs[f"sgu_ln_{l}"])
            shared[f"sguw{l}"] = f(inputs[f"sgu_w_{l}"])
            sb = f(inputs[f"sgu_b_{l}"])
            shared[f"sgub{l}"] = np.ascontiguousarray(np.broadcast_to(np.tile(sb, (1, 4))[None], (128, 4, 512)))
    xs = f(inputs["x_sample"])
    xp = f(inputs["x_prompt"])
    c = f(inputs["c"])
    cctx = f(inputs["c_ctx"])
    maps = []
    for core in range(8):
        p = core // 2
        m = dict(shared)
        m["xin"] = np.ascontiguousarray(np.concatenate([xs[p], xp[4 * core:4 * core + 4].reshape(NPS * LP, D)], axis=0))
        cv_ = np.stack([c[p], cctx], 0).reshape(2, 16, 128)
        m["cvec"] = np.ascontiguousarray(np.transpose(cv_, (2, 0, 1)))
        for l in range(n_layers):
            m[f"ck{l}"] = f(inputs[f"cache_k_l{l}"])[p]
            m[f"cv{l}"] = f(inputs[f"cache_v_l{l}"])[p]
        maps.append(m)
    return maps


def kernel(**inputs):
    if "nc" not in _CACHE:
        _CACHE["nc"] = build_program(4)[0]
    nc = _CACHE["nc"]
    maps = make_in_maps(inputs)
    res = run_bass_kernel_spmd(nc, maps, core_ids=list(range(8)))
    r = res.results
    y_prompt = np.concatenate([r[cidx]["yout"][NS:].reshape(NPS, LP, D) for cidx in range(8)], axis=0)
    y_sample = np.stack([r[2 * p]["yout"][:NS] for p in range(4)], axis=0)
    outs = [y_prompt.astype(np.float32), y_sample.astype(np.float32)]
    for l in range(4):
        outs.append(np.concatenate([r[cidx][f"kout{l}"].reshape(NPS, 8, LP, 128) for cidx in range(8)], axis=0).astype(np.float32))
        outs.append(np.concatenate([r[cidx][f"vout{l}"].reshape(NPS, 8, LP, 128) for cidx in range(8)], axis=0).astype(np.float32))
    return tuple(outs)
```

```python
import math
import os
from contextlib import ExitStack

import numpy as np
import concourse.bass as bass
import concourse.mybir as mybir
from concourse.bass_utils import run_bass_kernel_spmd

F32 = mybir.dt.float32
BF16 = mybir.dt.bfloat16
AF = mybir.ActivationFunctionType
ALU = mybir.AluOpType

D = 2048
NS = 4096
NPS = 4
LP = 256
NT = NS + NPS * LP
PAST = 512
NEG = -30000.0
LN_EPS = 1e-5
ALPHA = (2 * 4) ** 0.25
POOL_WINDOWS = (2, 4, 8, 16)


class Tok:
    __slots__ = ("w", "r")

    def __init__(self):
        self.w = None
        self.r = []


class Sched:
    ENG = ("pe", "act", "dve", "pool", "sp")

    def __init__(self, nc, es):
        self.nc = nc
        ndma = {"sp": 28, "pool": 12}
        self.sem = {k: es.enter_context(nc.semaphore("s_" + k)) for k in ("pe", "act", "dve", "pool")}
        self.cnt = {k: 0 for k in self.sem}
        self.dsem = {q: [es.enter_context(nc.semaphore(f"d_{q}{i}")) for i in range(n)] for q, n in ndma.items()}
        self.dcnt = {q: [0] * n for q, n in ndma.items()}
        self.dnext = {q: 0 for q in ndma}
        self.waited = {e: {} for e in self.ENG}
        self.prog = {e: [] for e in self.ENG}
        self.toks = {}
        self.semobj = {}
        self.nins = 0

    def tok(self, key):
        t = self.toks.get(key)
        if t is None:
            t = self.toks[key] = Tok()
        return t

    def _deps(self, e, reads, writes):
        deps = {}

        def add(ev):
            if ev is None:
                return
            s, v = ev
            k = id(s)
            self.semobj[k] = s
            if deps.get(k, 0) < v:
                deps[k] = v

        for t in reads:
            add(t.w)
        for t in writes:
            add(t.w)
            for ev in t.r:
                add(ev)
        waits = []
        own = id(self.sem["pe"]) if e == "pe" else None
        for k, v in deps.items():
            if k == own:
                continue
            if self.waited[e].get(k, 0) < v:
                self.waited[e][k] = v
                waits.append((self.semobj[k], v))
        return waits

    def _finish(self, ev, reads, writes):
        for t in reads:
            t.r.append(ev)
            if len(t.r) > 64:
                best = {}
                for s, v in t.r:
                    if best.get(id(s), (None, 0))[1] < v:
                        best[id(s)] = (s, v)
                t.r = list(best.values())
        for t in writes:
            t.w = ev
            t.r = []

    def _toks(self, lst):
        return [self.tok(t) if not isinstance(t, Tok) else t for t in lst]

    def op(self, e, fn, reads=(), writes=()):
        reads = self._toks(reads)
        writes = self._toks(writes)
        waits = self._deps(e, reads, writes)
        self.cnt[e] += 1
        ev = (self.sem[e], self.cnt[e])
        self.prog[e].append((waits, fn, ev, 1))
        self._finish(ev, reads, writes)
        self.nins += 1 + len(waits)
        return ev

    def dma(self, q, fn, reads=(), writes=()):
        reads = self._toks(reads)
        writes = self._toks(writes)
        waits = self._deps(q, reads, writes)
        j = self.dnext[q]
        self.dnext[q] = (j + 1) % len(self.dsem[q])
        s = self.dsem[q][j]
        prev = self.dcnt[q][j]
        k = id(s)
        self.semobj[k] = s
        if prev > 0 and self.waited[q].get(k, 0) < prev:
            self.waited[q][k] = prev
            waits.append((s, prev))
        self.dcnt[q][j] = prev + 16
        ev = (s, prev + 16)
        self.prog[q].append((waits, fn, ev, 16))
        self._finish(ev, reads, writes)
        self.nins += 1 + len(waits)
        return ev

    def wait_events(self, e, evs):
        waits = []
        for s, v in evs:
            k = id(s)
            if self.waited[e].get(k, 0) < v:
                self.waited[e][k] = v
                waits.append((s, v))
        if waits:
            self.prog[e].append((waits, None, None, 0))
            self.nins += len(waits)

    def all_events(self):
        evs = [(self.sem[k], self.cnt[k]) for k in self.sem if self.cnt[k] > 0]
        for q in self.dsem:
            for s, c in zip(self.dsem[q], self.dcnt[q]):
                if c > 0:
                    evs.append((s, c))
        return evs

    def barrier(self):
        evs = self.all_events()
        for e in self.ENG:
            self.wait_events(e, evs)
        self.toks = {}

    def emit(self):
        nc = self.nc
        prog = self.prog

        def run(eng, lst):
            for waits, fn, ev, inc in lst:
                for s, v in waits:
                    eng.wait_ge(s, v)
                if fn is not None:
                    fn(eng).then_inc(ev[0], inc)

        with nc.Block() as block:
            @block.tensor
            def _(eng):
                run(eng, prog["pe"])

            @block.scalar
            def _(eng):
                run(eng, prog["act"])

            @block.vector
            def _(eng):
                run(eng, prog["dve"])

            @block.gpsimd
            def _(eng):
                run(eng, prog["pool"])

            @block.sync
            def _(eng):
                run(eng, prog["sp"])


class Arena:
    def __init__(self, handle, nbytes):
        self.h32 = handle
        self.h16 = handle.bitcast(BF16)
        self.nbytes = nbytes
        self.off = 0
        self.uid = 0

    def reset(self):
        self.off = 0

    def alloc(self, dt, *shape, parts=128):
        n = 1
        for s in shape:
            n *= s
        esz = 4 if dt == F32 else 2
        size = (n * esz + 31) // 32 * 32
        assert self.off + size <= self.nbytes, (self.off, size, self.nbytes)
        o = self.off
        self.off += size
        h = self.h32 if dt == F32 else self.h16
        ap = h[0:parts, o // esz:o // esz + n]
        if len(shape) == 2:
            ap = ap.rearrange("p (a b) -> p a b", a=shape[0])
        elif len(shape) == 3:
            ap = ap.rearrange("p (a b c) -> p a b c", a=shape[0], b=shape[1])
        self.uid += 1
        return ap, f"ar{self.uid}"


def build_program(n_layers=4, stages=None):
    nc = bass.Bass("TRN2", target_bir_lowering=False)

    def din(name, shape, dt=F32):
        return nc.dram_tensor(name, list(shape), dt, kind="ExternalInput").ap()

    def dout(name, shape, dt=F32):
        return nc.dram_tensor(name, list(shape), dt, kind="ExternalOutput").ap()

    def dscr(name, shape, dt):
        return nc.dram_tensor(name, list(shape), dt, kind="Internal").ap()

    xin = din("xin", [NT, D])
    cvec = din("cvec", [128, 2, 16])
    NL = n_layers
    ck = [din(f"ck{l}", [8, PAST, 128]) for l in range(NL)]
    cv = [din(f"cv{l}", [8, PAST, 128]) for l in range(NL)]
    wmod = [din(f"wmod{l}", [D, 3 * D]) for l in range(NL)]
    bmodT = [din(f"bmodT{l}", [128, 32]) for l in range(NL)]
    bmodG = [din(f"bmodG{l}", [128, D]) for l in range(NL)]
    win = [din(f"win{l}", [D, 6144 if l % 2 == 0 else 7168]) for l in range(NL)]
    wout = [din(f"wout{l}", [D, D]) for l in range(NL)]
    lng = [din(f"lng{l}", [128, D]) for l in range(NL)]
    lnb = [din(f"lnb{l}", [128, D]) for l in range(NL)]
    poolw = {l: din(f"poolw{l}", [4, 256, 256]) for l in (0, 2) if l < NL}
    pscale = {l: din(f"pscale{l}", [128, 8]) for l in (0, 2) if l < NL}
    dlam = {l: din(f"dlam{l}", [128, 256]) for l in (0, 2) if l < NL}
    subln = {l: din(f"subln{l}", [128, 1]) for l in (0, 2) if l < NL}
    rpbT = {l: din(f"rpbT{l}", [64, 8, 15, 64]) for l in (1, 3) if l < NL}
    sguln = {l: din(f"sguln{l}", [128, 1024]) for l in (1, 3) if l < NL}
    sguw = {l: din(f"sguw{l}", [4, 128, 128]) for l in (1, 3) if l < NL}
    sgub = {l: din(f"sgub{l}", [128, 4, 512]) for l in (1, 3) if l < NL}
    ident_d = din("ident", [128, 128])
    permT_d = din("permT", [128, 128])
    ropec_d = din("ropec", [128, NS])
    ropes_d = din("ropes", [128, NS])
    pedge_d = din("pedge", [128, 4, 16])

    yout = dout("yout", [NT, D])
    kout = [dout(f"kout{l}", [NPS * 8 * LP, 128]) for l in range(NL)]
    vout = [dout(f"vout{l}", [NPS * 8 * LP, 128]) for l in range(NL)]

    X = dscr("X", [NT, D], F32)
    QT = dscr("QT", [1024, NT], BF16)
    KT = dscr("KT", [1024, NT], BF16)
    AINT = dscr("AINT", [1024, NT], BF16)
    AGT = dscr("AGT", [1024, NT], BF16)
    BGT = dscr("BGT", [1024, NT], BF16)
    V = dscr("V", [NT, 1024], BF16)
    VS = dscr("VS", [NT, 1024], F32)
    YT = dscr("YT", [D, NT], BF16)

    with ExitStack() as es:
        S = Sched(nc, es)

        def sbt(name, shape, dt):
            return es.enter_context(nc.sbuf_tensor(name, list(shape), dt))

        ident = sbt("ident_s", [128, 128], F32)
        ones_bf = sbt("ones_bf", [128, 128], BF16)
        zeros_bf = sbt("zeros_bf", [128, 128], BF16)
        perm_bf = sbt("perm_bf", [128, 128], BF16)
        lhsc = sbt("lhsc", [128, 2, 16, 128], BF16)
        silc = sbt("silc", [128, 16, 2], BF16)
        sil = sbt("sil", [128, 2, 16], F32)
        mods = sbt("mods", [128, 2, 2, 16], F32)
        G = sbt("G", [128, 2, D], F32)
        lngb = sbt("lngb", [128, 2, D], F32)
        small = sbt("small", [128, 64], F32)
        ARENA_BYTES = 157 * 1024
        arena_h = sbt("arena", [128, ARENA_BYTES // 4], F32)
        A = Arena(arena_h, ARENA_BYTES)
        PSall = es.enter_context(nc.psum_tensor("psall", [128, 4096], F32))
        PS = [PSall[:, i * 512:(i + 1) * 512] for i in range(8)]
        PSN = [f"ps{i}" for i in range(8)]

        eps_col = small[:, 0:1]
        S.dma("sp", lambda e: e.dma_start(out=ident[:], in_=ident_d), writes=["ident"])
        S.dma("pool", lambda e: e.dma_start(out=perm_bf[:], in_=permT_d), writes=["perm"])
        S.dma("sp", lambda e: e.dma_start(out=sil[:], in_=cvec), writes=["sil"])
        S.op("dve", lambda e: e.memset(ones_bf[:], 1.0), writes=["ones"])
        S.op("dve", lambda e: e.memset(zeros_bf[:], 0.0), writes=["zeros"])
        S.op("dve", lambda e: e.memset(eps_col, LN_EPS), writes=["eps"])
        S.op("act", lambda e: e.activation(out=sil[:], in_=sil[:], func=AF.Silu), reads=["sil"], writes=["sil"])
        for cnd in range(2):
            S.op("dve", lambda e, cnd=cnd: e.tensor_copy(out=silc[:, :, cnd], in_=sil[:, cnd, :]), reads=["sil"], writes=["silc"])
            for kc in range(16):
                S.op("dve", lambda e, cnd=cnd, kc=kc: e.tensor_scalar(out=lhsc[:, cnd, kc, :], in0=zeros_bf[:], scalar1=sil[:, cnd, kc:kc + 1],
                                                                      scalar2=None, op0=ALU.add),
                     reads=["sil", "zeros"], writes=["lhsc"])
        S.barrier()

        def wview(w, g):
            return w.rearrange("(c p) n -> p c n", p=128)[:, :, g * 512:(g + 1) * 512]

        def stage_mod(l):
            A.reset()
            wb = [A.alloc(BF16, 16, 512) for _ in range(2)]
            bmT, bmTn = A.alloc(F32, 32)
            bmG, bmGn = A.alloc(F32, D)
            S.dma("sp", lambda e: e.dma_start(out=bmT, in_=bmodT[l]), writes=[bmTn])
            S.dma("sp", lambda e: e.dma_start(out=bmG, in_=bmodG[l]), writes=[bmGn])
            S.dma("sp", lambda e: e.dma_start(out=lngb[:, 0, :], in_=lng[l]), writes=["lngb"])
            S.dma("sp", lambda e: e.dma_start(out=lngb[:, 1, :], in_=lnb[l]), writes=["lngb"])
            psM = PS[7]
            for gi in range(12):
                w_ap, w_n = wb[gi % 2]
                S.dma("pool", lambda e, w_ap=w_ap, gi=gi: e.dma_start(out=w_ap, in_=wview(wmod[l], gi)), writes=[w_n])
                if gi < 8:
                    for j in range(4):
                        cc = gi * 4 + j
                        for kc in range(16):
                            S.op("pe", lambda e, w_ap=w_ap, j=j, kc=kc, cc=cc: e.matmul(
                                out=psM[:, cc * 2:cc * 2 + 2], lhsT=w_ap[:, kc, j * 128:(j + 1) * 128], rhs=silc[:, kc, :],
                                start=(kc == 0), stop=(kc == 15)), reads=[w_n, "silc"], writes=[PSN[7]])
                else:
                    for cnd in range(2):
                        b = (gi * 2 + cnd) % 4
                        for kc in range(16):
                            S.op("pe", lambda e, w_ap=w_ap, cnd=cnd, kc=kc, b=b: e.matmul(
                                out=PS[b][:], lhsT=lhsc[:, cnd, kc, :], rhs=w_ap[:, kc, :], start=(kc == 0), stop=(kc == 15)),
                                reads=[w_n, "lhsc"], writes=[PSN[b]])
                        c0 = (gi - 8) * 512
                        S.op("dve", lambda e, cnd=cnd, b=b, c0=c0: e.tensor_tensor(out=G[:, cnd, c0:c0 + 512], in0=PS[b][:], in1=bmG[:, c0:c0 + 512], op=ALU.add),
                             reads=[PSN[b], bmGn], writes=["G"])
            pv = psM[:, 0:64].rearrange("p (c t) -> p c t", t=2)
            for cnd in range(2):
                S.op("dve", lambda e, cnd=cnd: e.tensor_tensor(out=mods[:, cnd, :, :].rearrange("p a b -> p (a b)"), in0=pv[:, :, cnd], in1=bmT, op=ALU.add),
                     reads=[PSN[7], bmTn], writes=["mods"])
                S.op("dve", lambda e, cnd=cnd: e.tensor_scalar(out=mods[:, cnd, 1, :], in0=mods[:, cnd, 1, :], scalar1=1.0, scalar2=None, op0=ALU.add),
                     reads=["mods"], writes=["mods"])
            S.barrier()

        PDBG = set(os.environ.get('PDBG', 'fm,rope,tm,kv').split(','))

        def stage_P(l):
            even = l % 2 == 0
            ngrp = 12 if even else 14
            src = xin if l == 0 else X
            A.reset()
            hT = [A.alloc(BF16, 16, 1024) for _ in range(2)]
            wb = [A.alloc(BF16, 16, 512) for _ in range(2)]
            xt = [A.alloc(F32, D) for _ in range(2)]
            obf = [A.alloc(BF16, 512) for _ in range(4)]
            of32 = [A.alloc(F32, 512) for _ in range(2)]
            qb = [A.alloc(BF16, 512) for _ in range(2)]
            t1 = [A.alloc(F32, 512) for _ in range(2)]
            t2 = [A.alloc(F32, 512) for _ in range(2)]
            cs = [A.alloc(F32, 2, 512) for _ in range(4)]
            cnt = {"obf": 0, "of32": 0, "ps": 0, "rope": 0, "w": 0, "x": 0, "tp": 0}
            if even:
                kinds = ["ain", "ain", "gA", "gA", "q", "q", "k", "k", "v", "v", "gB", "gB"]
            else:
                kinds = ["q", "q", "k", "k", "v", "v", "gB", "gB", "ain", "ain", "vs", "vs", "gA", "gA"]
            dstT = {"ain": AINT, "gA": AGT, "gB": BGT, "q": QT, "k": KT}
            def tr_tile(tsb, ti):
                tok0 = tsb * 1024
                cnd = 0 if tsb < 4 else 1
                h_ap, h_n = hT[tsb % 2]
                x_ap, x_n = xt[cnt["x"] % 2]
                cnt["x"] += 1
                r0 = tok0 + ti * 128
                S.dma("sp", lambda e: e.dma_start(out=x_ap, in_=src[r0:r0 + 128, :]), writes=[x_n])
                for q4 in range(4):
                    b = 4 + cnt["tp"] % 2
                    cnt["tp"] += 1
                    for j in range(4):
                        kc = q4 * 4 + j
                        S.op("pe", lambda e, kc=kc, j=j, b=b: e.transpose(out=PS[b][:, j * 128:(j + 1) * 128], in_=x_ap[:, kc * 128:(kc + 1) * 128], identity=ident[:]),
                             reads=[x_n, "ident"], writes=[PSN[b]])
                    for j in range(4):
                        kc = q4 * 4 + j
                        S.op("act", lambda e, kc=kc, j=j, b=b: e.activation(
                            out=h_ap[:, kc, ti * 128:(ti + 1) * 128], in_=PS[b][:, j * 128:(j + 1) * 128], func=AF.Identity,
                            scale=mods[:, cnd, 1, kc:kc + 1], bias=mods[:, cnd, 0, kc:kc + 1]),
                            reads=[PSN[b], "mods"], writes=[h_n])

            for ti in range(8):
                tr_tile(0, ti)
            for tsb in range(5):
                tok0 = tsb * 1024
                cnd = 0 if tsb < 4 else 1
                h_ap, h_n = hT[tsb % 2]
                if even and tsb < 4:
                    for hf_ in range(2):
                        cs_ap_, cs_n_ = cs[(tsb % 2) * 2 + hf_]
                        tt_ = tok0 + hf_ * 512
                        S.dma("sp", lambda e, cs_ap_=cs_ap_, tt_=tt_: e.dma_start(out=cs_ap_[:, 0, :], in_=ropec_d[:, tt_:tt_ + 512]), writes=[cs_n_])
                        S.dma("sp", lambda e, cs_ap_=cs_ap_, tt_=tt_: e.dma_start(out=cs_ap_[:, 1, :], in_=ropes_d[:, tt_:tt_ + 512]), writes=[cs_n_])
                for g in range(ngrp):
                    kind = kinds[g]
                    gsub = g % 2
                    if kind == "k" and False:
                        pass
                    w_ap, w_n = wb[cnt["w"] % 2]
                    cnt["w"] += 1
                    S.dma("pool", lambda e, w_ap=w_ap, g=g: e.dma_start(out=w_ap, in_=wview(win[l], g)), writes=[w_n])
                    if kind in ("ain", "gA", "gB", "q", "k") and "fm" in PDBG:
                        for j in range(4):
                            row0 = (gsub * 4 + j) * 128
                            for hf in range(2):
                                b = cnt["ps"] % 4
                                cnt["ps"] += 1
                                t0 = tok0 + hf * 512
                                for kc in range(16):
                                    S.op("pe", lambda e, w_ap=w_ap, h_ap=h_ap, j=j, kc=kc, hf=hf, b=b: e.matmul(
                                        out=PS[b][:], lhsT=w_ap[:, kc, j * 128:(j + 1) * 128], rhs=h_ap[:, kc, hf * 512:(hf + 1) * 512],
                                        start=(kc == 0), stop=(kc == 15)), reads=[w_n, h_n], writes=[PSN[b]])
                                o_ap, o_n = obf[cnt["obf"] % 4]
                                cnt["obf"] += 1
                                if kind in ("gA", "gB"):
                                    S.op("act", lambda e, o_ap=o_ap, b=b: e.activation(out=o_ap, in_=PS[b][:], func=AF.Silu), reads=[PSN[b]], writes=[o_n])
                                elif kind in ("q", "k") and even and tsb < 4 and "rope" in PDBG:
                                    ri = cnt["rope"] % 2
                                    cnt["rope"] += 1
                                    qb_ap, qb_n = qb[ri]
                                    t1_ap, t1_n = t1[ri]
                                    t2_ap, t2_n = t2[ri]
                                    cs_ap, cs_n = cs[(tsb % 2) * 2 + hf]
                                    S.op("act", lambda e, qb_ap=qb_ap, b=b: e.activation(out=qb_ap, in_=PS[b][:], func=AF.Copy), reads=[PSN[b]], writes=[qb_n])
                                    b2 = 6 + ri
                                    S.op("pe", lambda e, qb_ap=qb_ap, b2=b2: e.matmul(out=PS[b2][:], lhsT=perm_bf[:], rhs=qb_ap, start=True, stop=True),
                                         reads=[qb_n, "perm"], writes=[PSN[b2]])
                                    S.op("dve", lambda e, t1_ap=t1_ap, cs_ap=cs_ap, qb_ap=qb_ap: e.tensor_tensor(out=t1_ap, in0=qb_ap, in1=cs_ap[:, 0, :], op=ALU.mult),
                                         reads=[qb_n, cs_n], writes=[t1_n])
                                    S.op("dve", lambda e, t2_ap=t2_ap, cs_ap=cs_ap, b2=b2: e.tensor_tensor(out=t2_ap, in0=PS[b2][:], in1=cs_ap[:, 1, :], op=ALU.mult),
                                         reads=[PSN[b2], cs_n], writes=[t2_n])
                                    S.op("dve", lambda e, o_ap=o_ap, t1_ap=t1_ap, t2_ap=t2_ap: e.tensor_tensor(out=o_ap, in0=t1_ap, in1=t2_ap, op=ALU.add),
                                         reads=[t1_n, t2_n], writes=[o_n])
                                else:
                                    S.op("act", lambda e, o_ap=o_ap, b=b: e.activation(out=o_ap, in_=PS[b][:], func=AF.Copy), reads=[PSN[b]], writes=[o_n])
                                dst = dstT[kind]
                                S.dma("sp", lambda e, o_ap=o_ap, dst=dst, row0=row0, t0=t0: e.dma_start(out=dst[row0:row0 + 128, t0:t0 + 512], in_=o_ap), reads=[o_n])
                    if (kind in ("v", "vs") or (kind == "k" and tsb == 4)) and "tm" in PDBG:
                        for ti in range(8):
                            b = cnt["ps"] % 4
                            cnt["ps"] += 1
                            r0 = tok0 + ti * 128
                            c0 = gsub * 512
                            for kc in range(16):
                                S.op("pe", lambda e, w_ap=w_ap, h_ap=h_ap, kc=kc, ti=ti, b=b: e.matmul(
                                    out=PS[b][:], lhsT=h_ap[:, kc, ti * 128:(ti + 1) * 128], rhs=w_ap[:, kc, :], start=(kc == 0), stop=(kc == 15)),
                                    reads=[w_n, h_n], writes=[PSN[b]])
                            if kind == "v":
                                o_ap, o_n = obf[cnt["obf"] % 4]
                                cnt["obf"] += 1
                                S.op("act", lambda e, o_ap=o_ap, b=b: e.activation(out=o_ap, in_=PS[b][:], func=AF.Copy), reads=[PSN[b]], writes=[o_n])
                                S.dma("sp", lambda e, o_ap=o_ap, r0=r0, c0=c0: e.dma_start(out=V[r0:r0 + 128, c0:c0 + 512], in_=o_ap), reads=[o_n])
                            if (kind == "vs" or tsb == 4) and "kv" in PDBG:
                                f_ap, f_n = of32[cnt["of32"] % 2]
                                cnt["of32"] += 1
                                S.op("act", lambda e, f_ap=f_ap, b=b: e.activation(out=f_ap, in_=PS[b][:], func=AF.Copy), reads=[PSN[b]], writes=[f_n])
                                if kind == "vs":
                                    S.dma("sp", lambda e, f_ap=f_ap, r0=r0, c0=c0: e.dma_start(out=VS[r0:r0 + 128, c0:c0 + 512], in_=f_ap), reads=[f_n])
                                else:
                                    dsto = kout[l] if kind == "k" else vout[l]
                                    sq = ti // 2
                                    tt0 = (ti % 2) * 128
                                    for hh in range(4):
                                        S.dma("sp", lambda e, f_ap=f_ap, dsto=dsto, sq=sq, tt0=tt0, gsub=gsub, hh=hh: e.dma_start(
                                            out=(VS[((sq * 8 + gsub * 4 + hh) * LP + tt0) // 8:((sq * 8 + gsub * 4 + hh) * LP + tt0) // 8 + 128, 0:128] if os.environ.get("KVDBG") else dsto[(sq * 8 + gsub * 4 + hh) * LP + tt0:(sq * 8 + gsub * 4 + hh) * LP + tt0 + 128, :]), in_=f_ap[:, hh * 128:(hh + 1) * 128]), reads=[f_n])
                if tsb + 1 < 5:
                    for ti_ in range(8):
                        tr_tile(tsb + 1, ti_)
            S.barrier()

        def attention(keys, q_ap, q_n, nq, two, scale, bank_set, reads_extra, E_bufs, ecnt):
            nkc = len(keys)
            wide = two and nq == 512

            def emit_S(i):
                kT_ap, v_ap, nk = keys[i]
                sb_ = bank_set[i % len(bank_set)]
                if two:
                    S.op("pe", lambda e: e.matmul(out=PS[sb_[0]][0:nk, 0:nq], lhsT=kT_ap, rhs=q_ap[0], start=True, stop=True),
                         reads=reads_extra + [q_n], writes=[PSN[sb_[0]]])
                    S.op("pe", lambda e: e.matmul(out=PS[sb_[1]][0:nk, 0:nq], lhsT=kT_ap, rhs=q_ap[1], start=True, stop=True),
                         reads=reads_extra + [q_n], writes=[PSN[sb_[1]]])
                else:
                    S.op("pe", lambda e: e.matmul(out=PS[sb_[0]][0:nk, 0:nq], lhsT=kT_ap, rhs=q_ap, start=True, stop=True),
                         reads=reads_extra + [q_n], writes=[PSN[sb_[0]]])

            def emit_pv(i, s_i, rhs_ap, e_n):
                kT_ap, v_ap, nk = keys[i]
                ob, zb = 4 + 2 * s_i, 5 + 2 * s_i
                S.op("pe", lambda e: e.matmul(out=PS[ob][:, 0:nq], lhsT=v_ap, rhs=rhs_ap, start=(i == 0), stop=(i == nkc - 1)),
                     reads=reads_extra + [e_n], writes=[PSN[ob]])
                S.op("pe", lambda e: e.matmul(out=PS[zb][:, 0:nq], lhsT=ones_bf[0:nk, :], rhs=rhs_ap, start=(i == 0), stop=(i == nkc - 1)),
                     reads=[e_n, "ones"], writes=[PSN[zb]])

            def emit_exp(i):
                kT_ap, v_ap, nk = keys[i]
                sb_ = bank_set[i % len(bank_set)]
                pend = []
                if wide:
                    e_ap, e_n = E_bufs[ecnt[0] % len(E_bufs)]
                    ecnt[0] += 1
                    base = sb_[0] * 512
                    S.op("act", lambda e: e.activation(out=e_ap[0:nk, 0:1024], in_=PSall[0:nk, base:base + 1024], func=AF.Exp, scale=scale),
                         reads=[PSN[sb_[0]], PSN[sb_[1]]], writes=[e_n])
                    for s_i in range(2):
                        pend.append((s_i, e_ap[0:nk, s_i * 512:(s_i + 1) * 512], e_n))
                else:
                    for s_i in range(2 if two else 1):
                        e_ap, e_n = E_bufs[ecnt[0] % len(E_bufs)]
                        ecnt[0] += 1
                        S.op("act", lambda e, e_ap=e_ap, s_i=s_i: e.activation(out=e_ap[0:nk, 0:nq], in_=PS[sb_[s_i]][0:nk, 0:nq], func=AF.Exp, scale=scale),
                             reads=[PSN[sb_[s_i]]], writes=[e_n])
                        pend.append((s_i, e_ap[0:nk, 0:nq], e_n))
                return pend

            nset = len(bank_set)
            for i0 in range(min(nset, nkc)):
                emit_S(i0)
            for i in range(nkc):
                pend = emit_exp(i)
                if i + nset < nkc:
                    emit_S(i + nset)
                for (s_i, rhs_ap, e_n) in pend:
                    emit_pv(i, s_i, rhs_ap, e_n)

        def stage_pool(l):
            A.reset()
            pe_t, pe_n = A.alloc(F32, 4, 16)
            psc, psc_n = A.alloc(F32, 8)
            S.dma("sp", lambda e: e.dma_start(out=pe_t, in_=pedge_d), writes=[pe_n])
            S.dma("sp", lambda e: e.dma_start(out=psc, in_=pscale[l]), writes=[psc_n])
            wp = [A.alloc(BF16, 2, 256) for _ in range(4)]
            for g in range(4):
                S.dma("pool", lambda e, g=g: e.dma_start(out=wp[g][0], in_=poolw[l][g].rearrange("(c p) d -> p c d", p=128)), writes=[wp[g][1]])
            NB = NS + 16
            xb = [A.alloc(BF16, NB) for _ in range(2)]
            pa, pa_n = A.alloc(F32, NB)
            pb, pb_n = A.alloc(F32, NB)
            pooled = [A.alloc(BF16, 2, NS) for _ in range(2)]
            ag = [A.alloc(BF16, NS) for _ in range(2)]
            yo = [A.alloc(BF16, 512) for _ in range(3)]
            c = {"x": 0, "ps": 0, "yo": 0, "ag": 0, "pl": 0}
            seqs = [(0, NS)] + [(NS + s * LP, LP) for s in range(NPS)]
            for (t0, L) in seqs:
                N = L + 16
                for g in range(4):
                    w = POOL_WINDOWS[g]
                    half = w // 2
                    pl_ap, pl_n = pooled[c["pl"] % 2]
                    c["pl"] += 1
                    for j in range(2):
                        ch = 2 * g + j
                        x_ap, x_n = xb[c["x"] % 2]
                        c["x"] += 1
                        S.op("pool", lambda e, x_ap=x_ap: e.memset(x_ap[:, 0:8], 0.0), writes=[x_n])
                        S.op("pool", lambda e, x_ap=x_ap, L=L: e.memset(x_ap[:, 8 + L:16 + L], 0.0), writes=[x_n])
                        S.dma("sp", lambda e, x_ap=x_ap, ch=ch, t0=t0, L=L: e.dma_start(out=x_ap[:, 8:8 + L], in_=AINT[ch * 128:(ch + 1) * 128, t0:t0 + L]), writes=[x_n])
                        cur, cur_n, width, n = x_ap, x_n, 1, N
                        bufs = [(pa, pa_n), (pb, pb_n)]
                        bi = 0
                        while width < w:
                            o_ap, o_n = bufs[bi]
                            bi ^= 1
                            n2 = n - width
                            S.op("dve", lambda e, o_ap=o_ap, cur=cur, n2=n2, width=width: e.tensor_tensor(out=o_ap[:, 0:n2], in0=cur[:, 0:n2], in1=cur[:, width:width + n2], op=ALU.add),
                                 reads=[cur_n], writes=[o_n])
                            cur, cur_n, n, width = o_ap, o_n, n2, width * 2
                        s0 = 8 - half
                        S.op("dve", lambda e, pl_ap=pl_ap, j=j, cur=cur, s0=s0, L=L, w=w, x_ap=x_ap: e.scalar_tensor_tensor(
                            out=pl_ap[:, j, 0:L], in0=cur[:, s0:s0 + L], scalar=1.0 / w, in1=x_ap[:, 8:8 + L], op0=ALU.mult, op1=ALU.subtract),
                            reads=[cur_n, x_n], writes=[pl_n])
                        o_ap, o_n = bufs[bi]
                        for (a0, e0) in ((0, 0), (L - 8, 8)):
                            S.op("dve", lambda e, o_ap=o_ap, cur=cur, s0=s0, a0=a0, e0=e0, g=g: e.tensor_tensor(
                                out=o_ap[:, a0:a0 + 8], in0=cur[:, s0 + a0:s0 + a0 + 8], in1=pe_t[:, g, e0:e0 + 8], op=ALU.mult),
                                reads=[cur_n, pe_n], writes=[o_n])
                            S.op("dve", lambda e, o_ap=o_ap, pl_ap=pl_ap, j=j, a0=a0, x_ap=x_ap: e.tensor_tensor(
                                out=pl_ap[:, j, a0:a0 + 8], in0=o_ap[:, a0:a0 + 8], in1=x_ap[:, 8 + a0:16 + a0], op=ALU.subtract),
                                reads=[o_n, x_n], writes=[pl_n])
                    for dch in range(2):
                        ch = 2 * g + dch
                        ag_ap, ag_n = ag[c["ag"] % 2]
                        c["ag"] += 1
                        S.dma("sp", lambda e, ag_ap=ag_ap, ch=ch, t0=t0, L=L: e.dma_start(out=ag_ap[:, 0:L], in_=AGT[ch * 128:(ch + 1) * 128, t0:t0 + L]), writes=[ag_n])
                        nb = max(1, L // 512)
                        bw = min(L, 512)
                        for tb in range(nb):
                            b = c["ps"] % 4
                            c["ps"] += 1
                            for cc in range(2):
                                S.op("pe", lambda e, g=g, cc=cc, dch=dch, pl_ap=pl_ap, tb=tb, bw=bw, b=b: e.matmul(
                                    out=PS[b][:, 0:bw], lhsT=wp[g][0][:, cc, dch * 128:(dch + 1) * 128], rhs=pl_ap[:, cc, tb * 512:tb * 512 + bw],
                                    start=(cc == 0), stop=(cc == 1)), reads=[wp[g][1], pl_n], writes=[PSN[b]])
                            y_ap, y_n = yo[c["yo"] % 3]
                            c["yo"] += 1
                            S.op("dve", lambda e, y_ap=y_ap, b=b, bw=bw, ch=ch, ag_ap=ag_ap, tb=tb: e.scalar_tensor_tensor(
                                out=y_ap[:, 0:bw], in0=PS[b][:, 0:bw], scalar=psc[:, ch:ch + 1], in1=ag_ap[:, tb * 512:tb * 512 + bw], op0=ALU.mult, op1=ALU.mult),
                                reads=[PSN[b], psc_n, ag_n], writes=[y_n])
                            S.dma("sp", lambda e, y_ap=y_ap, ch=ch, t0=t0, tb=tb, bw=bw: e.dma_start(out=YT[ch * 128:(ch + 1) * 128, t0 + tb * 512:t0 + tb * 512 + bw], in_=y_ap[:, 0:bw]),
                                  reads=[y_n])
            S.barrier()

        def stage_diff(l):
            A.reset()
            lam_init = 0.8 - 0.6 * math.exp(-0.3 * l)
            dl, dl_n = A.alloc(F32, 256)
            sub, sub_n = A.alloc(F32, 1)
            lw, lw_n = A.alloc(F32, 136)
            S.dma("sp", lambda e: e.dma_start(out=dl, in_=dlam[l]), writes=[dl_n])
            S.dma("sp", lambda e: e.dma_start(out=sub, in_=subln[l]), writes=[sub_n])
            S.op("dve", lambda e: e.tensor_tensor(out=lw[:, 0:64], in0=dl[:, 0:64], in1=dl[:, 64:128], op=ALU.mult), reads=[dl_n], writes=[lw_n])
            S.op("dve", lambda e: e.tensor_tensor(out=lw[:, 64:128], in0=dl[:, 128:192], in1=dl[:, 192:256], op=ALU.mult), reads=[dl_n], writes=[lw_n])
            S.op("dve", lambda e: e.reduce_sum(out=lw[:, 128:129], in_=lw[:, 0:64], axis=mybir.AxisListType.X), reads=[lw_n], writes=[lw_n])
            S.op("dve", lambda e: e.reduce_sum(out=lw[:, 129:130], in_=lw[:, 64:128], axis=mybir.AxisListType.X), reads=[lw_n], writes=[lw_n])
            S.op("act", lambda e: e.activation(out=lw[:, 130:132], in_=lw[:, 128:130], func=AF.Exp), reads=[lw_n], writes=[lw_n])
            S.op("dve", lambda e: e.tensor_tensor(out=lw[:, 132:133], in0=lw[:, 131:132], in1=lw[:, 130:131], op=ALU.subtract), reads=[lw_n], writes=[lw_n])
            S.op("dve", lambda e: e.tensor_scalar(out=lw[:, 132:133], in0=lw[:, 132:133], scalar1=-lam_init, scalar2=None, op0=ALU.add), reads=[lw_n], writes=[lw_n])
            S.op("dve", lambda e: e.tensor_scalar(out=lw[:, 133:134], in0=sub, scalar1=(1.0 - lam_init), scalar2=None, op0=ALU.mult), reads=[lw_n, sub_n], writes=[lw_n])
            nlam = lw[:, 132:133]
            c1 = lw[:, 133:134]

            NK = NS + PAST
            kt = [A.alloc(BF16, NK) for _ in range(2)]
            vt = [A.alloc(BF16, NK // 128, 128) for _ in range(2)]
            ckf, ckf_n = A.alloc(F32, 4, 128)
            qt = [A.alloc(BF16, 2, 512) for _ in range(2)]
            for q_ap_, q_n_ in qt:
                S.op("dve", lambda e, q_ap_=q_ap_: e.memset(q_ap_.rearrange("p a b -> p (a b)"), 0.0), writes=[q_n_])
            bg = [A.alloc(BF16, 512) for _ in range(2)]
            E = [A.alloc(BF16, 1024) for _ in range(3)]
            tmp = [A.alloc(F32, 512) for _ in range(6)]
            sqb = [A.alloc(BF16, 512) for _ in range(2)]
            yo = [A.alloc(BF16, 512) for _ in range(2)]
            ecnt = [0]
            c = {"q": 0, "y": 0}

            def epilogue(nq, h, tq0, bg_ap, bg_n):
                (r1, r1n), (r2, r2n), (o1, o1n), (o2, o2n), (oo, oon), (rs, rsn) = tmp
                S.op("act", lambda e: e.activation(out=r1[:, 0:nq], in_=PS[5][:, 0:nq], func=AF.Copy), reads=[PSN[5]], writes=[r1n])
                S.op("act", lambda e: e.activation(out=r2[:, 0:nq], in_=PS[7][:, 0:nq], func=AF.Copy), reads=[PSN[7]], writes=[r2n])
                S.op("dve", lambda e: e.reciprocal(out=r1[:, 0:nq], in_=r1[:, 0:nq]), reads=[r1n], writes=[r1n])
                S.op("dve", lambda e: e.reciprocal(out=r2[:, 0:nq], in_=r2[:, 0:nq]), reads=[r2n], writes=[r2n])
                S.op("dve", lambda e: e.tensor_tensor(out=o1[:, 0:nq], in0=PS[4][:, 0:nq], in1=r1[:, 0:nq], op=ALU.mult), reads=[PSN[4], r1n], writes=[o1n])
                S.op("dve", lambda e: e.tensor_tensor(out=o2[:, 0:nq], in0=PS[6][:, 0:nq], in1=r2[:, 0:nq], op=ALU.mult), reads=[PSN[6], r2n], writes=[o2n])
                S.op("dve", lambda e: e.scalar_tensor_tensor(out=oo[:, 0:nq], in0=o2[:, 0:nq], scalar=nlam, in1=o1[:, 0:nq], op0=ALU.mult, op1=ALU.add),
                     reads=[o1n, o2n, lw_n], writes=[oon])
                sq_ap, sq_n = sqb[c["y"] % 2]
                S.op("act", lambda e: e.activation(out=sq_ap[:, 0:nq], in_=oo[:, 0:nq], func=AF.Square), reads=[oon], writes=[sq_n])
                S.op("pe", lambda e: e.matmul(out=PS[0][:, 0:nq], lhsT=ones_bf[:], rhs=sq_ap[:, 0:nq], start=True, stop=True), reads=[sq_n, "ones"], writes=[PSN[0]])
                S.op("act", lambda e: e.activation(out=rs[:, 0:nq], in_=PS[0][:, 0:nq], func=AF.Sqrt, bias=eps_col, scale=1.0 / 128.0), reads=[PSN[0], "eps"], writes=[rsn])
                S.op("dve", lambda e: e.reciprocal(out=rs[:, 0:nq], in_=rs[:, 0:nq]), reads=[rsn], writes=[rsn])
                S.op("dve", lambda e: e.tensor_tensor(out=oo[:, 0:nq], in0=oo[:, 0:nq], in1=rs[:, 0:nq], op=ALU.mult), reads=[oon, rsn], writes=[oon])
                y_ap, y_n = yo[c["y"] % 2]
                c["y"] += 1
                S.op("dve", lambda e: e.scalar_tensor_tensor(out=y_ap[:, 0:nq], in0=oo[:, 0:nq], scalar=c1, in1=bg_ap[:, 0:nq], op0=ALU.mult, op1=ALU.mult),
                     reads=[oon, lw_n, bg_n], writes=[y_n])
                S.dma("sp", lambda e: e.dma_start(out=YT[1024 + h * 128:1024 + (h + 1) * 128, tq0:tq0 + nq], in_=y_ap[:, 0:nq]), reads=[y_n])

            scale = 64 ** -0.5
            bank_sets = [(0, 1), (2, 3)]
            def load_head(h):
                kt_ap, kt_n = kt[h % 2]
                vt_ap, vt_n = vt[h % 2]
                S.dma("sp", lambda e: e.dma_start(out=kt_ap[:, 0:NS], in_=KT[h * 128:(h + 1) * 128, 0:NS]), writes=[kt_n])
                S.dma("sp", lambda e: e.dma_start(out=vt_ap[:, 0:NS // 128, :], in_=V[0:NS, h * 128:(h + 1) * 128].rearrange("(c p) d -> p c d", p=128)), writes=[vt_n])
                S.dma("pool", lambda e: e.dma_start(out=vt_ap[:, NS // 128:NK // 128, :], in_=cv[l][h].rearrange("(c p) d -> p c d", p=128)), writes=[vt_n])
                S.dma("sp", lambda e: e.dma_start(out=ckf, in_=ck[l][h].rearrange("(c p) d -> p c d", p=128)), writes=[ckf_n])

            def ctx_head(h):
                kt_ap, kt_n = kt[h % 2]
                for j in range(4):
                    S.op("pe", lambda e, j=j: e.transpose(out=PS[2][:, j * 128:(j + 1) * 128], in_=ckf[:, j, :], identity=ident[:]), reads=[ckf_n, "ident"], writes=[PSN[2]])
                S.op("act", lambda e: e.activation(out=kt_ap[:, NS:NK], in_=PS[2][:], func=AF.Copy), reads=[PSN[2]], writes=[kt_n])

            qcur = {}

            def load_q(h, qb_i):
                q_ap, q_n = qt[c["q"] % 2]
                bg_ap, bg_n = bg[c["q"] % 2]
                c["q"] += 1
                tq0 = qb_i * 512
                S.dma("sp", lambda e: e.dma_start(out=q_ap[0:64, 0, :], in_=QT[h * 128:h * 128 + 64, tq0:tq0 + 512]), writes=[q_n])
                S.dma("sp", lambda e: e.dma_start(out=q_ap[64:128, 1, :], in_=QT[h * 128 + 64:(h + 1) * 128, tq0:tq0 + 512]), writes=[q_n])
                S.dma("sp", lambda e: e.dma_start(out=bg_ap, in_=BGT[h * 128:(h + 1) * 128, tq0:tq0 + 512]), writes=[bg_n])
                qcur[(h, qb_i)] = (q_ap, q_n, bg_ap, bg_n)

            blocks = [(h, qb_i) for h in range(8) for qb_i in range(NS // 512)]
            load_head(0)
            ctx_head(0)
            load_q(0, 0)
            for bi, (h, qb_i) in enumerate(blocks):
                kt_ap, kt_n = kt[h % 2]
                vt_ap, vt_n = vt[h % 2]
                if bi + 1 < len(blocks):
                    load_q(*blocks[bi + 1])
                if qb_i == 0 and h + 1 < 8:
                    load_head(h + 1)
                keys = [(kt_ap[:, i * 128:(i + 1) * 128], vt_ap[:, i, :], 128) for i in range(NK // 128)]
                q_ap, q_n, bg_ap, bg_n = qcur[(h, qb_i)]
                tq0 = qb_i * 512
                attention(keys, (q_ap[:, 0, :], q_ap[:, 1, :]), q_n, 512, True, scale, bank_sets, [kt_n, vt_n], E, ecnt)
                epilogue(512, h, tq0, bg_ap, bg_n)
                if qb_i == NS // 512 - 1 and h + 1 < 8:
                    ctx_head(h + 1)
            plist = [(s_, h_) for s_ in range(NPS) for h_ in range(8)]
            pcur = {}

            def load_p(idx):
                s_, h_ = plist[idx]
                t0 = NS + s_ * LP
                kt_ap, kt_n = kt[idx % 2]
                vt_ap, vt_n = vt[idx % 2]
                q_ap, q_n = qt[c["q"] % 2]
                bg_ap, bg_n = bg[c["q"] % 2]
                c["q"] += 1
                S.dma("sp", lambda e: e.dma_start(out=kt_ap[:, 0:LP], in_=KT[h_ * 128:(h_ + 1) * 128, t0:t0 + LP]), writes=[kt_n])
                S.dma("sp", lambda e: e.dma_start(out=vt_ap[:, 0:2, :], in_=V[t0:t0 + LP, h_ * 128:(h_ + 1) * 128].rearrange("(c p) d -> p c d", p=128)), writes=[vt_n])
                S.dma("sp", lambda e: e.dma_start(out=q_ap[0:64, 0, 0:LP], in_=QT[h_ * 128:h_ * 128 + 64, t0:t0 + LP]), writes=[q_n])
                S.dma("sp", lambda e: e.dma_start(out=q_ap[64:128, 1, 0:LP], in_=QT[h_ * 128 + 64:(h_ + 1) * 128, t0:t0 + LP]), writes=[q_n])
                S.dma("sp", lambda e: e.dma_start(out=bg_ap[:, 0:LP], in_=BGT[h_ * 128:(h_ + 1) * 128, t0:t0 + LP]), writes=[bg_n])
                pcur[idx] = (kt_ap, kt_n, vt_ap, vt_n, q_ap, q_n, bg_ap, bg_n)

            load_p(0)
            for idx, (s_, h_) in enumerate(plist):
                if idx + 1 < len(plist):
                    load_p(idx + 1)
                kt_ap, kt_n, vt_ap, vt_n, q_ap, q_n, bg_ap, bg_n = pcur[idx]
                keys = [(kt_ap[:, i * 128:(i + 1) * 128], vt_ap[:, i, :], 128) for i in range(2)]
                attention(keys, (q_ap[:, 0, 0:LP], q_ap[:, 1, 0:LP]), q_n, LP, True, scale, bank_sets, [kt_n, vt_n], E, ecnt)
                epilogue(LP, h_, NS + s_ * LP, bg_ap, bg_n)
            S.barrier()

        def stage_na(l):
            A.reset()
            scale = 128 ** -0.5
            T, T_n = A.alloc(F32, 8, 15, 64, parts=64)
            S.dma("sp", lambda e: e.dma_start(out=T, in_=rpbT[l]), writes=[T_n])
            NK = NS + PAST
            ktb = [A.alloc(BF16, NK) for _ in range(2)]
            vlb = [A.alloc(BF16, 64, 128, parts=64) for _ in range(2)]
            vcb = [A.alloc(BF16, 4, 128) for _ in range(2)]
            ckf, ckf_n = A.alloc(F32, 4, 128)
            qhb = [A.alloc(BF16, NS) for _ in range(2)]
            cgb = [A.alloc(BF16, NS) for _ in range(2)]
            ost, ost_n = A.alloc(F32, NS)
            tm = [A.alloc(F32, 512, parts=64) for _ in range(2)]
            El = [A.alloc(BF16, 8, 64, parts=64) for _ in range(2)]
            Ec = [A.alloc(BF16, 4, 64) for _ in range(2)]
            rz = [A.alloc(F32, 64) for _ in range(2)]
            yo = [A.alloc(BF16, 512) for _ in range(2)]
            E2 = [A.alloc(BF16, 512) for _ in range(4)]
            def load_head(h):
                (kt, kt_n), (vl, vl_n), (vc, vc_n), (qh, qh_n), (cg, cg_n) = ktb[h % 2], vlb[h % 2], vcb[h % 2], qhb[h % 2], cgb[h % 2]
                S.dma("sp", lambda e: e.dma_start(out=kt[:, 0:NS], in_=KT[h * 128:(h + 1) * 128, 0:NS]), writes=[kt_n])
                S.dma("sp", lambda e: e.dma_start(out=qh, in_=QT[h * 128:(h + 1) * 128, 0:NS]), writes=[qh_n])
                S.dma("sp", lambda e: e.dma_start(out=vl, in_=V[0:NS, h * 128:(h + 1) * 128].rearrange("(r p) d -> p r d", p=64)), writes=[vl_n])
                S.dma("pool", lambda e: e.dma_start(out=vc, in_=cv[l][h].rearrange("(c p) d -> p c d", p=128)), writes=[vc_n])
                S.dma("sp", lambda e: e.dma_start(out=ckf, in_=ck[l][h].rearrange("(c p) d -> p c d", p=128)), writes=[ckf_n])
                S.dma("sp", lambda e: e.dma_start(out=cg, in_=BGT[h * 128:(h + 1) * 128, 0:NS]), writes=[cg_n])

            def ctx_head(h):
                kt, kt_n = ktb[h % 2]
                for j in range(4):
                    S.op("pe", lambda e, j=j: e.transpose(out=PS[7][:, j * 128:(j + 1) * 128], in_=ckf[:, j, :], identity=ident[:]), reads=[ckf_n, "ident"], writes=[PSN[7]])
                S.op("act", lambda e: e.activation(out=kt[:, NS:NK], in_=PS[7][:], func=AF.Copy), reads=[PSN[7]], writes=[kt_n])

            def do_head(h):
                (kt, kt_n), (vl, vl_n), (vc, vc_n), (qh, qh_n), (cg, cg_n) = ktb[h % 2], vlb[h % 2], vcb[h % 2], qhb[h % 2], cgb[h % 2]
                if h + 1 < 8:
                    load_head(h + 1)

                def na_front(r):
                    rs_ = min(max(r - 4, 0), 56)
                    dr0 = rs_ - r + 7
                    par = r % 2
                    bS, bC, bO, bZ = (0, 1, 4, 5) if par == 0 else (2, 3, 6, 7)
                    q_r = qh[:, r * 64:(r + 1) * 64]
                    for j in range(8):
                        kr = rs_ + j
                        S.op("pe", lambda e, j=j, kr=kr: e.matmul(out=PS[bS][0:64, j * 64:(j + 1) * 64], lhsT=kt[:, kr * 64:(kr + 1) * 64], rhs=q_r, start=True, stop=True),
                             reads=[kt_n, qh_n], writes=[PSN[bS]])
                    for j in range(4):
                        S.op("pe", lambda e, j=j: e.matmul(out=PS[bC][:, j * 64:(j + 1) * 64], lhsT=kt[:, NS + j * 128:NS + (j + 1) * 128], rhs=q_r, start=True, stop=True),
                             reads=[kt_n, qh_n], writes=[PSN[bC]])
                    tm_ap, tm_n = tm[par]
                    el_ap, el_n = El[par]
                    ec_ap, ec_n = Ec[par]
                    T_ap = T[:, h, dr0:dr0 + 8, :].rearrange("p a b -> p (a b)")
                    S.op("dve", lambda e: e.scalar_tensor_tensor(
                        out=tm_ap, in0=PS[bS][0:64, :], scalar=scale, in1=T_ap, op0=ALU.mult, op1=ALU.add),
                        reads=[PSN[bS], T_n], writes=[tm_n])
                    S.op("act", lambda e: e.activation(out=el_ap.rearrange("p a b -> p (a b)"), in_=tm_ap, func=AF.Exp), reads=[tm_n], writes=[el_n])
                    S.op("act", lambda e: e.activation(out=ec_ap.rearrange("p a b -> p (a b)"), in_=PS[bC][:, 0:256], func=AF.Exp, scale=scale), reads=[PSN[bC]], writes=[ec_n])

                def na_back(r):
                    rs_ = min(max(r - 4, 0), 56)
                    par = r % 2
                    bS, bC, bO, bZ = (0, 1, 4, 5) if par == 0 else (2, 3, 6, 7)
                    el_ap, el_n = El[par]
                    ec_ap, ec_n = Ec[par]
                    for which in range(2):
                        bb = bO if which == 0 else bZ
                        for j in range(8):
                            kr = rs_ + j
                            lh = vl[:, kr, :] if which == 0 else ones_bf[0:64, :]
                            S.op("pe", lambda e, lh=lh, j=j, bb=bb: e.matmul(out=PS[bb][:, 0:64], lhsT=lh, rhs=el_ap[:, j, :], start=(j == 0), stop=False),
                                 reads=[vl_n, el_n, "ones"], writes=[PSN[bb]])
                        for j in range(4):
                            lh = vc[:, j, :] if which == 0 else ones_bf[:]
                            S.op("pe", lambda e, lh=lh, j=j, bb=bb: e.matmul(out=PS[bb][:, 0:64], lhsT=lh, rhs=ec_ap[:, j, :], start=False, stop=(j == 3)),
                                 reads=[vc_n, ec_n, "ones"], writes=[PSN[bb]])
                    rz_ap, rz_n = rz[par]
                    S.op("act", lambda e: e.activation(out=rz_ap, in_=PS[bZ][:, 0:64], func=AF.Copy), reads=[PSN[bZ]], writes=[rz_n])
                    S.op("dve", lambda e: e.reciprocal(out=rz_ap, in_=rz_ap), reads=[rz_n], writes=[rz_n])
                    S.op("dve", lambda e: e.tensor_tensor(out=ost[:, r * 64:(r + 1) * 64], in0=PS[bO][:, 0:64], in1=rz_ap, op=ALU.mult),
                         reads=[PSN[bO], rz_n], writes=[ost_n])

                na_front(0)
                for r in range(64):
                    if r + 1 < 64:
                        na_front(r + 1)
                    na_back(r)
                for tb in range(8):
                    y_ap, y_n = yo[tb % 2]
                    S.op("dve", lambda e, y_ap=y_ap, tb=tb: e.tensor_tensor(out=y_ap, in0=ost[:, tb * 512:(tb + 1) * 512], in1=cg[:, tb * 512:(tb + 1) * 512], op=ALU.mult),
                         reads=[ost_n, cg_n], writes=[y_n])
                    S.dma("sp", lambda e, y_ap=y_ap, tb=tb, h=h: e.dma_start(out=YT[h * 128:(h + 1) * 128, tb * 512:(tb + 1) * 512], in_=y_ap), reads=[y_n])
                if h + 1 < 8:
                    ctx_head(h + 1)

            load_head(0)
            ctx_head(0)
            for h in range(8):
                do_head(h)
            ecnt = [0]
            pk = [A.alloc(BF16, LP) for _ in range(2)]
            pv_ = [A.alloc(BF16, 2, 128) for _ in range(2)]
            pq = [A.alloc(BF16, LP) for _ in range(2)]
            pg = [A.alloc(BF16, LP) for _ in range(2)]
            po = [A.alloc(F32, 2, LP) for _ in range(2)]
            plist = [(s_, h_) for s_ in range(NPS) for h_ in range(8)]
            pcur = {}

            def load_p(idx):
                s_, h_ = plist[idx]
                t0 = NS + s_ * LP
                (k_ap, k_n), (v_ap, v_n), (q_ap, q_n), (g_ap, g_n) = pk[idx % 2], pv_[idx % 2], pq[idx % 2], pg[idx % 2]
                S.dma("sp", lambda e: e.dma_start(out=k_ap, in_=KT[h_ * 128:(h_ + 1) * 128, t0:t0 + LP]), writes=[k_n])
                S.dma("sp", lambda e: e.dma_start(out=v_ap, in_=V[t0:t0 + LP, h_ * 128:(h_ + 1) * 128].rearrange("(c p) d -> p c d", p=128)), writes=[v_n])
                S.dma("sp", lambda e: e.dma_start(out=q_ap, in_=QT[h_ * 128:(h_ + 1) * 128, t0:t0 + LP]), writes=[q_n])
                S.dma("sp", lambda e: e.dma_start(out=g_ap, in_=BGT[h_ * 128:(h_ + 1) * 128, t0:t0 + LP]), writes=[g_n])

            load_p(0)
            for idx, (s_, h_) in enumerate(plist):
                if idx + 1 < len(plist):
                    load_p(idx + 1)
                t0 = NS + s_ * LP
                (k_ap, k_n), (v_ap, v_n), (q_ap, q_n), (g_ap, g_n) = pk[idx % 2], pv_[idx % 2], pq[idx % 2], pg[idx % 2]
                o_ap, o_n = po[idx % 2]
                keys = [(k_ap[:, i * 128:(i + 1) * 128], v_ap[:, i, :], 128) for i in range(2)]
                attention(keys, q_ap, q_n, LP, False, scale, [(0,), (1,), (2,), (3,)], [k_n, v_n], E2, ecnt)
                S.op("act", lambda e, o_ap=o_ap: e.activation(out=o_ap[:, 1, :], in_=PS[5][:, 0:LP], func=AF.Copy), reads=[PSN[5]], writes=[o_n])
                S.op("dve", lambda e, o_ap=o_ap: e.reciprocal(out=o_ap[:, 1, :], in_=o_ap[:, 1, :]), reads=[o_n], writes=[o_n])
                S.op("dve", lambda e, o_ap=o_ap: e.tensor_tensor(out=o_ap[:, 0, :], in0=PS[4][:, 0:LP], in1=o_ap[:, 1, :], op=ALU.mult), reads=[PSN[4], o_n], writes=[o_n])
                y_ap, y_n = yo[idx % 2]
                S.op("dve", lambda e, y_ap=y_ap, o_ap=o_ap, g_ap=g_ap: e.tensor_tensor(out=y_ap[:, 0:LP], in0=o_ap[:, 0, :], in1=g_ap, op=ALU.mult), reads=[o_n, g_n], writes=[y_n])
                S.dma("sp", lambda e, y_ap=y_ap, h_=h_, t0=t0: e.dma_start(out=YT[h_ * 128:(h_ + 1) * 128, t0:t0 + LP], in_=y_ap[:, 0:LP]), reads=[y_n])
            S.barrier()

        def stage_sgu(l):
            A.reset()
            gl, gl_n = A.alloc(F32, 1024)
            bs, bs_n = A.alloc(F32, 4, 512)
            wsf, wsf_n = A.alloc(F32, 4, 128)
            wsT, wsT_n = A.alloc(BF16, 4, 128)
            S.dma("sp", lambda e: e.dma_start(out=gl, in_=sguln[l]), writes=[gl_n])
            S.dma("sp", lambda e: e.dma_start(out=bs, in_=sgub[l]), writes=[bs_n])
            S.dma("sp", lambda e: e.dma_start(out=wsf, in_=sguw[l].rearrange("g i j -> i g j")), writes=[wsf_n])
            for g in range(4):
                S.op("pe", lambda e, g=g: e.transpose(out=PS[0][:, g * 128:(g + 1) * 128], in_=wsf[:, g, :], identity=ident[:]), reads=[wsf_n, "ident"], writes=[PSN[0]])
            S.op("act", lambda e: e.activation(out=wsT.rearrange("p a b -> p (a b)"), in_=PS[0][:], func=AF.Copy), reads=[PSN[0]], writes=[wsT_n])
            vsf = [A.alloc(F32, 1024) for _ in range(2)]
            vnf = [A.alloc(F32, 1024) for _ in range(2)]
            vn = [A.alloc(BF16, 4, 1024) for _ in range(2)]
            st = [A.alloc(F32, 24) for _ in range(2)]
            ut = [A.alloc(BF16, 512) for _ in range(3)]
            dg = [A.alloc(BF16, 512) for _ in range(3)]
            t1 = [A.alloc(F32, 512) for _ in range(2)]
            yo = [A.alloc(BF16, 512) for _ in range(3)]
            c = {"v": 0, "u": 0, "ps": 0, "y": 0}
            for tb in range(NT // 512):
                vn_ap, vn_n = vn[tb % 2]
                for ti in range(4):
                    r0 = tb * 512 + ti * 128
                    v_ap, v_n = vsf[c["v"] % 2]
                    f_ap, f_n = vnf[c["v"] % 2]
                    s_ap, s_n = st[c["v"] % 2]
                    c["v"] += 1
                    S.dma("sp", lambda e, v_ap=v_ap, r0=r0: e.dma_start(out=v_ap, in_=VS[r0:r0 + 128, :]), writes=[v_n])
                    for k2 in range(2):
                        S.op("dve", lambda e, s_ap=s_ap, v_ap=v_ap, k2=k2: e.bn_stats(out=s_ap[:, k2 * 6:(k2 + 1) * 6], in_=v_ap[:, k2 * 512:(k2 + 1) * 512]), reads=[v_n], writes=[s_n])
                    S.op("dve", lambda e, s_ap=s_ap: e.bn_aggr(out=s_ap[:, 12:14], in_=s_ap[:, 0:12]), reads=[s_n], writes=[s_n])
                    S.op("act", lambda e, s_ap=s_ap: e.activation(out=s_ap[:, 14:15], in_=s_ap[:, 13:14], func=AF.Sqrt, bias=eps_col, scale=1.0), reads=[s_n, "eps"], writes=[s_n])
                    S.op("dve", lambda e, s_ap=s_ap: e.reciprocal(out=s_ap[:, 14:15], in_=s_ap[:, 14:15]), reads=[s_n], writes=[s_n])
                    S.op("dve", lambda e, s_ap=s_ap: e.scalar_tensor_tensor(out=s_ap[:, 15:16], in0=s_ap[:, 12:13], scalar=-1.0, in1=s_ap[:, 14:15], op0=ALU.mult, op1=ALU.mult), reads=[s_n], writes=[s_n])
                    S.op("act", lambda e, f_ap=f_ap, v_ap=v_ap, s_ap=s_ap: e.activation(out=f_ap, in_=v_ap, func=AF.Identity, scale=s_ap[:, 14:15], bias=s_ap[:, 15:16]), reads=[v_n, s_n], writes=[f_n])
                    S.op("dve", lambda e, vn_ap=vn_ap, ti=ti, f_ap=f_ap: e.tensor_tensor(out=vn_ap[:, ti, :], in0=f_ap, in1=gl, op=ALU.mult), reads=[f_n, gl_n], writes=[vn_n])
                for cc in range(8):
                    g = cc // 2
                    b = c["ps"] % 4
                    c["ps"] += 1
                    for ti in range(4):
                        S.op("pe", lambda e, vn_ap=vn_ap, ti=ti, cc=cc, g=g, b=b: e.matmul(out=PS[b][:, ti * 128:(ti + 1) * 128], lhsT=vn_ap[:, ti, cc * 128:(cc + 1) * 128], rhs=wsT[:, g, :], start=True, stop=True),
                             reads=[vn_n, wsT_n], writes=[PSN[b]])
                    u_ap, u_n = ut[c["u"] % 3]
                    d_ap, d_n = dg[c["u"] % 3]
                    c["u"] += 1
                    S.dma("sp", lambda e, u_ap=u_ap, cc=cc, tb=tb: e.dma_start(out=u_ap, in_=AINT[cc * 128:(cc + 1) * 128, tb * 512:(tb + 1) * 512]), writes=[u_n])
                    S.dma("sp", lambda e, d_ap=d_ap, cc=cc, tb=tb: e.dma_start(out=d_ap, in_=AGT[cc * 128:(cc + 1) * 128, tb * 512:(tb + 1) * 512]), writes=[d_n])
                    t_ap, t_n = t1[c["y"] % 2]
                    y_ap, y_n = yo[c["y"] % 3]
                    c["y"] += 1
                    S.op("dve", lambda e, t_ap=t_ap, b=b, g=g: e.tensor_tensor(out=t_ap, in0=PS[b][:], in1=bs[:, g, :], op=ALU.add), reads=[PSN[b], bs_n], writes=[t_n])
                    S.op("dve", lambda e, t_ap=t_ap, u_ap=u_ap: e.tensor_tensor(out=t_ap, in0=t_ap, in1=u_ap, op=ALU.mult), reads=[t_n, u_n], writes=[t_n])
                    S.op("dve", lambda e, t_ap=t_ap, d_ap=d_ap, y_ap=y_ap: e.tensor_tensor(out=y_ap, in0=t_ap, in1=d_ap, op=ALU.mult), reads=[t_n, d_n], writes=[y_n])
                    S.dma("sp", lambda e, y_ap=y_ap, cc=cc, tb=tb: e.dma_start(out=YT[1024 + cc * 128:1024 + (cc + 1) * 128, tb * 512:(tb + 1) * 512], in_=y_ap), reads=[y_n])
            S.barrier()

        def stage_O(l, last):
            A.reset()
            src = xin if l == 0 else X
            dst = yout if last else X
            wo, wo_n = A.alloc(BF16, 16, D)
            for g in range(4):
                S.dma("pool", lambda e, g=g: e.dma_start(out=wo[:, :, g * 512:(g + 1) * 512], in_=wview(wout[l], g)), writes=[wo_n])
            yT = [A.alloc(BF16, 16, 256) for _ in range(2)]
            NB_O = 4
            NB_X = 5
            xt = [A.alloc(F32, D) for _ in range(NB_X)]
            tt = [A.alloc(F32, D) for _ in range(NB_O)]
            st = [A.alloc(F32, 32) for _ in range(NB_O)]
            ntile = NT // 128
            ycur = {}

            def o_load(n):
                blk, ti = n // 2, n % 2
                if ti == 0:
                    y_ap, y_n = yT[blk % 2]
                    S.dma("sp", lambda e: e.dma_start(out=y_ap, in_=YT.rearrange("(c p) t -> p c t", p=128)[:, :, blk * 256:(blk + 1) * 256]), writes=[y_n])
                    ycur[blk] = (y_ap, y_n)
                r0 = n * 128
                x_ap, x_n = xt[n % NB_X]
                S.dma("sp", lambda e: e.dma_start(out=x_ap, in_=src[r0:r0 + 128, :]), writes=[x_n])

            def o_front(n):
                blk, ti = n // 2, n % 2
                y_ap, y_n = ycur[blk]
                r0 = n * 128
                cnd = 0 if r0 < NS else 1
                x_ap, x_n = xt[n % NB_X]
                t_ap, t_n = tt[n % NB_O]
                pb = (n % 2) * 4
                for g in range(4):
                    for kc in range(16):
                        S.op("pe", lambda e, kc=kc, g=g: e.matmul(out=PS[pb + g][:], lhsT=y_ap[:, kc, ti * 128:(ti + 1) * 128], rhs=wo[:, kc, g * 512:(g + 1) * 512],
                                                              start=(kc == 0), stop=(kc == 15)), reads=[y_n, wo_n], writes=[PSN[pb + g]])
                    S.op("dve", lambda e, g=g: e.tensor_tensor(out=t_ap[:, g * 512:(g + 1) * 512], in0=PS[pb + g][:], in1=G[:, cnd, g * 512:(g + 1) * 512], op=ALU.mult),
                         reads=[PSN[pb + g], "G"], writes=[t_n])
                S.op("dve", lambda e: e.scalar_tensor_tensor(out=t_ap, in0=x_ap, scalar=ALPHA, in1=t_ap, op0=ALU.mult, op1=ALU.add),
                     reads=[x_n, t_n], writes=[t_n])

            def o_mid1(n):
                t_ap, t_n = tt[n % NB_O]
                s_ap, s_n = st[n % NB_O]
                for k4 in range(4):
                    S.op("dve", lambda e, k4=k4: e.bn_stats(out=s_ap[:, k4 * 6:(k4 + 1) * 6], in_=t_ap[:, k4 * 512:(k4 + 1) * 512]), reads=[t_n], writes=[s_n])
                S.op("dve", lambda e: e.bn_aggr(out=s_ap[:, 24:26], in_=s_ap[:, 0:24]), reads=[s_n], writes=[s_n])
                S.op("act", lambda e: e.activation(out=s_ap[:, 26:27], in_=s_ap[:, 25:26], func=AF.Sqrt, bias=eps_col, scale=1.0), reads=[s_n, "eps"], writes=[s_n])

            def o_mid2(n):
                u_ap, u_n = xt[n % NB_X]
                t_ap, t_n = tt[n % NB_O]
                s_ap, s_n = st[n % NB_O]
                S.op("dve", lambda e: e.reciprocal(out=s_ap[:, 26:27], in_=s_ap[:, 26:27]), reads=[s_n], writes=[s_n])
                S.op("dve", lambda e: e.scalar_tensor_tensor(out=s_ap[:, 27:28], in0=s_ap[:, 24:25], scalar=-1.0, in1=s_ap[:, 26:27], op0=ALU.mult, op1=ALU.mult), reads=[s_n], writes=[s_n])
                S.op("act", lambda e: e.activation(out=u_ap, in_=t_ap, func=AF.Identity, scale=s_ap[:, 26:27], bias=s_ap[:, 27:28]), reads=[t_n, s_n], writes=[u_n])

            def o_back(n):
                r0 = n * 128
                u_ap, u_n = xt[n % NB_X]
                S.op("dve", lambda e: e.tensor_tensor(out=u_ap, in0=u_ap, in1=lngb[:, 0, :], op=ALU.mult), reads=[u_n, "lngb"], writes=[u_n])
                S.op("dve", lambda e: e.tensor_tensor(out=u_ap, in0=u_ap, in1=lngb[:, 1, :], op=ALU.add), reads=[u_n, "lngb"], writes=[u_n])
                S.dma("sp", lambda e: e.dma_start(out=dst[r0:r0 + 128, :], in_=u_ap), reads=[u_n])

            o_load(0)
            for step in range(ntile + 3):
                if step + 1 < ntile:
                    o_load(step + 1)
                if step < ntile:
                    o_front(step)
                if 0 <= step - 1 < ntile:
                    o_mid1(step - 1)
                if 0 <= step - 2 < ntile:
                    o_mid2(step - 2)
                if 0 <= step - 3 < ntile:
                    o_back(step - 3)
            S.barrier()

        def on(name):
            return stages is None or name in stages

        for l in range(n_layers):
            if on("mod"):
                stage_mod(l)
            if on("P"):
                stage_P(l)
            if l % 2 == 0:
                if on("pool"):
                    stage_pool(l)
                if on("diff"):
                    stage_diff(l)
            else:
                if on("na"):
                    stage_na(l)
                if on("sgu"):
                    stage_sgu(l)
            if on("O"):
                stage_O(l, last=(l == n_layers - 1))
        S.wait_events("sp", S.all_events())
        S.emit()
        nins = S.nins
    return nc, nins


def _consts():
    ident = np.eye(128, dtype=np.float32)
    permT = np.zeros((128, 128), np.float32)
    for i in range(128):
        d = i % 64
        half = (d % 32) // 16
        p = i + 16 if half == 0 else i - 16
        permT[p, i] = 1.0
    t = np.arange(NS)
    row = (t // 64).astype(np.float32)
    col = (t % 64).astype(np.float32)
    inv = (1.0 / (10000.0 ** (np.arange(0, 32, 2, dtype=np.float32) / 32.0))).astype(np.float32)
    ropec = np.zeros((128, NS), np.float32)
    ropes = np.zeros((128, NS), np.float32)
    for i in range(128):
        d = i % 64
        axis = d // 32
        j = d % 16
        half = (d % 32) // 16
        pos = row if axis == 0 else col
        ang = (pos * inv[j]).astype(np.float32)
        ropec[i] = np.cos(ang)
        ropes[i] = np.sin(ang) * (-1.0 if half == 0 else 1.0)
    pedge = np.zeros((128, 4, 16), np.float32)
    for g, w in enumerate(POOL_WINDOWS):
        half = w // 2
        for i in range(8):
            pedge[:, g, i] = 1.0 / min(w, i + half)
            pedge[:, g, 8 + i] = 1.0 / min(w, 8 - i + half)
    return ident, permT, ropec, ropes, pedge


def _rpb_table(rpb):
    kc = np.arange(64)[:, None]
    qc = np.arange(64)[None, :]
    cstart = np.clip(qc - 8, 0, 48)
    valid = (kc >= cstart) & (kc < cstart + 16)
    dc = np.clip(kc - qc + 15, 0, 30)
    g = rpb[:, :, dc]
    g = np.where(valid[None, None], g, np.float32(NEG)).astype(np.float32)
    return np.ascontiguousarray(np.transpose(g, (2, 0, 1, 3)))


def _rep(v, n=128):
    return np.ascontiguousarray(np.broadcast_to(np.asarray(v, np.float32).reshape(1, -1), (n, np.asarray(v).size)))


_CACHE = {}


def make_in_maps(inputs, n_layers=4):
    f = lambda a: np.ascontiguousarray(np.asarray(a, dtype=np.float32))
    ident, permT, ropec, ropes, pedge = _consts()
    shared = {"ident": ident, "permT": permT, "ropec": ropec, "ropes": ropes, "pedge": pedge}
    for l in range(n_layers):
        bm = f(inputs[f"b_mod_{l}"])
        shared[f"wmod{l}"] = f(inputs[f"w_mod_{l}"])
        shared[f"bmodT{l}"] = np.ascontiguousarray(bm[:4096].reshape(32, 128).T)
        shared[f"bmodG{l}"] = _rep(bm[4096:])
        shared[f"win{l}"] = f(inputs[f"w_in_{l}"])
        shared[f"wout{l}"] = f(inputs[f"w_out_{l}"])
        shared[f"lng{l}"] = _rep(inputs[f"ln_g_{l}"])
        shared[f"lnb{l}"] = _rep(inputs[f"ln_b_{l}"])
        if l % 2 == 0:
            shared[f"poolw{l}"] = f(inputs[f"pool_w_{l}"])
            shared[f"pscale{l}"] = np.ascontiguousarray(f(inputs[f"pool_scale_{l}"]).reshape(8, 128).T)
            shared[f"dlam{l}"] = _rep(f(inputs[f"diff_lam_{l}"]).reshape(-1))
            shared[f"subln{l}"] = np.ascontiguousarray(f(inputs[f"diff_subln_{l}"]).reshape(128, 1))
        else:
            shared[f"rpbT{l}"] = _rpb_table(f(inputs[f"rpb_{l}"]))
            shared[f"sguln{l}"] = _rep(inputs[f"sgu_ln_{l}"])
            shared[f"sguw{l}"] = f(inputs[f"sgu_w_{l}"])
            sb = f(inputs[f"sgu_b_{l}"])
            shared[f"sgub{l}"] = np.ascontiguousarray(np.broadcast_to(np.tile(sb, (1, 4))[None], (128, 4, 512)))
    xs = f(inputs["x_sample"])
    xp = f(inputs["x_prompt"])
    c = f(inputs["c"])
    cctx = f(inputs["c_ctx"])
    maps = []
    for core in range(8):
        p = core // 2
        m = dict(shared)
        m["xin"] = np.ascontiguousarray(np.concatenate([xs[p], xp[4 * core:4 * core + 4].reshape(NPS * LP, D)], axis=0))
        cv_ = np.stack([c[p], cctx], 0).reshape(2, 16, 128)
        m["cvec"] = np.ascontiguousarray(np.transpose(cv_, (2, 0, 1)))
        for l in range(n_layers):
            m[f"ck{l}"] = f(inputs[f"cache_k_l{l}"])[p]
            m[f"cv{l}"] = f(inputs[f"cache_v_l{l}"])[p]
        maps.append(m)
    return maps


def kernel(**inputs):
    if "nc" not in _CACHE:
        _CACHE["nc"] = build_program(4)[0]
    nc = _CACHE["nc"]
    maps = make_in_maps(inputs)
    res = run_bass_kernel_spmd(nc, maps, core_ids=list(range(8)))
    r = res.results
    y_prompt = np.concatenate([r[cidx]["yout"][NS:].reshape(NPS, LP, D) for cidx in range(8)], axis=0)
    y_sample = np.stack([r[2 * p]["yout"][:NS] for p in range(4)], axis=0)
    outs = [y_prompt.astype(np.float32), y_sample.astype(np.float32)]
    for l in range(4):
        outs.append(np.concatenate([r[cidx][f"kout{l}"].reshape(NPS, 8, LP, 128) for cidx in range(8)], axis=0).astype(np.float32))
        outs.append(np.concatenate([r[cidx][f"vout{l}"].reshape(NPS, 8, LP, 128) for cidx in range(8)], axis=0).astype(np.float32))
    return tuple(outs)
```

```python
import math
import os
from contextlib import ExitStack

import numpy as np
import concourse.bass as bass
import concourse.mybir as mybir
from concourse.bass_utils import run_bass_kernel_spmd

F32 = mybir.dt.float32
BF16 = mybir.dt.bfloat16
AF = mybir.ActivationFunctionType
ALU = mybir.AluOpType

D = 2048
NS = 4096
NPS = 4
LP = 256
NT = NS + NPS * LP
PAST = 512
NEG = -30000.0
LN_EPS = 1e-5
ALPHA = (2 * 4) ** 0.25
POOL_WINDOWS = (2, 4, 8, 16)


class Tok:
    __slots__ = ("w", "r")

    def __init__(self):
        self.w = None
        self.r = []


class Sched:
    ENG = ("pe", "act", "dve", "pool", "sp")

    def __init__(self, nc, es):
        self.nc = nc
        ndma = {"sp": 28, "pool": 12}
        self.sem = {k: es.enter_context(nc.semaphore("s_" + k)) for k in ("pe", "act", "dve", "pool")}
        self.cnt = {k: 0 for k in self.sem}
        self.dsem = {q: [es.enter_context(nc.semaphore(f"d_{q}{i}")) for i in range(n)] for q, n in ndma.items()}
        self.dcnt = {q: [0] * n for q, n in ndma.items()}
        self.dnext = {q: 0 for q in ndma}
        self.waited = {e: {} for e in self.ENG}
        self.prog = {e: [] for e in self.ENG}
        self.toks = {}
        self.semobj = {}
        self.nins = 0

    def tok(self, key):
        t = self.toks.get(key)
        if t is None:
            t = self.toks[key] = Tok()
        return t

    def _deps(self, e, reads, writes):
        deps = {}

        def add(ev):
            if ev is None:
                return
            s, v = ev
            k = id(s)
            self.semobj[k] = s
            if deps.get(k, 0) < v:
                deps[k] = v

        for t in reads:
            add(t.w)
        for t in writes:
            add(t.w)
            for ev in t.r:
                add(ev)
        waits = []
        own = id(self.sem["pe"]) if e == "pe" else None
        for k, v in deps.items():
            if k == own:
                continue
            if self.waited[e].get(k, 0) < v:
                self.waited[e][k] = v
                waits.append((self.semobj[k], v))
        return waits

    def _finish(self, ev, reads, writes):
        for t in reads:
            t.r.append(ev)
            if len(t.r) > 64:
                best = {}
                for s, v in t.r:
                    if best.get(id(s), (None, 0))[1] < v:
                        best[id(s)] = (s, v)
                t.r = list(best.values())
        for t in writes:
            t.w = ev
            t.r = []

    def _toks(self, lst):
        return [self.tok(t) if not isinstance(t, Tok) else t for t in lst]

    def op(self, e, fn, reads=(), writes=()):
        reads = self._toks(reads)
        writes = self._toks(writes)
        waits = self._deps(e, reads, writes)
        self.cnt[e] += 1
        ev = (self.sem[e], self.cnt[e])
        self.prog[e].append((waits, fn, ev, 1))
        self._finish(ev, reads, writes)
        self.nins += 1 + len(waits)
        return ev

    def dma(self, q, fn, reads=(), writes=()):
        reads = self._toks(reads)
        writes = self._toks(writes)
        waits = self._deps(q, reads, writes)
        j = self.dnext[q]
        self.dnext[q] = (j + 1) % len(self.dsem[q])
        s = self.dsem[q][j]
        prev = self.dcnt[q][j]
        k = id(s)
        self.semobj[k] = s
        if prev > 0 and self.waited[q].get(k, 0) < prev:
            self.waited[q][k] = prev
            waits.append((s, prev))
        self.dcnt[q][j] = prev + 16
        ev = (s, prev + 16)
        self.prog[q].append((waits, fn, ev, 16))
        self._finish(ev, reads, writes)
        self.nins += 1 + len(waits)
        return ev

    def wait_events(self, e, evs):
        waits = []
        for s, v in evs:
            k = id(s)
            if self.waited[e].get(k, 0) < v:
                self.waited[e][k] = v
                waits.append((s, v))
        if waits:
            self.prog[e].append((waits, None, None, 0))
            self.nins += len(waits)

    def all_events(self):
        evs = [(self.sem[k], self.cnt[k]) for k in self.sem if self.cnt[k] > 0]
        for q in self.dsem:
            for s, c in zip(self.dsem[q], self.dcnt[q]):
                if c > 0:
                    evs.append((s, c))
        return evs

    def barrier(self):
        evs = self.all_events()
        for e in self.ENG:
            self.wait_events(e, evs)
        self.toks = {}

    def emit(self):
        nc = self.nc
        prog = self.prog

        def run(eng, lst):
            for waits, fn, ev, inc in lst:
                for s, v in waits:
                    eng.wait_ge(s, v)
                if fn is not None:
                    fn(eng).then_inc(ev[0], inc)

        with nc.Block() as block:
            @block.tensor
            def _(eng):
                run(eng, prog["pe"])

            @block.scalar
            def _(eng):
                run(eng, prog["act"])

            @block.vector
            def _(eng):
                run(eng, prog["dve"])

            @block.gpsimd
            def _(eng):
                run(eng, prog["pool"])

            @block.sync
            def _(eng):
                run(eng, prog["sp"])


class Arena:
    def __init__(self, handle, nbytes):
        self.h32 = handle
        self.h16 = handle.bitcast(BF16)
        self.nbytes = nbytes
        self.off = 0
        self.uid = 0

    def reset(self):
        self.off = 0

    def alloc(self, dt, *shape, parts=128):
        n = 1
        for s in shape:
            n *= s
        esz = 4 if dt == F32 else 2
        size = (n * esz + 31) // 32 * 32
        assert self.off + size <= self.nbytes, (self.off, size, self.nbytes)
        o = self.off
        self.off += size
        h = self.h32 if dt == F32 else self.h16
        ap = h[0:parts, o // esz:o // esz + n]
        if len(shape) == 2:
            ap = ap.rearrange("p (a b) -> p a b", a=shape[0])
        elif len(shape) == 3:
            ap = ap.rearrange("p (a b c) -> p a b c", a=shape[0], b=shape[1])
        self.uid += 1
        return ap, f"ar{self.uid}"


def build_program(n_layers=4, stages=None):
    nc = bass.Bass("TRN2", target_bir_lowering=False)

    def din(name, shape, dt=F32):
        return nc.dram_tensor(name, list(shape), dt, kind="ExternalInput").ap()

    def dout(name, shape, dt=F32):
        return nc.dram_tensor(name, list(shape), dt, kind="ExternalOutput").ap()

    def dscr(name, shape, dt):
        return nc.dram_tensor(name, list(shape), dt, kind="Internal").ap()

    xin = din("xin", [NT, D])
    cvec = din("cvec", [128, 2, 16])
    NL = n_layers
    ck = [din(f"ck{l}", [8, PAST, 128]) for l in range(NL)]
    cv = [din(f"cv{l}", [8, PAST, 128]) for l in range(NL)]
    wmod = [din(f"wmod{l}", [D, 3 * D]) for l in range(NL)]
    bmodT = [din(f"bmodT{l}", [128, 32]) for l in range(NL)]
    bmodG = [din(f"bmodG{l}", [128, D]) for l in range(NL)]
    win = [din(f"win{l}", [D, 6144 if l % 2 == 0 else 7168]) for l in range(NL)]
    wout = [din(f"wout{l}", [D, D]) for l in range(NL)]
    lng = [din(f"lng{l}", [128, D]) for l in range(NL)]
    lnb = [din(f"lnb{l}", [128, D]) for l in range(NL)]
    poolw = {l: din(f"poolw{l}", [4, 256, 256]) for l in (0, 2) if l < NL}
    pscale = {l: din(f"pscale{l}", [128, 8]) for l in (0, 2) if l < NL}
    dlam = {l: din(f"dlam{l}", [128, 256]) for l in (0, 2) if l < NL}
    subln = {l: din(f"subln{l}", [128, 1]) for l in (0, 2) if l < NL}
    rpbT = {l: din(f"rpbT{l}", [64, 8, 15, 64]) for l in (1, 3) if l < NL}
    sguln = {l: din(f"sguln{l}", [128, 1024]) for l in (1, 3) if l < NL}
    sguw = {l: din(f"sguw{l}", [4, 128, 128]) for l in (1, 3) if l < NL}
    sgub = {l: din(f"sgub{l}", [128, 4, 512]) for l in (1, 3) if l < NL}
    ident_d = din("ident", [128, 128])
    permT_d = din("permT", [128, 128])
    ropec_d = din("ropec", [128, NS])
    ropes_d = din("ropes", [128, NS])
    pedge_d = din("pedge", [128, 4, 16])

    yout = dout("yout", [NT, D])
    kout = [dout(f"kout{l}", [NPS * 8 * LP, 128]) for l in range(NL)]
    vout = [dout(f"vout{l}", [NPS * 8 * LP, 128]) for l in range(NL)]

    X = dscr("X", [NT, D], F32)
    QT = dscr("QT", [1024, NT], BF16)
    KT = dscr("KT", [1024, NT], BF16)
    AINT = dscr("AINT", [1024, NT], BF16)
    AGT = dscr("AGT", [1024, NT], BF16)
    BGT = dscr("BGT", [1024, NT], BF16)
    V = dscr("V", [NT, 1024], BF16)
    VS = dscr("VS", [NT, 1024], F32)
    YT = dscr("YT", [D, NT], BF16)

    with ExitStack() as es:
        S = Sched(nc, es)

        def sbt(name, shape, dt):
            return es.enter_context(nc.sbuf_tensor(name, list(shape), dt))

        ident = sbt("ident_s", [128, 128], F32)
        ones_bf = sbt("ones_bf", [128, 128], BF16)
        zeros_bf = sbt("zeros_bf", [128, 128], BF16)
        perm_bf = sbt("perm_bf", [128, 128], BF16)
        lhsc = sbt("lhsc", [128, 2, 16, 128], BF16)
        silc = sbt("silc", [128, 16, 2], BF16)
        sil = sbt("sil", [128, 2, 16], F32)
        mods = sbt("mods", [128, 2, 2, 16], F32)
        G = sbt("G", [128, 2, D], F32)
        lngb = sbt("lngb", [128, 2, D], F32)
        small = sbt("small", [128, 64], F32)
        ARENA_BYTES = 157 * 1024
        arena_h = sbt("arena", [128, ARENA_BYTES // 4], F32)
        A = Arena(arena_h, ARENA_BYTES)
        PSall = es.enter_context(nc.psum_tensor("psall", [128, 4096], F32))
        PS = [PSall[:, i * 512:(i + 1) * 512] for i in range(8)]
        PSN = [f"ps{i}" for i in range(8)]

        eps_col = small[:, 0:1]
        S.dma("sp", lambda e: e.dma_start(out=ident[:], in_=ident_d), writes=["ident"])
        S.dma("pool", lambda e: e.dma_start(out=perm_bf[:], in_=permT_d), writes=["perm"])
        S.dma("sp", lambda e: e.dma_start(out=sil[:], in_=cvec), writes=["sil"])
        S.op("dve", lambda e: e.memset(ones_bf[:], 1.0), writes=["ones"])
        S.op("dve", lambda e: e.memset(zeros_bf[:], 0.0), writes=["zeros"])
        S.op("dve", lambda e: e.memset(eps_col, LN_EPS), writes=["eps"])
        S.op("act", lambda e: e.activation(out=sil[:], in_=sil[:], func=AF.Silu), reads=["sil"], writes=["sil"])
        for cnd in range(2):
            S.op("dve", lambda e, cnd=cnd: e.tensor_copy(out=silc[:, :, cnd], in_=sil[:, cnd, :]), reads=["sil"], writes=["silc"])
            for kc in range(16):
                S.op("dve", lambda e, cnd=cnd, kc=kc: e.tensor_scalar(out=lhsc[:, cnd, kc, :], in0=zeros_bf[:], scalar1=sil[:, cnd, kc:kc + 1],
                                                                      scalar2=None, op0=ALU.add),
                     reads=["sil", "zeros"], writes=["lhsc"])
        S.barrier()

        def wview(w, g):
            return w.rearrange("(c p) n -> p c n", p=128)[:, :, g * 512:(g + 1) * 512]

        def stage_mod(l):
            A.reset()
            wb = [A.alloc(BF16, 16, 512) for _ in range(2)]
            bmT, bmTn = A.alloc(F32, 32)
            bmG, bmGn = A.alloc(F32, D)
            S.dma("sp", lambda e: e.dma_start(out=bmT, in_=bmodT[l]), writes=[bmTn])
            S.dma("sp", lambda e: e.dma_start(out=bmG, in_=bmodG[l]), writes=[bmGn])
            S.dma("sp", lambda e: e.dma_start(out=lngb[:, 0, :], in_=lng[l]), writes=["lngb"])
            S.dma("sp", lambda e: e.dma_start(out=lngb[:, 1, :], in_=lnb[l]), writes=["lngb"])
            psM = PS[7]
            for gi in range(12):
                w_ap, w_n = wb[gi % 2]
                S.dma("pool", lambda e, w_ap=w_ap, gi=gi: e.dma_start(out=w_ap, in_=wview(wmod[l], gi)), writes=[w_n])
                if gi < 8:
                    for j in range(4):
                        cc = gi * 4 + j
                        for kc in range(16):
                            S.op("pe", lambda e, w_ap=w_ap, j=j, kc=kc, cc=cc: e.matmul(
                                out=psM[:, cc * 2:cc * 2 + 2], lhsT=w_ap[:, kc, j * 128:(j + 1) * 128], rhs=silc[:, kc, :],
                                start=(kc == 0), stop=(kc == 15)), reads=[w_n, "silc"], writes=[PSN[7]])
                else:
                    for cnd in range(2):
                        b = (gi * 2 + cnd) % 4
                        for kc in range(16):
                            S.op("pe", lambda e, w_ap=w_ap, cnd=cnd, kc=kc, b=b: e.matmul(
                                out=PS[b][:], lhsT=lhsc[:, cnd, kc, :], rhs=w_ap[:, kc, :], start=(kc == 0), stop=(kc == 15)),
                                reads=[w_n, "lhsc"], writes=[PSN[b]])
                        c0 = (gi - 8) * 512
                        S.op("dve", lambda e, cnd=cnd, b=b, c0=c0: e.tensor_tensor(out=G[:, cnd, c0:c0 + 512], in0=PS[b][:], in1=bmG[:, c0:c0 + 512], op=ALU.add),
                             reads=[PSN[b], bmGn], writes=["G"])
            pv = psM[:, 0:64].rearrange("p (c t) -> p c t", t=2)
            for cnd in range(2):
                S.op("dve", lambda e, cnd=cnd: e.tensor_tensor(out=mods[:, cnd, :, :].rearrange("p a b -> p (a b)"), in0=pv[:, :, cnd], in1=bmT, op=ALU.add),
                     reads=[PSN[7], bmTn], writes=["mods"])
                S.op("dve", lambda e, cnd=cnd: e.tensor_scalar(out=mods[:, cnd, 1, :], in0=mods[:, cnd, 1, :], scalar1=1.0, scalar2=None, op0=ALU.add),
                     reads=["mods"], writes=["mods"])
            S.barrier()

        PDBG = set(os.environ.get('PDBG', 'fm,rope,tm,kv').split(','))

        def stage_P(l):
            even = l % 2 == 0
            ngrp = 12 if even else 14
            src = xin if l == 0 else X
            A.reset()
            hT = [A.alloc(BF16, 16, 1024) for _ in range(2)]
            wb = [A.alloc(BF16, 16, 512) for _ in range(2)]
            xt = [A.alloc(F32, D) for _ in range(2)]
            obf = [A.alloc(BF16, 512) for _ in range(4)]
            of32 = [A.alloc(F32, 512) for _ in range(2)]
            qb = [A.alloc(BF16, 512) for _ in range(2)]
            t1 = [A.alloc(F32, 512) for _ in range(2)]
            t2 = [A.alloc(F32, 512) for _ in range(2)]
            cs = [A.alloc(F32, 2, 512) for _ in range(4)]
            cnt = {"obf": 0, "of32": 0, "ps": 0, "rope": 0, "w": 0, "x": 0, "tp": 0}
            if even:
                kinds = ["ain", "ain", "gA", "gA", "q", "q", "k", "k", "v", "v", "gB", "gB"]
            else:
                kinds = ["q", "q", "k", "k", "v", "v", "gB", "gB", "ain", "ain", "vs", "vs", "gA", "gA"]
            dstT = {"ain": AINT, "gA": AGT, "gB": BGT, "q": QT, "k": KT}
            def tr_tile(tsb, ti):
                tok0 = tsb * 1024
                cnd = 0 if tsb < 4 else 1
                h_ap, h_n = hT[tsb % 2]
                x_ap, x_n = xt[cnt["x"] % 2]
                cnt["x"] += 1
                r0 = tok0 + ti * 128
                S.dma("sp", lambda e: e.dma_start(out=x_ap, in_=src[r0:r0 + 128, :]), writes=[x_n])
                for q4 in range(4):
                    b = 4 + cnt["tp"] % 2
                    cnt["tp"] += 1
                    for j in range(4):
                        kc = q4 * 4 + j
                        S.op("pe", lambda e, kc=kc, j=j, b=b: e.transpose(out=PS[b][:, j * 128:(j + 1) * 128], in_=x_ap[:, kc * 128:(kc + 1) * 128], identity=ident[:]),
                             reads=[x_n, "ident"], writes=[PSN[b]])
                    for j in range(4):
                        kc = q4 * 4 + j
                        S.op("act", lambda e, kc=kc, j=j, b=b: e.activation(
                            out=h_ap[:, kc, ti * 128:(ti + 1) * 128], in_=PS[b][:, j * 128:(j + 1) * 128], func=AF.Identity,
                            scale=mods[:, cnd, 1, kc:kc + 1], bias=mods[:, cnd, 0, kc:kc + 1]),
                            reads=[PSN[b], "mods"], writes=[h_n])

            for ti in range(8):
                tr_tile(0, ti)
            for tsb in range(5):
                tok0 = tsb * 1024
                cnd = 0 if tsb < 4 else 1
                h_ap, h_n = hT[tsb % 2]
                if even and tsb < 4:
                    for hf_ in range(2):
                        cs_ap_, cs_n_ = cs[(tsb % 2) * 2 + hf_]
                        tt_ = tok0 + hf_ * 512
                        S.dma("sp", lambda e, cs_ap_=cs_ap_, tt_=tt_: e.dma_start(out=cs_ap_[:, 0, :], in_=ropec_d[:, tt_:tt_ + 512]), writes=[cs_n_])
                        S.dma("sp", lambda e, cs_ap_=cs_ap_, tt_=tt_: e.dma_start(out=cs_ap_[:, 1, :], in_=ropes_d[:, tt_:tt_ + 512]), writes=[cs_n_])
                for g in range(ngrp):
                    kind = kinds[g]
                    gsub = g % 2
                    if kind == "k" and False:
                        pass
                    w_ap, w_n = wb[cnt["w"] % 2]
                    cnt["w"] += 1
                    S.dma("pool", lambda e, w_ap=w_ap, g=g: e.dma_start(out=w_ap, in_=wview(win[l], g)), writes=[w_n])
                    if kind in ("ain", "gA", "gB", "q", "k") and "fm" in PDBG:
                        for j in range(4):
                            row0 = (gsub * 4 + j) * 128
                            for hf in range(2):
                                b = cnt["ps"] % 4
                                cnt["ps"] += 1
                                t0 = tok0 + hf * 512
                                for kc in range(16):
                                    S.op("pe", lambda e, w_ap=w_ap, h_ap=h_ap, j=j, kc=kc, hf=hf, b=b: e.matmul(
                                        out=PS[b][:], lhsT=w_ap[:, kc, j * 128:(j + 1) * 128], rhs=h_ap[:, kc, hf * 512:(hf + 1) * 512],
                                        start=(kc == 0), stop=(kc == 15)), reads=[w_n, h_n], writes=[PSN[b]])
                                o_ap, o_n = obf[cnt["obf"] % 4]
                                cnt["obf"] += 1
                                if kind in ("gA", "gB"):
                                    S.op("act", lambda e, o_ap=o_ap, b=b: e.activation(out=o_ap, in_=PS[b][:], func=AF.Silu), reads=[PSN[b]], writes=[o_n])
                                elif kind in ("q", "k") and even and tsb < 4 and "rope" in PDBG:
                                    ri = cnt["rope"] % 2
                                    cnt["rope"] += 1
                                    qb_ap, qb_n = qb[ri]
                                    t1_ap, t1_n = t1[ri]
                                    t2_ap, t2_n = t2[ri]
                                    cs_ap, cs_n = cs[(tsb % 2) * 2 + hf]
                                    S.op("act", lambda e, qb_ap=qb_ap, b=b: e.activation(out=qb_ap, in_=PS[b][:], func=AF.Copy), reads=[PSN[b]], writes=[qb_n])
                                    b2 = 6 + ri
                                    S.op("pe", lambda e, qb_ap=qb_ap, b2=b2: e.matmul(out=PS[b2][:], lhsT=perm_bf[:], rhs=qb_ap, start=True, stop=True),
                                         reads=[qb_n, "perm"], writes=[PSN[b2]])
                                    S.op("dve", lambda e, t1_ap=t1_ap, cs_ap=cs_ap, qb_ap=qb_ap: e.tensor_tensor(out=t1_ap, in0=qb_ap, in1=cs_ap[:, 0, :], op=ALU.mult),
                                         reads=[qb_n, cs_n], writes=[t1_n])
                                    S.op("dve", lambda e, t2_ap=t2_ap, cs_ap=cs_ap, b2=b2: e.tensor_tensor(out=t2_ap, in0=PS[b2][:], in1=cs_ap[:, 1, :], op=ALU.mult),
                                         reads=[PSN[b2], cs_n], writes=[t2_n])
                                    S.op("dve", lambda e, o_ap=o_ap, t1_ap=t1_ap, t2_ap=t2_ap: e.tensor_tensor(out=o_ap, in0=t1_ap, in1=t2_ap, op=ALU.add),
                                         reads=[t1_n, t2_n], writes=[o_n])
                                else:
                                    S.op("act", lambda e, o_ap=o_ap, b=b: e.activation(out=o_ap, in_=PS[b][:], func=AF.Copy), reads=[PSN[b]], writes=[o_n])
                                dst = dstT[kind]
                                S.dma("sp", lambda e, o_ap=o_ap, dst=dst, row0=row0, t0=t0: e.dma_start(out=dst[row0:row0 + 128, t0:t0 + 512], in_=o_ap), reads=[o_n])
                    if (kind in ("v", "vs") or (kind == "k" and tsb == 4)) and "tm" in PDBG:
                        for ti in range(8):
                            b = cnt["ps"] % 4
                            cnt["ps"] += 1
                            r0 = tok0 + ti * 128
                            c0 = gsub * 512
                            for kc in range(16):
                                S.op("pe", lambda e, w_ap=w_ap, h_ap=h_ap, kc=kc, ti=ti, b=b: e.matmul(
                                    out=PS[b][:], lhsT=h_ap[:, kc, ti * 128:(ti + 1) * 128], rhs=w_ap[:, kc, :], start=(kc == 0), stop=(kc == 15)),
                                    reads=[w_n, h_n], writes=[PSN[b]])
                            if kind == "v":
                                o_ap, o_n = obf[cnt["obf"] % 4]
                                cnt["obf"] += 1
                                S.op("act", lambda e, o_ap=o_ap, b=b: e.activation(out=o_ap, in_=PS[b][:], func=AF.Copy), reads=[PSN[b]], writes=[o_n])
                                S.dma("sp", lambda e, o_ap=o_ap, r0=r0, c0=c0: e.dma_start(out=V[r0:r0 + 128, c0:c0 + 512], in_=o_ap), reads=[o_n])
                            if (kind == "vs" or tsb == 4) and "kv" in PDBG:
                                f_ap, f_n = of32[cnt["of32"] % 2]
                                cnt["of32"] += 1
                                S.op("act", lambda e, f_ap=f_ap, b=b: e.activation(out=f_ap, in_=PS[b][:], func=AF.Copy), reads=[PSN[b]], writes=[f_n])
                                if kind == "vs":
                                    S.dma("sp", lambda e, f_ap=f_ap, r0=r0, c0=c0: e.dma_start(out=VS[r0:r0 + 128, c0:c0 + 512], in_=f_ap), reads=[f_n])
                                else:
                                    dsto = kout[l] if kind == "k" else vout[l]
                                    sq = ti // 2
                                    tt0 = (ti % 2) * 128
                                    for hh in range(4):
                                        S.dma("sp", lambda e, f_ap=f_ap, dsto=dsto, sq=sq, tt0=tt0, gsub=gsub, hh=hh: e.dma_start(
                                            out=(VS[((sq * 8 + gsub * 4 + hh) * LP + tt0) // 8:((sq * 8 + gsub * 4 + hh) * LP + tt0) // 8 + 128, 0:128] if os.environ.get("KVDBG") else dsto[(sq * 8 + gsub * 4 + hh) * LP + tt0:(sq * 8 + gsub * 4 + hh) * LP + tt0 + 128, :]), in_=f_ap[:, hh * 128:(hh + 1) * 128]), reads=[f_n])
                if tsb + 1 < 5:
                    for ti_ in range(8):
                        tr_tile(tsb + 1, ti_)
            S.barrier()

        def attention(keys, q_ap, q_n, nq, two, scale, bank_set, reads_extra, E_bufs, ecnt):
            nkc = len(keys)
            wide = two and nq == 512

            def emit_S(i):
                kT_ap, v_ap, nk = keys[i]
                sb_ = bank_set[i % len(bank_set)]
                if two:
                    S.op("pe", lambda e: e.matmul(out=PS[sb_[0]][0:nk, 0:nq], lhsT=kT_ap, rhs=q_ap[0], start=True, stop=True),
                         reads=reads_extra + [q_n], writes=[PSN[sb_[0]]])
                    S.op("pe", lambda e: e.matmul(out=PS[sb_[1]][0:nk, 0:nq], lhsT=kT_ap, rhs=q_ap[1], start=True, stop=True),
                         reads=reads_extra + [q_n], writes=[PSN[sb_[1]]])
                else:
                    S.op("pe", lambda e: e.matmul(out=PS[sb_[0]][0:nk, 0:nq], lhsT=kT_ap, rhs=q_ap, start=True, stop=True),
                         reads=reads_extra + [q_n], writes=[PSN[sb_[0]]])

            def emit_pv(i, s_i, rhs_ap, e_n):
                kT_ap, v_ap, nk = keys[i]
                ob, zb = 4 + 2 * s_i, 5 + 2 * s_i
                S.op("pe", lambda e: e.matmul(out=PS[ob][:, 0:nq], lhsT=v_ap, rhs=rhs_ap, start=(i == 0), stop=(i == nkc - 1)),
                     reads=reads_extra + [e_n], writes=[PSN[ob]])
                S.op("pe", lambda e: e.matmul(out=PS[zb][:, 0:nq], lhsT=ones_bf[0:nk, :], rhs=rhs_ap, start=(i == 0), stop=(i == nkc - 1)),
                     reads=[e_n, "ones"], writes=[PSN[zb]])

            def emit_exp(i):
                kT_ap, v_ap, nk = keys[i]
                sb_ = bank_set[i % len(bank_set)]
                pend = []
                if wide:
                    e_ap, e_n = E_bufs[ecnt[0] % len(E_bufs)]
                    ecnt[0] += 1
                    base = sb_[0] * 512
                    S.op("act", lambda e: e.activation(out=e_ap[0:nk, 0:1024], in_=PSall[0:nk, base:base + 1024], func=AF.Exp, scale=scale),
                         reads=[PSN[sb_[0]], PSN[sb_[1]]], writes=[e_n])
                    for s_i in range(2):
                        pend.append((s_i, e_ap[0:nk, s_i * 512:(s_i + 1) * 512], e_n))
                else:
                    for s_i in range(2 if two else 1):
                        e_ap, e_n = E_bufs[ecnt[0] % len(E_bufs)]
                        ecnt[0] += 1
                        S.op("act", lambda e, e_ap=e_ap, s_i=s_i: e.activation(out=e_ap[0:nk, 0:nq], in_=PS[sb_[s_i]][0:nk, 0:nq], func=AF.Exp, scale=scale),
                             reads=[PSN[sb_[s_i]]], writes=[e_n])
                        pend.append((s_i, e_ap[0:nk, 0:nq], e_n))
                return pend

            nset = len(bank_set)
            for i0 in range(min(nset, nkc)):
                emit_S(i0)
            for i in range(nkc):
                pend = emit_exp(i)
                if i + nset < nkc:
                    emit_S(i + nset)
                for (s_i, rhs_ap, e_n) in pend:
                    emit_pv(i, s_i, rhs_ap, e_n)

        def stage_pool(l):
            A.reset()
            pe_t, pe_n = A.alloc(F32, 4, 16)
            psc, psc_n = A.alloc(F32, 8)
            S.dma("sp", lambda e: e.dma_start(out=pe_t, in_=pedge_d), writes=[pe_n])
            S.dma("sp", lambda e: e.dma_start(out=psc, in_=pscale[l]), writes=[psc_n])
            wp = [A.alloc(BF16, 2, 256) for _ in range(4)]
            for g in range(4):
                S.dma("pool", lambda e, g=g: e.dma_start(out=wp[g][0], in_=poolw[l][g].rearrange("(c p) d -> p c d", p=128)), writes=[wp[g][1]])
            NB = NS + 16
            xb = [A.alloc(BF16, NB) for _ in range(2)]
            pa, pa_n = A.alloc(F32, NB)
            pb, pb_n = A.alloc(F32, NB)
            pooled = [A.alloc(BF16, 2, NS) for _ in range(2)]
            ag = [A.alloc(BF16, NS) for _ in range(2)]
            yo = [A.alloc(BF16, 512) for _ in range(3)]
            c = {"x": 0, "ps": 0, "yo": 0, "ag": 0, "pl": 0}
            seqs = [(0, NS)] + [(NS + s * LP, LP) for s in range(NPS)]
            for (t0, L) in seqs:
                N = L + 16
                for g in range(4):
                    w = POOL_WINDOWS[g]
                    half = w // 2
                    pl_ap, pl_n = pooled[c["pl"] % 2]
                    c["pl"] += 1
                    for j in range(2):
                        ch = 2 * g + j
                        x_ap, x_n = xb[c["x"] % 2]
                        c["x"] += 1
                        S.op("pool", lambda e, x_ap=x_ap: e.memset(x_ap[:, 0:8], 0.0), writes=[x_n])
                        S.op("pool", lambda e, x_ap=x_ap, L=L: e.memset(x_ap[:, 8 + L:16 + L], 0.0), writes=[x_n])
                        S.dma("sp", lambda e, x_ap=x_ap, ch=ch, t0=t0, L=L: e.dma_start(out=x_ap[:, 8:8 + L], in_=AINT[ch * 128:(ch + 1) * 128, t0:t0 + L]), writes=[x_n])
                        cur, cur_n, width, n = x_ap, x_n, 1, N
                        bufs = [(pa, pa_n), (pb, pb_n)]
                        bi = 0
                        while width < w:
                            o_ap, o_n = bufs[bi]
                            bi ^= 1
                            n2 = n - width
                            S.op("dve", lambda e, o_ap=o_ap, cur=cur, n2=n2, width=width: e.tensor_tensor(out=o_ap[:, 0:n2], in0=cur[:, 0:n2], in1=cur[:, width:width + n2], op=ALU.add),
                                 reads=[cur_n], writes=[o_n])
                            cur, cur_n, n, width = o_ap, o_n, n2, width * 2
                        s0 = 8 - half
                        S.op("dve", lambda e, pl_ap=pl_ap, j=j, cur=cur, s0=s0, L=L, w=w, x_ap=x_ap: e.scalar_tensor_tensor(
                            out=pl_ap[:, j, 0:L], in0=cur[:, s0:s0 + L], scalar=1.0 / w, in1=x_ap[:, 8:8 + L], op0=ALU.mult, op1=ALU.subtract),
                            reads=[cur_n, x_n], writes=[pl_n])
                        o_ap, o_n = bufs[bi]
                        for (a0, e0) in ((0, 0), (L - 8, 8)):
                            S.op("dve", lambda e, o_ap=o_ap, cur=cur, s0=s0, a0=a0, e0=e0, g=g: e.tensor_tensor(
                                out=o_ap[:, a0:a0 + 8], in0=cur[:, s0 + a0:s0 + a0 + 8], in1=pe_t[:, g, e0:e0 + 8], op=ALU.mult),
                                reads=[cur_n, pe_n], writes=[o_n])
                            S.op("dve", lambda e, o_ap=o_ap, pl_ap=pl_ap, j=j, a0=a0, x_ap=x_ap: e.tensor_tensor(
                                out=pl_ap[:, j, a0:a0 + 8], in0=o_ap[:, a0:a0 + 8], in1=x_ap[:, 8 + a0:16 + a0], op=ALU.subtract),
                                reads=[o_n, x_n], writes=[pl_n])
                    for dch in range(2):
                        ch = 2 * g + dch
                        ag_ap, ag_n = ag[c["ag"] % 2]
                        c["ag"] += 1
                        S.dma("sp", lambda e, ag_ap=ag_ap, ch=ch, t0=t0, L=L: e.dma_start(out=ag_ap[:, 0:L], in_=AGT[ch * 128:(ch + 1) * 128, t0:t0 + L]), writes=[ag_n])
                        nb = max(1, L // 512)
                        bw = min(L, 512)
                        for tb in range(nb):
                            b = c["ps"] % 4
                            c["ps"] += 1
                            for cc in range(2):
                                S.op("pe", lambda e, g=g, cc=cc, dch=dch, pl_ap=pl_ap, tb=tb, bw=bw, b=b: e.matmul(
                                    out=PS[b][:, 0:bw], lhsT=wp[g][0][:, cc, dch * 128:(dch + 1) * 128], rhs=pl_ap[:, cc, tb * 512:tb * 512 + bw],
                                    start=(cc == 0), stop=(cc == 1)), reads=[wp[g][1], pl_n], writes=[PSN[b]])
                            y_ap, y_n = yo[c["yo"] % 3]
                            c["yo"] += 1
                            S.op("dve", lambda e, y_ap=y_ap, b=b, bw=bw, ch=ch, ag_ap=ag_ap, tb=tb: e.scalar_tensor_tensor(
                                out=y_ap[:, 0:bw], in0=PS[b][:, 0:bw], scalar=psc[:, ch:ch + 1], in1=ag_ap[:, tb * 512:tb * 512 + bw], op0=ALU.mult, op1=ALU.mult),
                                reads=[PSN[b], psc_n, ag_n], writes=[y_n])
                            S.dma("sp", lambda e, y_ap=y_ap, ch=ch, t0=t0, tb=tb, bw=bw: e.dma_start(out=YT[ch * 128:(ch + 1) * 128, t0 + tb * 512:t0 + tb * 512 + bw], in_=y_ap[:, 0:bw]),
                                  reads=[y_n])
            S.barrier()

        def stage_diff(l):
            A.reset()
            lam_init = 0.8 - 0.6 * math.exp(-0.3 * l)
            dl, dl_n = A.alloc(F32, 256)
            sub, sub_n = A.alloc(F32, 1)
            lw, lw_n = A.alloc(F32, 136)
            S.dma("sp", lambda e: e.dma_start(out=dl, in_=dlam[l]), writes=[dl_n])
            S.dma("sp", lambda e: e.dma_start(out=sub, in_=subln[l]), writes=[sub_n])
            S.op("dve", lambda e: e.tensor_tensor(out=lw[:, 0:64], in0=dl[:, 0:64], in1=dl[:, 64:128], op=ALU.mult), reads=[dl_n], writes=[lw_n])
            S.op("dve", lambda e: e.tensor_tensor(out=lw[:, 64:128], in0=dl[:, 128:192], in1=dl[:, 192:256], op=ALU.mult), reads=[dl_n], writes=[lw_n])
            S.op("dve", lambda e: e.reduce_sum(out=lw[:, 128:129], in_=lw[:, 0:64], axis=mybir.AxisListType.X), reads=[lw_n], writes=[lw_n])
            S.op("dve", lambda e: e.reduce_sum(out=lw[:, 129:130], in_=lw[:, 64:128], axis=mybir.AxisListType.X), reads=[lw_n], writes=[lw_n])
            S.op("act", lambda e: e.activation(out=lw[:, 130:132], in_=lw[:, 128:130], func=AF.Exp), reads=[lw_n], writes=[lw_n])
            S.op("dve", lambda e: e.tensor_tensor(out=lw[:, 132:133], in0=lw[:, 131:132], in1=lw[:, 130:131], op=ALU.subtract), reads=[lw_n], writes=[lw_n])
            S.op("dve", lambda e: e.tensor_scalar(out=lw[:, 132:133], in0=lw[:, 132:133], scalar1=-lam_init, scalar2=None, op0=ALU.add), reads=[lw_n], writes=[lw_n])
            S.op("dve", lambda e: e.tensor_scalar(out=lw[:, 133:134], in0=sub, scalar1=(1.0 - lam_init), scalar2=None, op0=ALU.mult), reads=[lw_n, sub_n], writes=[lw_n])
            nlam = lw[:, 132:133]
            c1 = lw[:, 133:134]

            NK = NS + PAST
            kt = [A.alloc(BF16, NK) for _ in range(2)]
            vt = [A.alloc(BF16, NK // 128, 128) for _ in range(2)]
            ckf, ckf_n = A.alloc(F32, 4, 128)
            qt = [A.alloc(BF16, 2, 512) for _ in range(2)]
            for q_ap_, q_n_ in qt:
                S.op("dve", lambda e, q_ap_=q_ap_: e.memset(q_ap_.rearrange("p a b -> p (a b)"), 0.0), writes=[q_n_])
            bg = [A.alloc(BF16, 512) for _ in range(2)]
            E = [A.alloc(BF16, 1024) for _ in range(3)]
            tmp = [A.alloc(F32, 512) for _ in range(6)]
            sqb = [A.alloc(BF16, 512) for _ in range(2)]
            yo = [A.alloc(BF16, 512) for _ in range(2)]
            ecnt = [0]
            c = {"q": 0, "y": 0}

            def epilogue(nq, h, tq0, bg_ap, bg_n):
                (r1, r1n), (r2, r2n), (o1, o1n), (o2, o2n), (oo, oon), (rs, rsn) = tmp
                S.op("act", lambda e: e.activation(out=r1[:, 0:nq], in_=PS[5][:, 0:nq], func=AF.Copy), reads=[PSN[5]], writes=[r1n])
                S.op("act", lambda e: e.activation(out=r2[:, 0:nq], in_=PS[7][:, 0:nq], func=AF.Copy), reads=[PSN[7]], writes=[r2n])
                S.op("dve", lambda e: e.reciprocal(out=r1[:, 0:nq], in_=r1[:, 0:nq]), reads=[r1n], writes=[r1n])
                S.op("dve", lambda e: e.reciprocal(out=r2[:, 0:nq], in_=r2[:, 0:nq]), reads=[r2n], writes=[r2n])
                S.op("dve", lambda e: e.tensor_tensor(out=o1[:, 0:nq], in0=PS[4][:, 0:nq], in1=r1[:, 0:nq], op=ALU.mult), reads=[PSN[4], r1n], writes=[o1n])
                S.op("dve", lambda e: e.tensor_tensor(out=o2[:, 0:nq], in0=PS[6][:, 0:nq], in1=r2[:, 0:nq], op=ALU.mult), reads=[PSN[6], r2n], writes=[o2n])
                S.op("dve", lambda e: e.scalar_tensor_tensor(out=oo[:, 0:nq], in0=o2[:, 0:nq], scalar=nlam, in1=o1[:, 0:nq], op0=ALU.mult, op1=ALU.add),
                     reads=[o1n, o2n, lw_n], writes=[oon])
                sq_ap, sq_n = sqb[c["y"] % 2]
                S.op("act", lambda e: e.activation(out=sq_ap[:, 0:nq], in_=oo[:, 0:nq], func=AF.Square), reads=[oon], writes=[sq_n])
                S.op("pe", lambda e: e.matmul(out=PS[0][:, 0:nq], lhsT=ones_bf[:], rhs=sq_ap[:, 0:nq], start=True, stop=True), reads=[sq_n, "ones"], writes=[PSN[0]])
                S.op("act", lambda e: e.activation(out=rs[:, 0:nq], in_=PS[0][:, 0:nq], func=AF.Sqrt, bias=eps_col, scale=1.0 / 128.0), reads=[PSN[0], "eps"], writes=[rsn])
                S.op("dve", lambda e: e.reciprocal(out=rs[:, 0:nq], in_=rs[:, 0:nq]), reads=[rsn], writes=[rsn])
                S.op("dve", lambda e: e.tensor_tensor(out=oo[:, 0:nq], in0=oo[:, 0:nq], in1=rs[:, 0:nq], op=ALU.mult), reads=[oon, rsn], writes=[oon])
                y_ap, y_n = yo[c["y"] % 2]
                c["y"] += 1
                S.op("dve", lambda e: e.scalar_tensor_tensor(out=y_ap[:, 0:nq], in0=oo[:, 0:nq], scalar=c1, in1=bg_ap[:, 0:nq], op0=ALU.mult, op1=ALU.mult),
                     reads=[oon, lw_n, bg_n], writes=[y_n])
                S.dma("sp", lambda e: e.dma_start(out=YT[1024 + h * 128:1024 + (h + 1) * 128, tq0:tq0 + nq], in_=y_ap[:, 0:nq]), reads=[y_n])

            scale = 64 ** -0.5
            bank_sets = [(0, 1), (2, 3)]
            def load_head(h):
                kt_ap, kt_n = kt[h % 2]
                vt_ap, vt_n = vt[h % 2]
                S.dma("sp", lambda e: e.dma_start(out=kt_ap[:, 0:NS], in_=KT[h * 128:(h + 1) * 128, 0:NS]), writes=[kt_n])
                S.dma("sp", lambda e: e.dma_start(out=vt_ap[:, 0:NS // 128, :], in_=V[0:NS, h * 128:(h + 1) * 128].rearrange("(c p) d -> p c d", p=128)), writes=[vt_n])
                S.dma("pool", lambda e: e.dma_start(out=vt_ap[:, NS // 128:NK // 128, :], in_=cv[l][h].rearrange("(c p) d -> p c d", p=128)), writes=[vt_n])
                S.dma("sp", lambda e: e.dma_start(out=ckf, in_=ck[l][h].rearrange("(c p) d -> p c d", p=128)), writes=[ckf_n])

            def ctx_head(h):
                kt_ap, kt_n = kt[h % 2]
                for j in range(4):
                    S.op("pe", lambda e, j=j: e.transpose(out=PS[2][:, j * 128:(j + 1) * 128], in_=ckf[:, j, :], identity=ident[:]), reads=[ckf_n, "ident"], writes=[PSN[2]])
                S.op("act", lambda e: e.activation(out=kt_ap[:, NS:NK], in_=PS[2][:], func=AF.Copy), reads=[PSN[2]], writes=[kt_n])

            qcur = {}

            def load_q(h, qb_i):
                q_ap, q_n = qt[c["q"] % 2]
                bg_ap, bg_n = bg[c["q"] % 2]
                c["q"] += 1
                tq0 = qb_i * 512
                S.dma("sp", lambda e: e.dma_start(out=q_ap[0:64, 0, :], in_=QT[h * 128:h * 128 + 64, tq0:tq0 + 512]), writes=[q_n])
                S.dma("sp", lambda e: e.dma_start(out=q_ap[64:128, 1, :], in_=QT[h * 128 + 64:(h + 1) * 128, tq0:tq0 + 512]), writes=[q_n])
                S.dma("sp", lambda e: e.dma_start(out=bg_ap, in_=BGT[h * 128:(h + 1) * 128, tq0:tq0 + 512]), writes=[bg_n])
                qcur[(h, qb_i)] = (q_ap, q_n, bg_ap, bg_n)

            blocks = [(h, qb_i) for h in range(8) for qb_i in range(NS // 512)]
            load_head(0)
            ctx_head(0)
            load_q(0, 0)
            for bi, (h, qb_i) in enumerate(blocks):
                kt_ap, kt_n = kt[h % 2]
                vt_ap, vt_n = vt[h % 2]
                if bi + 1 < len(blocks):
                    load_q(*blocks[bi + 1])
                if qb_i == 0 and h + 1 < 8:
                    load_head(h + 1)
                keys = [(kt_ap[:, i * 128:(i + 1) * 128], vt_ap[:, i, :], 128) for i in range(NK // 128)]
                q_ap, q_n, bg_ap, bg_n = qcur[(h, qb_i)]
                tq0 = qb_i * 512
                attention(keys, (q_ap[:, 0, :], q_ap[:, 1, :]), q_n, 512, True, scale, bank_sets, [kt_n, vt_n], E, ecnt)
                epilogue(512, h, tq0, bg_ap, bg_n)
                if qb_i == NS // 512 - 1 and h + 1 < 8:
                    ctx_head(h + 1)
            plist = [(s_, h_) for s_ in range(NPS) for h_ in range(8)]
            pcur = {}

            def load_p(idx):
                s_, h_ = plist[idx]
                t0 = NS + s_ * LP
                kt_ap, kt_n = kt[idx % 2]
                vt_ap, vt_n = vt[idx % 2]
                q_ap, q_n = qt[c["q"] % 2]
                bg_ap, bg_n = bg[c["q"] % 2]
                c["q"] += 1
                S.dma("sp", lambda e: e.dma_start(out=kt_ap[:, 0:LP], in_=KT[h_ * 128:(h_ + 1) * 128, t0:t0 + LP]), writes=[kt_n])
                S.dma("sp", lambda e: e.dma_start(out=vt_ap[:, 0:2, :], in_=V[t0:t0 + LP, h_ * 128:(h_ + 1) * 128].rearrange("(c p) d -> p c d", p=128)), writes=[vt_n])
                S.dma("sp", lambda e: e.dma_start(out=q_ap[0:64, 0, 0:LP], in_=QT[h_ * 128:h_ * 128 + 64, t0:t0 + LP]), writes=[q_n])
                S.dma("sp", lambda e: e.dma_start(out=q_ap[64:128, 1, 0:LP], in_=QT[h_ * 128 + 64:(h_ + 1) * 128, t0:t0 + LP]), writes=[q_n])
                S.dma("sp", lambda e: e.dma_start(out=bg_ap[:, 0:LP], in_=BGT[h_ * 128:(h_ + 1) * 128, t0:t0 + LP]), writes=[bg_n])
                pcur[idx] = (kt_ap, kt_n, vt_ap, vt_n, q_ap, q_n, bg_ap, bg_n)

            load_p(0)
            for idx, (s_, h_) in enumerate(plist):
                if idx + 1 < len(plist):
                    load_p(idx + 1)
                kt_ap, kt_n, vt_ap, vt_n, q_ap, q_n, bg_ap, bg_n = pcur[idx]
                keys = [(kt_ap[:, i * 128:(i + 1) * 128], vt_ap[:, i, :], 128) for i in range(2)]
                attention(keys, (q_ap[:, 0, 0:LP], q_ap[:, 1, 0:LP]), q_n, LP, True, scale, bank_sets, [kt_n, vt_n], E, ecnt)
                epilogue(LP, h_, NS + s_ * LP, bg_ap, bg_n)
            S.barrier()

        def stage_na(l):
            A.reset()
            scale = 128 ** -0.5
            T, T_n = A.alloc(F32, 8, 15, 64, parts=64)
            S.dma("sp", lambda e: e.dma_start(out=T, in_=rpbT[l]), writes=[T_n])
            NK = NS + PAST
            ktb = [A.alloc(BF16, NK) for _ in range(2)]
            vlb = [A.alloc(BF16, 64, 128, parts=64) for _ in range(2)]
            vcb = [A.alloc(BF16, 4, 128) for _ in range(2)]
            ckf, ckf_n = A.alloc(F32, 4, 128)
            qhb = [A.alloc(BF16, NS) for _ in range(2)]
            cgb = [A.alloc(BF16, NS) for _ in range(2)]
            ost, ost_n = A.alloc(F32, NS)
            tm = [A.alloc(F32, 512, parts=64) for _ in range(2)]
            El = [A.alloc(BF16, 8, 64, parts=64) for _ in range(2)]
            Ec = [A.alloc(BF16, 4, 64) for _ in range(2)]
            rz = [A.alloc(F32, 64) for _ in range(2)]
            yo = [A.alloc(BF16, 512) for _ in range(2)]
            E2 = [A.alloc(BF16, 512) for _ in range(4)]
            def load_head(h):
                (kt, kt_n), (vl, vl_n), (vc, vc_n), (qh, qh_n), (cg, cg_n) = ktb[h % 2], vlb[h % 2], vcb[h % 2], qhb[h % 2], cgb[h % 2]
                S.dma("sp", lambda e: e.dma_start(out=kt[:, 0:NS], in_=KT[h * 128:(h + 1) * 128, 0:NS]), writes=[kt_n])
                S.dma("sp", lambda e: e.dma_start(out=qh, in_=QT[h * 128:(h + 1) * 128, 0:NS]), writes=[qh_n])
                S.dma("sp", lambda e: e.dma_start(out=vl, in_=V[0:NS, h * 128:(h + 1) * 128].rearrange("(r p) d -> p r d", p=64)), writes=[vl_n])
                S.dma("pool", lambda e: e.dma_start(out=vc, in_=cv[l][h].rearrange("(c p) d -> p c d", p=128)), writes=[vc_n])
                S.dma("sp", lambda e: e.dma_start(out=ckf, in_=ck[l][h].rearrange("(c p) d -> p c d", p=128)), writes=[ckf_n])
                S.dma("sp", lambda e: e.dma_start(out=cg, in_=BGT[h * 128:(h + 1) * 128, 0:NS]), writes=[cg_n])

            def ctx_head(h):
                kt, kt_n = ktb[h % 2]
                for j in range(4):
                    S.op("pe", lambda e, j=j: e.transpose(out=PS[7][:, j * 128:(j + 1) * 128], in_=ckf[:, j, :], identity=ident[:]), reads=[ckf_n, "ident"], writes=[PSN[7]])
                S.op("act", lambda e: e.activation(out=kt[:, NS:NK], in_=PS[7][:], func=AF.Copy), reads=[PSN[7]], writes=[kt_n])

            def do_head(h):
                (kt, kt_n), (vl, vl_n), (vc, vc_n), (qh, qh_n), (cg, cg_n) = ktb[h % 2], vlb[h % 2], vcb[h % 2], qhb[h % 2], cgb[h % 2]
                if h + 1 < 8:
                    load_head(h + 1)

                def na_front(r):
                    rs_ = min(max(r - 4, 0), 56)
                    dr0 = rs_ - r + 7
                    par = r % 2
                    bS, bC, bO, bZ = (0, 1, 4, 5) if par == 0 else (2, 3, 6, 7)
                    q_r = qh[:, r * 64:(r + 1) * 64]
                    for j in range(8):
                        kr = rs_ + j
                        S.op("pe", lambda e, j=j, kr=kr: e.matmul(out=PS[bS][0:64, j * 64:(j + 1) * 64], lhsT=kt[:, kr * 64:(kr + 1) * 64], rhs=q_r, start=True, stop=True),
                             reads=[kt_n, qh_n], writes=[PSN[bS]])
                    for j in range(4):
                        S.op("pe", lambda e, j=j: e.matmul(out=PS[bC][:, j * 64:(j + 1) * 64], lhsT=kt[:, NS + j * 128:NS + (j + 1) * 128], rhs=q_r, start=True, stop=True),
                             reads=[kt_n, qh_n], writes=[PSN[bC]])
                    tm_ap, tm_n = tm[par]
                    el_ap, el_n = El[par]
                    ec_ap, ec_n = Ec[par]
                    T_ap = T[:, h, dr0:dr0 + 8, :].rearrange("p a b -> p (a b)")
                    S.op("dve", lambda e: e.scalar_tensor_tensor(
                        out=tm_ap, in0=PS[bS][0:64, :], scalar=scale, in1=T_ap, op0=ALU.mult, op1=ALU.add),
                        reads=[PSN[bS], T_n], writes=[tm_n])
                    S.op("act", lambda e: e.activation(out=el_ap.rearrange("p a b -> p (a b)"), in_=tm_ap, func=AF.Exp), reads=[tm_n], writes=[el_n])
                    S.op("act", lambda e: e.activation(out=ec_ap.rearrange("p a b -> p (a b)"), in_=PS[bC][:, 0:256], func=AF.Exp, scale=scale), reads=[PSN[bC]], writes=[ec_n])

                def na_back(r):
                    rs_ = min(max(r - 4, 0), 56)
                    par = r % 2
                    bS, bC, bO, bZ = (0, 1, 4, 5) if par == 0 else (2, 3, 6, 7)
                    el_ap, el_n = El[par]
                    ec_ap, ec_n = Ec[par]
                    for which in range(2):
                        bb = bO if which == 0 else bZ
                        for j in range(8):
                            kr = rs_ + j
                            lh = vl[:, kr, :] if which == 0 else ones_bf[0:64, :]
                            S.op("pe", lambda e, lh=lh, j=j, bb=bb: e.matmul(out=PS[bb][:, 0:64], lhsT=lh, rhs=el_ap[:, j, :], start=(j == 0), stop=False),
                                 reads=[vl_n, el_n, "ones"], writes=[PSN[bb]])
                        for j in range(4):
                            lh = vc[:, j, :] if which == 0 else ones_bf[:]
                            S.op("pe", lambda e, lh=lh, j=j, bb=bb: e.matmul(out=PS[bb][:, 0:64], lhsT=lh, rhs=ec_ap[:, j, :], start=False, stop=(j == 3)),
                                 reads=[vc_n, ec_n, "ones"], writes=[PSN[bb]])
                    rz_ap, rz_n = rz[par]
                    S.op("act", lambda e: e.activation(out=rz_ap, in_=PS[bZ][:, 0:64], func=AF.Copy), reads=[PSN[bZ]], writes=[rz_n])
                    S.op("dve", lambda e: e.reciprocal(out=rz_ap, in_=rz_ap), reads=[rz_n], writes=[rz_n])
                    S.op("dve", lambda e: e.tensor_tensor(out=ost[:, r * 64:(r + 1) * 64], in0=PS[bO][:, 0:64], in1=rz_ap, op=ALU.mult),
                         reads=[PSN[bO], rz_n], writes=[ost_n])

                na_front(0)
                for r in range(64):
                    if r + 1 < 64:
                        na_front(r + 1)
                    na_back(r)
                for tb in range(8):
                    y_ap, y_n = yo[tb % 2]
                    S.op("dve", lambda e, y_ap=y_ap, tb=tb: e.tensor_tensor(out=y_ap, in0=ost[:, tb * 512:(tb + 1) * 512], in1=cg[:, tb * 512:(tb + 1) * 512], op=ALU.mult),
                         reads=[ost_n, cg_n], writes=[y_n])
                    S.dma("sp", lambda e, y_ap=y_ap, tb=tb, h=h: e.dma_start(out=YT[h * 128:(h + 1) * 128, tb * 512:(tb + 1) * 512], in_=y_ap), reads=[y_n])
                if h + 1 < 8:
                    ctx_head(h + 1)

            load_head(0)
            ctx_head(0)
            for h in range(8):
                do_head(h)
            ecnt = [0]
            pk = [A.alloc(BF16, LP) for _ in range(2)]
            pv_ = [A.alloc(BF16, 2, 128) for _ in range(2)]
            pq = [A.alloc(BF16, LP) for _ in range(2)]
            pg = [A.alloc(BF16, LP) for _ in range(2)]
            po = [A.alloc(F32, 2, LP) for _ in range(2)]
            plist = [(s_, h_) for s_ in range(NPS) for h_ in range(8)]
            pcur = {}

            def load_p(idx):
                s_, h_ = plist[idx]
                t0 = NS + s_ * LP
                (k_ap, k_n), (v_ap, v_n), (q_ap, q_n), (g_ap, g_n) = pk[idx % 2], pv_[idx % 2], pq[idx % 2], pg[idx % 2]
                S.dma("sp", lambda e: e.dma_start(out=k_ap, in_=KT[h_ * 128:(h_ + 1) * 128, t0:t0 + LP]), writes=[k_n])
                S.dma("sp", lambda e: e.dma_start(out=v_ap, in_=V[t0:t0 + LP, h_ * 128:(h_ + 1) * 128].rearrange("(c p) d -> p c d", p=128)), writes=[v_n])
                S.dma("sp", lambda e: e.dma_start(out=q_ap, in_=QT[h_ * 128:(h_ + 1) * 128, t0:t0 + LP]), writes=[q_n])
                S.dma("sp", lambda e: e.dma_start(out=g_ap, in_=BGT[h_ * 128:(h_ + 1) * 128, t0:t0 + LP]), writes=[g_n])

            load_p(0)
            for idx, (s_, h_) in enumerate(plist):
                if idx + 1 < len(plist):
                    load_p(idx + 1)
                t0 = NS + s_ * LP
                (k_ap, k_n), (v_ap, v_n), (q_ap, q_n), (g_ap, g_n) = pk[idx % 2], pv_[idx % 2], pq[idx % 2], pg[idx % 2]
                o_ap, o_n = po[idx % 2]
                keys = [(k_ap[:, i * 128:(i + 1) * 128], v_ap[:, i, :], 128) for i in range(2)]
                attention(keys, q_ap, q_n, LP, False, scale, [(0,), (1,), (2,), (3,)], [k_n, v_n], E2, ecnt)
                S.op("act", lambda e, o_ap=o_ap: e.activation(out=o_ap[:, 1, :], in_=PS[5][:, 0:LP], func=AF.Copy), reads=[PSN[5]], writes=[o_n])
                S.op("dve", lambda e, o_ap=o_ap: e.reciprocal(out=o_ap[:, 1, :], in_=o_ap[:, 1, :]), reads=[o_n], writes=[o_n])
                S.op("dve", lambda e, o_ap=o_ap: e.tensor_tensor(out=o_ap[:, 0, :], in0=PS[4][:, 0:LP], in1=o_ap[:, 1, :], op=ALU.mult), reads=[PSN[4], o_n], writes=[o_n])
                y_ap, y_n = yo[idx % 2]
                S.op("dve", lambda e, y_ap=y_ap, o_ap=o_ap, g_ap=g_ap: e.tensor_tensor(out=y_ap[:, 0:LP], in0=o_ap[:, 0, :], in1=g_ap, op=ALU.mult), reads=[o_n, g_n], writes=[y_n])
                S.dma("sp", lambda e, y_ap=y_ap, h_=h_, t0=t0: e.dma_start(out=YT[h_ * 128:(h_ + 1) * 128, t0:t0 + LP], in_=y_ap[:, 0:LP]), reads=[y_n])
            S.barrier()

        def stage_sgu(l):
            A.reset()
            gl, gl_n = A.alloc(F32, 1024)
            bs, bs_n = A.alloc(F32, 4, 512)
            wsf, wsf_n = A.alloc(F32, 4, 128)
            wsT, wsT_n = A.alloc(BF16, 4, 128)
            S.dma("sp", lambda e: e.dma_start(out=gl, in_=sguln[l]), writes=[gl_n])
            S.dma("sp", lambda e: e.dma_start(out=bs, in_=sgub[l]), writes=[bs_n])
            S.dma("sp", lambda e: e.dma_start(out=wsf, in_=sguw[l].rearrange("g i j -> i g j")), writes=[wsf_n])
            for g in range(4):
                S.op("pe", lambda e, g=g: e.transpose(out=PS[0][:, g * 128:(g + 1) * 128], in_=wsf[:, g, :], identity=ident[:]), reads=[wsf_n, "ident"], writes=[PSN[0]])
            S.op("act", lambda e: e.activation(out=wsT.rearrange("p a b -> p (a b)"), in_=PS[0][:], func=AF.Copy), reads=[PSN[0]], writes=[wsT_n])
            vsf = [A.alloc(F32, 1024) for _ in range(2)]
            vnf = [A.alloc(F32, 1024) for _ in range(2)]
            vn = [A.alloc(BF16, 4, 1024) for _ in range(2)]
            st = [A.alloc(F32, 24) for _ in range(2)]
            ut = [A.alloc(BF16, 512) for _ in range(3)]
            dg = [A.alloc(BF16, 512) for _ in range(3)]
            t1 = [A.alloc(F32, 512) for _ in range(2)]
            yo = [A.alloc(BF16, 512) for _ in range(3)]
            c = {"v": 0, "u": 0, "ps": 0, "y": 0}
            for tb in range(NT // 512):
                vn_ap, vn_n = vn[tb % 2]
                for ti in range(4):
                    r0 = tb * 512 + ti * 128
                    v_ap, v_n = vsf[c["v"] % 2]
                    f_ap, f_n = vnf[c["v"] % 2]
                    s_ap, s_n = st[c["v"] % 2]
                    c["v"] += 1
                    S.dma("sp", lambda e, v_ap=v_ap, r0=r0: e.dma_start(out=v_ap, in_=VS[r0:r0 + 128, :]), writes=[v_n])
                    for k2 in range(2):
                        S.op("dve", lambda e, s_ap=s_ap, v_ap=v_ap, k2=k2: e.bn_stats(out=s_ap[:, k2 * 6:(k2 + 1) * 6], in_=v_ap[:, k2 * 512:(k2 + 1) * 512]), reads=[v_n], writes=[s_n])
                    S.op("dve", lambda e, s_ap=s_ap: e.bn_aggr(out=s_ap[:, 12:14], in_=s_ap[:, 0:12]), reads=[s_n], writes=[s_n])
                    S.op("act", lambda e, s_ap=s_ap: e.activation(out=s_ap[:, 14:15], in_=s_ap[:, 13:14], func=AF.Sqrt, bias=eps_col, scale=1.0), reads=[s_n, "eps"], writes=[s_n])
                    S.op("dve", lambda e, s_ap=s_ap: e.reciprocal(out=s_ap[:, 14:15], in_=s_ap[:, 14:15]), reads=[s_n], writes=[s_n])
                    S.op("dve", lambda e, s_ap=s_ap: e.scalar_tensor_tensor(out=s_ap[:, 15:16], in0=s_ap[:, 12:13], scalar=-1.0, in1=s_ap[:, 14:15], op0=ALU.mult, op1=ALU.mult), reads=[s_n], writes=[s_n])
                    S.op("act", lambda e, f_ap=f_ap, v_ap=v_ap, s_ap=s_ap: e.activation(out=f_ap, in_=v_ap, func=AF.Identity, scale=s_ap[:, 14:15], bias=s_ap[:, 15:16]), reads=[v_n, s_n], writes=[f_n])
                    S.op("dve", lambda e, vn_ap=vn_ap, ti=ti, f_ap=f_ap: e.tensor_tensor(out=vn_ap[:, ti, :], in0=f_ap, in1=gl, op=ALU.mult), reads=[f_n, gl_n], writes=[vn_n])
                for cc in range(8):
                    g = cc // 2
                    b = c["ps"] % 4
                    c["ps"] += 1
                    for ti in range(4):
                        S.op("pe", lambda e, vn_ap=vn_ap, ti=ti, cc=cc, g=g, b=b: e.matmul(out=PS[b][:, ti * 128:(ti + 1) * 128], lhsT=vn_ap[:, ti, cc * 128:(cc + 1) * 128], rhs=wsT[:, g, :], start=True, stop=True),
                             reads=[vn_n, wsT_n], writes=[PSN[b]])
                    u_ap, u_n = ut[c["u"] % 3]
                    d_ap, d_n = dg[c["u"] % 3]
                    c["u"] += 1
                    S.dma("sp", lambda e, u_ap=u_ap, cc=cc, tb=tb: e.dma_start(out=u_ap, in_=AINT[cc * 128:(cc + 1) * 128, tb * 512:(tb + 1) * 512]), writes=[u_n])
                    S.dma("sp", lambda e, d_ap=d_ap, cc=cc, tb=tb: e.dma_start(out=d_ap, in_=AGT[cc * 128:(cc + 1) * 128, tb * 512:(tb + 1) * 512]), writes=[d_n])
                    t_ap, t_n = t1[c["y"] % 2]
                    y_ap, y_n = yo[c["y"] % 3]
                    c["y"] += 1
                    S.op("dve", lambda e, t_ap=t_ap, b=b, g=g: e.tensor_tensor(out=t_ap, in0=PS[b][:], in1=bs[:, g, :], op=ALU.add), reads=[PSN[b], bs_n], writes=[t_n])
                    S.op("dve", lambda e, t_ap=t_ap, u_ap=u_ap: e.tensor_tensor(out=t_ap, in0=t_ap, in1=u_ap, op=ALU.mult), reads=[t_n, u_n], writes=[t_n])
                    S.op("dve", lambda e, t_ap=t_ap, d_ap=d_ap, y_ap=y_ap: e.tensor_tensor(out=y_ap, in0=t_ap, in1=d_ap, op=ALU.mult), reads=[t_n, d_n], writes=[y_n])
                    S.dma("sp", lambda e, y_ap=y_ap, cc=cc, tb=tb: e.dma_start(out=YT[1024 + cc * 128:1024 + (cc + 1) * 128, tb * 512:(tb + 1) * 512], in_=y_ap), reads=[y_n])
            S.barrier()

        def stage_O(l, last):
            A.reset()
            src = xin if l == 0 else X
            dst = yout if last else X
            wo, wo_n = A.alloc(BF16, 16, D)
            for g in range(4):
                S.dma("pool", lambda e, g=g: e.dma_start(out=wo[:, :, g * 512:(g + 1) * 512], in_=wview(wout[l], g)), writes=[wo_n])
            yT = [A.alloc(BF16, 16, 256) for _ in range(2)]
            NB_O = 4
            NB_X = 5
            xt = [A.alloc(F32, D) for _ in range(NB_X)]
            tt = [A.alloc(F32, D) for _ in range(NB_O)]
            st = [A.alloc(F32, 32) for _ in range(NB_O)]
            ntile = NT // 128
            ycur = {}

            def o_load(n):
                blk, ti = n // 2, n % 2
                if ti == 0:
                    y_ap, y_n = yT[blk % 2]
                    S.dma("sp", lambda e: e.dma_start(out=y_ap, in_=YT.rearrange("(c p) t -> p c t", p=128)[:, :, blk * 256:(blk + 1) * 256]), writes=[y_n])
                    ycur[blk] = (y_ap, y_n)
                r0 = n * 128
                x_ap, x_n = xt[n % NB_X]
                S.dma("sp", lambda e: e.dma_start(out=x_ap, in_=src[r0:r0 + 128, :]), writes=[x_n])

            def o_front(n):
                blk, ti = n // 2, n % 2
                y_ap, y_n = ycur[blk]
                r0 = n * 128
                cnd = 0 if r0 < NS else 1
                x_ap, x_n = xt[n % NB_X]
                t_ap, t_n = tt[n % NB_O]
                pb = (n % 2) * 4
                for g in range(4):
                    for kc in range(16):
                        S.op("pe", lambda e, kc=kc, g=g: e.matmul(out=PS[pb + g][:], lhsT=y_ap[:, kc, ti * 128:(ti + 1) * 128], rhs=wo[:, kc, g * 512:(g + 1) * 512],
                                                              start=(kc == 0), stop=(kc == 15)), reads=[y_n, wo_n], writes=[PSN[pb + g]])
                    S.op("dve", lambda e, g=g: e.tensor_tensor(out=t_ap[:, g * 512:(g + 1) * 512], in0=PS[pb + g][:], in1=G[:, cnd, g * 512:(g + 1) * 512], op=ALU.mult),
                         reads=[PSN[pb + g], "G"], writes=[t_n])
                S.op("dve", lambda e: e.scalar_tensor_tensor(out=t_ap, in0=x_ap, scalar=ALPHA, in1=t_ap, op0=ALU.mult, op1=ALU.add),
                     reads=[x_n, t_n], writes=[t_n])

            def o_mid1(n):
                t_ap, t_n = tt[n % NB_O]
                s_ap, s_n = st[n % NB_O]
                for k4 in range(4):
                    S.op("dve", lambda e, k4=k4: e.bn_stats(out=s_ap[:, k4 * 6:(k4 + 1) * 6], in_=t_ap[:, k4 * 512:(k4 + 1) * 512]), reads=[t_n], writes=[s_n])
                S.op("dve", lambda e: e.bn_aggr(out=s_ap[:, 24:26], in_=s_ap[:, 0:24]), reads=[s_n], writes=[s_n])
                S.op("act", lambda e: e.activation(out=s_ap[:, 26:27], in_=s_ap[:, 25:26], func=AF.Sqrt, bias=eps_col, scale=1.0), reads=[s_n, "eps"], writes=[s_n])

            def o_mid2(n):
                u_ap, u_n = xt[n % NB_X]
                t_ap, t_n = tt[n % NB_O]
                s_ap, s_n = st[n % NB_O]
                S.op("dve", lambda e: e.reciprocal(out=s_ap[:, 26:27], in_=s_ap[:, 26:27]), reads=[s_n], writes=[s_n])
                S.op("dve", lambda e: e.scalar_tensor_tensor(out=s_ap[:, 27:28], in0=s_ap[:, 24:25], scalar=-1.0, in1=s_ap[:, 26:27], op0=ALU.mult, op1=ALU.mult), reads=[s_n], writes=[s_n])
                S.op("act", lambda e: e.activation(out=u_ap, in_=t_ap, func=AF.Identity, scale=s_ap[:, 26:27], bias=s_ap[:, 27:28]), reads=[t_n, s_n], writes=[u_n])

            def o_back(n):
                r0 = n * 128
                u_ap, u_n = xt[n % NB_X]
                S.op("dve", lambda e: e.tensor_tensor(out=u_ap, in0=u_ap, in1=lngb[:, 0, :], op=ALU.mult), reads=[u_n, "lngb"], writes=[u_n])
                S.op("dve", lambda e: e.tensor_tensor(out=u_ap, in0=u_ap, in1=lngb[:, 1, :], op=ALU.add), reads=[u_n, "lngb"], writes=[u_n])
                S.dma("pool", lambda e: e.dma_start(out=dst[r0:r0 + 128, :], in_=u_ap), reads=[u_n])

            o_load(0)
            for step in range(ntile + 3):
                if step + 1 < ntile:
                    o_load(step + 1)
                if step < ntile:
                    o_front(step)
                if 0 <= step - 1 < ntile:
                    o_mid1(step - 1)
                if 0 <= step - 2 < ntile:
                    o_mid2(step - 2)
                if 0 <= step - 3 < ntile:
                    o_back(step - 3)
            S.barrier()

        def on(name):
            return stages is None or name in stages

        for l in range(n_layers):
            if on("mod"):
                stage_mod(l)
            if on("P"):
                stage_P(l)
            if l % 2 == 0:
                if on("pool"):
                    stage_pool(l)
                if on("diff"):
                    stage_diff(l)
            else:
                if on("na"):
                    stage_na(l)
                if on("sgu"):
                    stage_sgu(l)
            if on("O"):
                stage_O(l, last=(l == n_layers - 1))
        S.wait_events("sp", S.all_events())
        S.emit()
        nins = S.nins
    return nc, nins


def _consts():
    ident = np.eye(128, dtype=np.float32)
    permT = np.zeros((128, 128), np.float32)
    for i in range(128):
        d = i % 64
        half = (d % 32) // 16
        p = i + 16 if half == 0 else i - 16
        permT[p, i] = 1.0
    t = np.arange(NS)
    row = (t // 64).astype(np.float32)
    col = (t % 64).astype(np.float32)
    inv = (1.0 / (10000.0 ** (np.arange(0, 32, 2, dtype=np.float32) / 32.0))).astype(np.float32)
    ropec = np.zeros((128, NS), np.float32)
    ropes = np.zeros((128, NS), np.float32)
    for i in range(128):
        d = i % 64
        axis = d // 32
        j = d % 16
        half = (d % 32) // 16
        pos = row if axis == 0 else col
        ang = (pos * inv[j]).astype(np.float32)
        ropec[i] = np.cos(ang)
        ropes[i] = np.sin(ang) * (-1.0 if half == 0 else 1.0)
    pedge = np.zeros((128, 4, 16), np.float32)
    for g, w in enumerate(POOL_WINDOWS):
        half = w // 2
        for i in range(8):
            pedge[:, g, i] = 1.0 / min(w, i + half)
            pedge[:, g, 8 + i] = 1.0 / min(w, 8 - i + half)
    return ident, permT, ropec, ropes, pedge


def _rpb_table(rpb):
    kc = np.arange(64)[:, None]
    qc = np.arange(64)[None, :]
    cstart = np.clip(qc - 8, 0, 48)
    valid = (kc >= cstart) & (kc < cstart + 16)
    dc = np.clip(kc - qc + 15, 0, 30)
    g = rpb[:, :, dc]
    g = np.where(valid[None, None], g, np.float32(NEG)).astype(np.float32)
    return np.ascontiguousarray(np.transpose(g, (2, 0, 1, 3)))


def _rep(v, n=128):
    return np.ascontiguousarray(np.broadcast_to(np.asarray(v, np.float32).reshape(1, -1), (n, np.asarray(v).size)))


_CACHE = {}


def make_in_maps(inputs, n_layers=4):
    f = lambda a: np.ascontiguousarray(np.asarray(a, dtype=np.float32))
    ident, permT, ropec, ropes, pedge = _consts()
    shared = {"ident": ident, "permT": permT, "ropec": ropec, "ropes": ropes, "pedge": pedge}
    for l in range(n_layers):
        bm = f(inputs[f"b_mod_{l}"])
        shared[f"wmod{l}"] = f(inputs[f"w_mod_{l}"])
        shared[f"bmodT{l}"] = np.ascontiguousarray(bm[:4096].reshape(32, 128).T)
        shared[f"bmodG{l}"] = _rep(bm[4096:])
        shared[f"win{l}"] = f(inputs[f"w_in_{l}"])
        shared[f"wout{l}"] = f(inputs[f"w_out_{l}"])
        shared[f"lng{l}"] = _rep(inputs[f"ln_g_{l}"])
        shared[f"lnb{l}"] = _rep(inputs[f"ln_b_{l}"])
        if l % 2 == 0:
            shared[f"poolw{l}"] = f(inputs[f"pool_w_{l}"])
            shared[f"pscale{l}"] = np.ascontiguousarray(f(inputs[f"pool_scale_{l}"]).reshape(8, 128).T)
            shared[f"dlam{l}"] = _rep(f(inputs[f"diff_lam_{l}"]).reshape(-1))
            shared[f"subln{l}"] = np.ascontiguousarray(f(inputs[f"diff_subln_{l}"]).reshape(128, 1))
        else:
            shared[f"rpbT{l}"] = _rpb_table(f(inputs[f"rpb_{l}"]))
            shared[f"sguln{l}"] = _rep(inputs[f"sgu_ln_{l}"])
            shared[f"sguw{l}"] = f(inputs[f"sgu_w_{l}"])
            sb = f(inputs[f"sgu_b_{l}"])
            shared[f"sgub{l}"] = np.ascontiguousarray(np.broadcast_to(np.tile(sb, (1, 4))[None], (128, 4, 512)))
    xs = f(inputs["x_sample"])
    xp = f(inputs["x_prompt"])
    c = f(inputs["c"])
    cctx = f(inputs["c_ctx"])
    maps = []
    for core in range(8):
        p = core // 2
        m = dict(shared)
        m["xin"] = np.ascontiguousarray(np.concatenate([xs[p], xp[4 * core:4 * core + 4].reshape(NPS * LP, D)], axis=0))
        cv_ = np.stack([c[p], cctx], 0).reshape(2, 16, 128)
        m["cvec"] = np.ascontiguousarray(np.transpose(cv_, (2, 0, 1)))
        for l in range(n_layers):
            m[f"ck{l}"] = f(inputs[f"cache_k_l{l}"])[p]
            m[f"cv{l}"] = f(inputs[f"cache_v_l{l}"])[p]
        maps.append(m)
    return maps


def kernel(**inputs):
    if "nc" not in _CACHE:
        _CACHE["nc"] = build_program(4)[0]
    nc = _CACHE["nc"]
    maps = make_in_maps(inputs)
    res = run_bass_kernel_spmd(nc, maps, core_ids=list(range(8)))
    r = res.results
    y_prompt = np.concatenate([r[cidx]["yout"][NS:].reshape(NPS, LP, D) for cidx in range(8)], axis=0)
    y_sample = np.stack([r[2 * p]["yout"][:NS] for p in range(4)], axis=0)
    outs = [y_prompt.astype(np.float32), y_sample.astype(np.float32)]
    for l in range(4):
        outs.append(np.concatenate([r[cidx][f"kout{l}"].reshape(NPS, 8, LP, 128) for cidx in range(8)], axis=0).astype(np.float32))
        outs.append(np.concatenate([r[cidx][f"vout{l}"].reshape(NPS, 8, LP, 128) for cidx in range(8)], axis=0).astype(np.float32))
    return tuple(outs)
```
